# Optimizing a Trainium2 kernel written in Bass

```python
import math
import jax, jax.numpy as jnp
from jax import lax
import numpy as np

D_MODEL = 1024
BATCH = 8
SEQ = 2048
DEPTH = 2
DEC_BATCH = 8
DEC_SEQ = 32
PAST_LEN = 1024

CHUNK = 64
Q_BLOCK = 128
HEAD_DIM = 64
H_A = 4
H_B = 8
H_C = 4
W_A = H_A * HEAD_DIM
MLA_NOPE = 64
MLA_ROPE = 32
MLA_V = 64
W_B = H_B * MLA_V
W_C = H_C * HEAD_DIM
MIX_WIDTH = W_A + W_B + W_C
Q_LORA = 256
KV_LORA = 128
ROPE_THETA = 10000.0
D_FF = 2816
PLE_DIM = 256
EPS = 1e-6
FFN_RES = 0.5
SB_SCALE = HEAD_DIM ** -0.5
MLA_SCALE = (MLA_NOPE + MLA_ROPE) ** -0.5
FOX_SCALE = HEAD_DIM ** -0.5
A_COLS = 3 * W_A
B_COLS = Q_LORA + KV_LORA + MLA_ROPE
C_COLS = 3 * W_C + H_C
IN_COLS = A_COLS + B_COLS + C_COLS
SPLITS = (W_A, 2 * W_A, 3 * W_A, 3 * W_A + Q_LORA, 3 * W_A + Q_LORA + KV_LORA, A_COLS + B_COLS,
          A_COLS + B_COLS + W_C, A_COLS + B_COLS + 2 * W_C, A_COLS + B_COLS + 3 * W_C)
N_STATE = 7

kernel_name = "hybrid_stream_sb_mla_fox_step"


def rmsnorm(x, g):
    xf = x.astype(jnp.float32)
    y = xf * lax.rsqrt(jnp.mean(xf * xf, axis=-1, keepdims=True) + EPS)
    return (y * g.astype(jnp.float32)).astype(x.dtype)


def swiglu(x, w_gu, w_down):
    g, u = jnp.split(x @ w_gu, 2, axis=-1)
    return (jax.nn.silu(g) * u) @ w_down


def rope(x, pos):
    half = MLA_ROPE // 2
    inv = ROPE_THETA ** (-jnp.arange(half, dtype=jnp.float32) / half)
    ang = pos.astype(jnp.float32)[:, None] * inv[None, :]
    shape = (pos.shape[0],) + (1,) * (x.ndim - 3) + (half,)
    cos, sin = jnp.cos(ang).reshape(shape), jnp.sin(ang).reshape(shape)
    xf = x.astype(jnp.float32)
    x1, x2 = xf[..., :half], xf[..., half:]
    return jnp.concatenate([x1 * cos - x2 * sin, x1 * sin + x2 * cos], axis=-1).astype(x.dtype)


def blockify(a):
    b, s = a.shape[0], a.shape[1]
    return jnp.moveaxis(a.reshape((b, s // Q_BLOCK, Q_BLOCK) + a.shape[2:]), 1, 0)


def unblockify(a):
    a = jnp.moveaxis(a, 0, 1)
    return a.reshape((a.shape[0], a.shape[1] * a.shape[2]) + a.shape[3:])


def stick_breaking(q, k, v, q_pos, k_pos):
    z = jnp.einsum('bqhd,bkhd->bhqk', q, k).astype(jnp.float32) * SB_SCALE
    mask = k_pos[None, :] < q_pos[:, None]
    log_1m = jnp.where(mask, jax.nn.log_sigmoid(-z), 0.0)
    between = lax.cumsum(log_1m, axis=3, reverse=True) - log_1m
    w = jnp.where(mask, jnp.exp(jax.nn.log_sigmoid(z) + between), 0.0)
    return jnp.einsum('bhqk,bkhd->bqhd', w.astype(v.dtype), v)


def chunk_softmax(q, k, v, q_pos, k_pos):
    z = jnp.einsum('bqhd,bkhd->bhqk', q, k).astype(jnp.float32) * MLA_SCALE
    mask = (k_pos // CHUNK)[None, :] <= (q_pos // CHUNK)[:, None]
    pr = jax.nn.softmax(jnp.where(mask, z, -jnp.inf), axis=-1)
    return jnp.einsum('bhqk,bkhd->bqhd', pr.astype(v.dtype), v)


def forgetting(q, k, v, q_pos, k_pos, fq, fk):
    z = jnp.einsum('bqhd,bkhd->bhqk', q, k).astype(jnp.float32) * FOX_SCALE
    z = z + (jnp.swapaxes(fq, 1, 2)[..., :, None] - jnp.swapaxes(fk, 1, 2)[..., None, :])
    mask = k_pos[None, :] <= q_pos[:, None]
    pr = jax.nn.softmax(jnp.where(mask, z, -jnp.inf), axis=-1)
    return jnp.einsum('bhqk,bkhd->bqhd', pr.astype(v.dtype), v)


def token_mix(xn, q_pos, past, w_in, b_f, g_bq, g_bkv, w_uq, w_ukv, g_grp, w_out):
    bsz, t = xn.shape[0], xn.shape[1]
    qa, ka, va, cq, ckv, kr, qc, kc, vc, fl = jnp.split(xn @ w_in, SPLITS, axis=-1)
    qa, ka, va = (a.reshape(bsz, t, H_A, HEAD_DIM) for a in (qa, ka, va))
    qc, kc, vc = (a.reshape(bsz, t, H_C, HEAD_DIM) for a in (qc, kc, vc))
    q_b = (rmsnorm(cq, g_bq) @ w_uq).reshape(bsz, t, H_B, MLA_NOPE + MLA_ROPE)
    q_b = jnp.concatenate([q_b[..., :MLA_NOPE], rope(q_b[..., MLA_NOPE:], q_pos)], axis=-1)
    ckv = rmsnorm(ckv, g_bkv)
    kr = rope(kr, q_pos)
    logf = jax.nn.log_sigmoid((fl + b_f).astype(jnp.float32)).astype(xn.dtype)
    new_state = (ka, va, ckv, kr, kc, vc, logf)
    if past is None:
        ka_all, va_all, ckv_all, kr_all, kc_all, vc_all, logf_all = new_state
        k_pos = q_pos
    else:
        ka_all, va_all, ckv_all, kr_all, kc_all, vc_all, logf_all = [
            jnp.concatenate([c, n.astype(c.dtype)], axis=1) for c, n in zip(past, new_state)]
        k_pos = jnp.arange(past[0].shape[1] + t, dtype=jnp.int32)
    tk = ka_all.shape[1]
    kv_b = (ckv_all @ w_ukv).reshape(bsz, tk, H_B, MLA_NOPE + MLA_V)
    k_b = jnp.concatenate([kv_b[..., :MLA_NOPE],
                           jnp.broadcast_to(kr_all[:, :, None, :], (bsz, tk, H_B, MLA_ROPE)).astype(kv_b.dtype)], axis=-1)
    v_b = kv_b[..., MLA_NOPE:]
    f_all = jnp.cumsum(logf_all.astype(jnp.float32), axis=1)
    f_q = f_all[:, tk - t:]

    def attend(qa_blk, qb_blk, qc_blk, fq_blk, pos_blk):
        return (stick_breaking(qa_blk, ka_all, va_all, pos_blk, k_pos),
                chunk_softmax(qb_blk, k_b, v_b, pos_blk, k_pos),
                forgetting(qc_blk, kc_all, vc_all, pos_blk, k_pos, fq_blk, f_all))

    if past is None:
        oa, ob, oc = lax.map(lambda a: attend(*a),
                             (blockify(qa), blockify(q_b), blockify(qc), blockify(f_q), q_pos.reshape(-1, Q_BLOCK)))
        oa, ob, oc = unblockify(oa), unblockify(ob), unblockify(oc)
    else:
        oa, ob, oc = attend(qa, q_b, qc, f_q, q_pos)
    g_a, g_b, g_c = jnp.split(g_grp, [W_A, W_A + W_B])
    o = jnp.concatenate([rmsnorm(oa.reshape(bsz, t, W_A), g_a),
                         rmsnorm(ob.reshape(bsz, t, W_B), g_b),
                         rmsnorm(oc.reshape(bsz, t, W_C), g_c)], axis=-1)
    return o @ w_out, new_state


def layer(h, p, q_pos, past, g_ff1_pre, g_ff1_post, w_ff1_gu, w_ff1_down, g_mix_pre, g_mix_post, w_in, b_f,
          g_bq, g_bkv, w_uq, w_ukv, g_grp, w_out, g_ff2_pre, g_ff2_post, w_ff2_gu, w_ff2_down,
          g_ple_pre, w_ple_gate, w_ple_proj, g_ple_post):
    h = h + FFN_RES * rmsnorm(swiglu(rmsnorm(h, g_ff1_pre), w_ff1_gu, w_ff1_down), g_ff1_post)
    m, state = token_mix(rmsnorm(h, g_mix_pre), q_pos, past, w_in, b_f, g_bq, g_bkv, w_uq, w_ukv, g_grp, w_out)
    h = h + rmsnorm(m, g_mix_post)
    h = h + FFN_RES * rmsnorm(swiglu(rmsnorm(h, g_ff2_pre), w_ff2_gu, w_ff2_down), g_ff2_post)
    gate = jax.nn.sigmoid(rmsnorm(h, g_ple_pre) @ w_ple_gate)
    h = h + rmsnorm((p @ w_ple_proj) * gate, g_ple_post)
    return h, state


def run_trunk(x, p, q_pos, caches, weights):
    h = x
    per_layer = []
    for i in range(DEPTH):
        lw = [w[i] for w in weights]
        past = None if caches is None else [c[i] for c in caches]
        h, st = layer(h, p[i], q_pos, past, *lw)
        per_layer.append(st)
    stacked = [jnp.stack([st[j] for st in per_layer]) for j in range(N_STATE)]
    return h, stacked


def setup_inputs(seed: int = 0) -> dict:
    key = jax.random.key(seed)
    ks = iter(jax.random.split(key, 48))

    def nrm(shape, scale=1.0):
        return jax.random.normal(next(ks), shape, jnp.float32) * scale

    def gain(width):
        return 1.0 + nrm((DEPTH, width), 0.05)

    d = D_MODEL
    return {
        "x_prompt": nrm((BATCH, SEQ, d)),
        "x_sample": nrm((DEC_BATCH, DEC_SEQ, d)),
        "p_prompt": nrm((DEPTH, BATCH, SEQ, PLE_DIM)),
        "p_sample": nrm((DEPTH, DEC_BATCH, DEC_SEQ, PLE_DIM)),
        "cache_a_k": nrm((DEPTH, DEC_BATCH, PAST_LEN, H_A, HEAD_DIM)),
        "cache_a_v": nrm((DEPTH, DEC_BATCH, PAST_LEN, H_A, HEAD_DIM)),
        "cache_b_ckv": nrm((DEPTH, DEC_BATCH, PAST_LEN, KV_LORA)),
        "cache_b_krope": nrm((DEPTH, DEC_BATCH, PAST_LEN, MLA_ROPE)),
        "cache_c_k": nrm((DEPTH, DEC_BATCH, PAST_LEN, H_C, HEAD_DIM)),
        "cache_c_v": nrm((DEPTH, DEC_BATCH, PAST_LEN, H_C, HEAD_DIM)),
        "cache_c_logf": jax.nn.log_sigmoid(2.0 + nrm((DEPTH, DEC_BATCH, PAST_LEN, H_C))),
        "g_ff1_pre": gain(d),
        "g_ff1_post": gain(d),
        "w_ff1_gu": nrm((DEPTH, d, 2 * D_FF), d ** -0.5),
        "w_ff1_down": nrm((DEPTH, D_FF, d), D_FF ** -0.5),
        "g_mix_pre": gain(d),
        "g_mix_post": gain(d),
        "w_in": nrm((DEPTH, d, IN_COLS), d ** -0.5),
        "b_f": 2.0 + nrm((DEPTH, H_C), 0.1),
        "g_bq": gain(Q_LORA),
        "g_bkv": gain(KV_LORA),
        "w_uq": nrm((DEPTH, Q_LORA, H_B * (MLA_NOPE + MLA_ROPE)), Q_LORA ** -0.5),
        "w_ukv": nrm((DEPTH, KV_LORA, H_B * (MLA_NOPE + MLA_V)), KV_LORA ** -0.5),
        "g_grp": gain(MIX_WIDTH),
        "w_out": nrm((DEPTH, MIX_WIDTH, d), MIX_WIDTH ** -0.5),
        "g_ff2_pre": gain(d),
        "g_ff2_post": gain(d),
        "w_ff2_gu": nrm((DEPTH, d, 2 * D_FF), d ** -0.5),
        "w_ff2_down": nrm((DEPTH, D_FF, d), D_FF ** -0.5),
        "g_ple_pre": gain(d),
        "w_ple_gate": nrm((DEPTH, d, d), d ** -0.5),
        "w_ple_proj": nrm((DEPTH, PLE_DIM, d), PLE_DIM ** -0.5),
        "g_ple_post": gain(d),
    }


def reference(x_prompt, x_sample, p_prompt, p_sample, cache_a_k, cache_a_v, cache_b_ckv, cache_b_krope,
              cache_c_k, cache_c_v, cache_c_logf, g_ff1_pre, g_ff1_post, w_ff1_gu, w_ff1_down,
              g_mix_pre, g_mix_post, w_in, b_f, g_bq, g_bkv, w_uq, w_ukv, g_grp, w_out,
              g_ff2_pre, g_ff2_post, w_ff2_gu, w_ff2_down, g_ple_pre, w_ple_gate, w_ple_proj, g_ple_post):
    weights = (g_ff1_pre, g_ff1_post, w_ff1_gu, w_ff1_down, g_mix_pre, g_mix_post, w_in, b_f,
               g_bq, g_bkv, w_uq, w_ukv, g_grp, w_out, g_ff2_pre, g_ff2_post, w_ff2_gu, w_ff2_down,
               g_ple_pre, w_ple_gate, w_ple_proj, g_ple_post)
    pos_p = jnp.arange(x_prompt.shape[1], dtype=jnp.int32)
    y_prompt, sp = run_trunk(x_prompt, p_prompt, pos_p, None, weights)
    past_len = cache_a_k.shape[2]
    pos_s = past_len + jnp.arange(x_sample.shape[1], dtype=jnp.int32)
    caches = (cache_a_k, cache_a_v, cache_b_ckv, cache_b_krope, cache_c_k, cache_c_v, cache_c_logf)
    y_sample, ss = run_trunk(x_sample, p_sample, pos_s, caches, weights)
    a_k_p, a_v_p, b_ckv_p, b_krope_p, c_k_p, c_v_p, c_logf_p = sp
    a_k_s, a_v_s, b_ckv_s, b_krope_s, c_k_s, c_v_s, c_logf_s = ss
    return (y_prompt, y_sample, a_k_p, a_v_p, b_ckv_p, b_krope_p, c_k_p, c_v_p, c_logf_p,
            a_k_s, a_v_s, b_ckv_s, b_krope_s, c_k_s, c_v_s, c_logf_s)
```

```python
import numpy as np
import concourse.bass as bass
import concourse.mybir as mybir

F32 = mybir.dt.float32
BF16 = mybir.dt.bfloat16
AF = mybir.ActivationFunctionType
ALU = mybir.AluOpType

import os
NS_QK = int(os.environ.get('K_NS', '4'))
NS_MLA = int(os.environ.get('K_MLA', '4'))
USE_SIG = int(os.environ.get('K_SIG', '1'))
POOL_RES = int(os.environ.get('K_POOL', '1'))
STRICT_SAME = bool(int(os.environ.get('K_STRICT', '0')))
G = 64
SB_BYTES = 211968
PS_BYTES = 16384
NG_SB = SB_BYTES // G
NG_PS = PS_BYTES // G
NGT = 4 * (NG_SB + NG_PS)
ENG = ['pe', 'act', 'dve', 'pool', 'sp']
_ES = {F32: 4, BF16: 2}


class Reg:
    __slots__ = ('ap', 'gr')

    def __init__(self, ap, gr):
        self.ap = ap
        self.gr = gr


class Buf:
    def __init__(self, root, gbase, ngs, off, fshape, dtype, P=128, p0=0):
        self.root, self.gbase, self.ngs = root, gbase, ngs
        self.off, self.fshape, self.dtype, self.P, self.p0 = off, tuple(fshape), dtype, P, p0
        es = _ES[dtype]
        self.es = es
        n = int(np.prod(fshape))
        self.nbytes = n * es
        assert off % 4 == 0, (off, self.nbytes)
        ap = root[p0:p0 + P, off // 4: (off + self.nbytes + 3) // 4]
        if dtype != F32:
            ap = ap.bitcast(dtype)[:, 0:n]
        if len(fshape) > 1:
            names = ' '.join('d%d' % i for i in range(len(fshape)))
            kw = {'d%d' % i: int(s) for i, s in enumerate(fshape)}
            ap = ap.rearrange('p (%s) -> p %s' % (names, names), **kw)
        self.ap = ap
        st = [1] * len(fshape)
        for i in range(len(fshape) - 2, -1, -1):
            st[i] = st[i + 1] * fshape[i + 1]
        self.st = st
        self._cache = {}

    def view(self, fshape, dtype, boff=0, P=None, p0=None):
        return Buf(self.root, self.gbase, self.ngs, self.off + boff, fshape, dtype,
                   self.P if P is None else P, self.p0 if p0 is None else p0)

    def __getitem__(self, key):
        if not isinstance(key, tuple):
            key = (key,)
        key = key + (slice(None),) * (1 + len(self.fshape) - len(key))
        ck = tuple((k.start, k.stop) if isinstance(k, slice) else k for k in key)
        r = self._cache.get(ck)
        if r is not None:
            return r
        ps = key[0]
        if isinstance(ps, int):
            pa, pb = ps, ps + 1
            key = (slice(pa, pb),) + key[1:]
        else:
            pa, pb, _ = ps.indices(self.P)
        ap = self.ap[key]
        offs = np.zeros(1, dtype=np.int64)
        fk = key[1:]
        nd = len(fk)
        for d in range(nd - 1):
            k = fk[d]
            if isinstance(k, int):
                ix = np.array([k])
            else:
                a, b, _ = k.indices(self.fshape[d])
                ix = np.arange(a, b)
            offs = (offs[:, None] + ix[None, :] * self.st[d]).ravel()
        k = fk[-1]
        if isinstance(k, int):
            a, b = k, k + 1
        else:
            a, b, _ = k.indices(self.fshape[-1])
        s = self.off + (offs + a) * self.es
        e = self.off + (offs + b) * self.es - 1
        g0 = s // G
        g1 = e // G
        span = int((g1 - g0).max()) + 1
        gg = g0[:, None] + np.arange(span)[None, :]
        gg = np.unique(gg[gg <= g1[:, None]])
        if self.gbase > 0:
            gg = np.unique(gg * G // 2048)
        q0, q1 = (self.p0 + pa) // 32, (self.p0 + pb - 1) // 32
        if self.gbase > 0:
            q0, q1 = 0, 3
        gr = np.concatenate([self.gbase + q * self.ngs + gg for q in range(q0, q1 + 1)])
        r = Reg(ap, gr)
        self._cache[ck] = r
        return r


class Op:
    __slots__ = ('eng', 'fn', 'dma', 'deps', 'signal', 'val', 'idx')


class Prog:
    def __init__(self, nc, stack):
        self.nc = nc
        self.ops = []
        self.lw = np.full(NGT, -1, dtype=np.int64)
        self.lr = np.full((len(ENG) + 1, NGT), -1, dtype=np.int64)
        self.last_dma = {}
        self.arena_t = stack.enter_context(nc.sbuf_tensor("arena", [128, SB_BYTES // 4], F32))
        self.psum_t = stack.enter_context(nc.psum_tensor("psum", [128, PS_BYTES // 4], F32))
        self.aroot = self.arena_t[:, :]
        self.proot = self.psum_t[:, :]
        self.top = 0
        self.stack = stack

    def alloc(self, fshape, dtype, P=128, p0=0):
        n = int(np.prod(fshape)) * _ES[dtype]
        n = (n + 63) // 64 * 64
        off = self.top
        self.top += n
        assert self.top <= SB_BYTES, "SBUF arena overflow %d" % self.top
        return Buf(self.aroot, 0, NG_SB, off, fshape, dtype, P, p0)

    def psum(self, bank, fshape=(512,), dtype=F32, boff=0, P=128, p0=0):
        return Buf(self.proot, 4 * NG_SB, NG_PS, bank * 2048 + boff, fshape, dtype, P, p0)

    def add(self, eng, fn, reads=(), writes=(), dma=None):
        op = Op()
        op.eng, op.fn, op.dma = eng, fn, dma
        op.signal, op.val = False, None
        i = len(self.ops)
        op.idx = i
        ei = ENG.index(eng)
        deps = set()
        rg = [r.gr for r in reads if r is not None and len(r.gr)]
        wg = [w.gr for w in writes if w is not None and len(w.gr)]
        rg = np.concatenate(rg) if rg else np.zeros(0, dtype=np.int64)
        wg = np.concatenate(wg) if wg else np.zeros(0, dtype=np.int64)
        same_ok = dma is None
        if len(rg):
            for j in np.unique(self.lw[rg]):
                if j < 0:
                    continue
                pj = self.ops[j]
                if same_ok and pj.dma is None and pj.eng == eng and eng == 'pe':
                    continue
                deps.add(int(j))
            prg = rg[rg >= 4 * NG_SB]
            if len(prg):
                for e2 in range(len(ENG)):
                    if e2 != ei:
                        j = int(self.lr[e2][prg].max())
                        if j >= 0:
                            deps.add(j)
        if len(wg):
            for j in np.unique(self.lw[wg]):
                if j < 0:
                    continue
                pj = self.ops[j]
                if same_ok and pj.dma is None and pj.eng == eng and (eng == 'pe' or not STRICT_SAME):
                    continue
                deps.add(int(j))
            for e2 in range(len(ENG)):
                j = int(self.lr[e2][wg].max())
                if j < 0:
                    continue
                if same_ok and e2 == ei and (eng == 'pe' or not STRICT_SAME):
                    continue
                deps.add(j)
            for j in np.unique(self.lr[len(ENG)][wg]):
                if j >= 0:
                    deps.add(int(j))
        if dma is not None:
            pj = self.last_dma.get(dma)
            if pj is not None:
                deps.add(pj)
            self.last_dma[dma] = i
            if len(rg):
                for j in np.unique(self.lr[len(ENG)][rg]):
                    if j >= 0:
                        deps.add(int(j))
        deps.discard(i)
        op.deps = sorted(deps)
        for j in op.deps:
            self.ops[j].signal = True
        if len(rg):
            if dma is None:
                self.lr[ei][rg] = i
            else:
                self.lr[len(ENG)][rg] = i
        if len(wg):
            self.lw[wg] = i
            self.lr[:, wg] = -1
        self.ops.append(op)
        return op

    def emit(self):
        nc = self.nc
        stack = self.stack
        cnt = {e: 0 for e in ENG}
        dcnt = {}
        for op in self.ops:
            if op.dma is not None:
                dcnt[op.dma] = dcnt.get(op.dma, 0) + 16
                op.val = dcnt[op.dma]
            elif op.signal:
                cnt[op.eng] += 1
                op.val = cnt[op.eng]
        sems = {e: stack.enter_context(nc.semaphore("s_" + e)) for e in ENG}
        dsem = {k: stack.enter_context(nc.semaphore("d_" + k)) for k in dcnt}
        self.nsem = len(sems) + len(dsem)
        per = {e: [op for op in self.ops if op.eng == e] for e in ENG}
        ops = self.ops
        final = [(dsem[k], v) for k, v in dcnt.items()]

        def run(e, h):
            waited = {}
            for op in per[e]:
                need = {}
                for j in op.deps:
                    pj = ops[j]
                    if pj.dma is not None:
                        s = dsem[pj.dma]
                    else:
                        s = sems[pj.eng]
                    k = id(s)
                    if pj.val > need.get(k, (None, 0))[1]:
                        need[k] = (s, pj.val)
                for k, (s, v) in need.items():
                    if waited.get(k, 0) < v:
                        h.wait_ge(s, v)
                        waited[k] = v
                ins = op.fn(h)
                if op.dma is not None:
                    ins.then_inc(dsem[op.dma], 16)
                elif op.signal:
                    ins.then_inc(sems[e], 1)
            if e == 'sp':
                for s, v in final:
                    h.wait_ge(s, v)

        with nc.Block() as block:
            @block.tensor
            def _(h):
                run('pe', h)

            @block.scalar
            def _(h):
                run('act', h)

            @block.vector
            def _(h):
                run('dve', h)

            @block.gpsimd
            def _(h):
                run('pool', h)

            @block.sync
            def _(h):
                run('sp', h)

    def mm(self, out, lhsT, rhs, start=True, stop=True):
        return self.add('pe', lambda h: h.matmul(out.ap, lhsT.ap, rhs.ap, start=start, stop=stop, skip_group_check=True),
                        [lhsT, rhs], [out])

    def tr(self, out, in_, ident):
        return self.add('pe', lambda h: h.transpose(out.ap, in_.ap, ident.ap), [in_, ident], [out])

    def act(self, out, in_, func, bias=0.0, scale=1.0, eng='act'):
        rd = [in_]
        b = bias
        s = scale
        if isinstance(bias, Reg):
            rd.append(bias)
            b = bias.ap
        if isinstance(scale, Reg):
            rd.append(scale)
            s = scale.ap
        return self.add('act', lambda h: h.activation(out=out.ap, in_=in_.ap, func=func, bias=b, scale=s), rd, [out])

    def tt(self, out, in0, in1, op, eng='dve'):
        return self.add(eng, lambda h: h.tensor_tensor(out=out.ap, in0=in0.ap, in1=in1.ap, op=op), [in0, in1], [out])

    def ts(self, out, in0, s1, s2=None, op0=ALU.mult, op1=None, eng='dve'):
        rd = [in0]
        a1, a2 = s1, s2
        if isinstance(s1, Reg):
            rd.append(s1)
            a1 = s1.ap
        if isinstance(s2, Reg):
            rd.append(s2)
            a2 = s2.ap
        if op1 is None:
            return self.add(eng, lambda h: h.tensor_scalar(out=out.ap, in0=in0.ap, scalar1=a1, scalar2=None, op0=op0), rd, [out])
        return self.add(eng, lambda h: h.tensor_scalar(out=out.ap, in0=in0.ap, scalar1=a1, scalar2=a2, op0=op0, op1=op1), rd, [out])

    def stt(self, out, in0, scalar, in1, op0, op1):
        rd = [in0, in1]
        a = scalar
        if isinstance(scalar, Reg):
            rd.append(scalar)
            a = scalar.ap
        return self.add('dve', lambda h: h.scalar_tensor_tensor(out=out.ap, in0=in0.ap, scalar=a, in1=in1.ap, op0=op0, op1=op1), rd, [out])

    def copy(self, out, in_, eng='dve'):
        if eng == 'act':
            return self.add('act', lambda h: h.activation(out=out.ap, in_=in_.ap, func=AF.Copy), [in_], [out])
        return self.add(eng, lambda h: h.tensor_copy(out=out.ap, in_=in_.ap), [in_], [out])

    def memset(self, out, v, eng='pool'):
        return self.add(eng, lambda h: h.memset(out.ap, v), [], [out])

    def dma_in(self, q, out, src_ap, key):
        return self.add(q, lambda h: h.dma_start(out=out.ap, in_=src_ap), [], [out], dma=key)

    def dma_out(self, q, dst_ap, in_, key):
        return self.add(q, lambda h: h.dma_start(out=dst_ap, in_=in_.ap), [in_], [], dma=key)


from contextlib import ExitStack
from concourse.bass_utils import run_bass_kernel_spmd

D = 1024; KD = 8; TP = 2048; TS = 32; TT = 2080; PAST = 1024; DFF = 2816; NL = 2
TILES = [(0, 512), (512, 512), (1024, 512), (1536, 512), (2048, 32)]
STS = [[0, 1], [2, 3, 4]]
EPS = 1e-6
SB_SCALE = 0.125; FOX_SCALE = 0.125; MLA_SCALE = 96.0 ** -0.5
GN = ['g_ff1_pre', 'g_ff1_post', 'g_mix_pre', 'g_mix_post', 'g_ff2_pre', 'g_ff2_post', 'g_ple_pre', 'g_ple_post']
GC_BQ = 2 * 8 * 8
GC_BKV = GC_BQ + 4
GC_BF = GC_BKV + 2
NGC = GC_BF + 2
C_ONES, C_NTRI, C_NONES, C_ID, C_MS, C_MI, C_MC = [i * 128 for i in range(7)]
C_SELK = 7 * 128
C_SELM = C_SELK + 512
NCC = C_SELM + 4


def gcol(l, name, k):
    return (l * 8 + GN.index(name)) * 8 + k


def build_program(stage=99):
    nc = bass.Bass("TRN2", target_bir_lowering=False)
    dt_in = lambda n, s: nc.dram_tensor(n, list(s), F32, kind="ExternalInput").ap()
    dt_out = lambda n, s: nc.dram_tensor(n, list(s), F32, kind="ExternalOutput").ap()
    xT_d = dt_in("xT", (128, 8, TT)); pT_d = dt_in("pT", (NL, 128, 2, TT))
    gp_d = dt_in("gp", (128, NGC)); ggrp_d = dt_in("ggrp", (NL, 128, 1024)); cst_d = dt_in("cst", (128, NCC))
    ropeK_d = dt_in("ropeK", (32, 2, TT)); ropeQ_d = dt_in("ropeQ", (32, 2, TT))
    wgu_d = [dt_in("wgu%d" % i, (NL, 22, 128, 8, 256)) for i in (1, 2)]
    wd_d = [dt_in("wd%d" % i, (NL, 2, 8, 128, 11, 128)) for i in (1, 2)]
    winA_d = dt_in("winA", (NL, 128, 8, 512)); winVa_d = dt_in("winVa", (NL, 128, 8, 256))
    winC_d = dt_in("winC", (NL, 128, 8, 512)); winVc_d = dt_in("winVc", (NL, 128, 8, 256))
    winFl_d = dt_in("winFl", (NL, 128, 8, 36)); winB_d = dt_in("winB", (NL, 128, 8, 384))
    winKr_d = dt_in("winKr", (NL, 128, 8, 2, 96)); wuq_d = dt_in("wuq", (NL, 128, 2, 8, 2, 96))
    wukvk_d = dt_in("wukvk", (NL, 128, 8, 64)); wukvv_d = dt_in("wukvv", (NL, 128, 512))
    wout_d = dt_in("wout", (NL, 128, 8, 1024)); wgate_d = dt_in("wgate", (NL, 128, 8, 1024)); wproj_d = dt_in("wproj", (NL, 128, 2, 1024))
    akc_d = dt_in("akcT", (NL, 128, 2, PAST)); avc_d = dt_in("avc", (NL, 128, 8, 256))
    ckvc_d = dt_in("ckvcT", (NL, 128, PAST)); krc_d = dt_in("krcT", (NL, 32, PAST))
    ckc_d = dt_in("ckcT", (NL, 128, 2, PAST)); cvc_d = dt_in("cvc", (NL, 128, 8, 256)); clfc_d = dt_in("clfcT", (NL, 36, PAST))
    yT_d = dt_out("yT", (128, 8, TT))
    akT_o = dt_out("akT_o", (NL, 128, 2, TT)); av_o = dt_out("av_o", (NL, TT, 256))
    ckvT_o = dt_out("ckvT_o", (NL, 128, TT)); krT_o = dt_out("krT_o", (NL, 32, TT))
    ckT_o = dt_out("ckT_o", (NL, 128, 2, TT)); cv_o = dt_out("cv_o", (NL, TT, 256)); clfT_o = dt_out("clfT_o", (NL, 4, TT))

    st = ExitStack()
    with st:
        P = Prog(nc, st)
        hT = P.alloc((8, TT), F32)
        gp = P.alloc((NGC,), F32)
        cst = P.alloc((NCC,), BF16)
        P.dma_in('sp', gp[:], gp_d, 'gp')
        gph = P.alloc((NGC,), F32)
        P.ts(gph[:], gp[:], 0.5, op0=ALU.mult)
        P.dma_in('pool', cst[:], cst_d, 'cst')
        for k in range(8):
            P.dma_in('sp', hT[:, k, :], xT_d[:, k, :], 'x%d' % k)
        ones = cst[:, C_ONES:C_ONES + 128]
        ident = lambda n: cst[0:n, C_ID:C_ID + n]
        PBASE = P.top
        ring = [0]

        def bank():
            b = ring[0]
            ring[0] = (b + 1) % 4
            return b

        sqn = [0]

        def rstd_of(srcs, w, width, out, sq, post=1.0):
            b = P.psum(bank())
            n = len(srcs)
            for i, s in enumerate(srcs):
                t = sq[:, sqn[0] % 2, 0:w]
                sqn[0] += 1
                P.act(t, s, AF.Square)
                P.mm(b[:, 0:w], ones, t, start=(i == 0), stop=(i == n - 1))
            P.act(out, b[:, 0:w], AF.Ln, bias=EPS, scale=1.0 / width)
            P.act(out, out, AF.Exp, scale=-0.5)

        def ffn(l, which):
            P.top = PBASE
            gpre, gpost = ('g_ff1_pre', 'g_ff1_post') if which == 0 else ('g_ff2_pre', 'g_ff2_post')
            xn2 = [P.alloc((8, 1056), BF16) for _ in range(2)]; actb = P.alloc((11, 1056), BF16); fo = P.alloc((8, 1056), F32)
            rstd = P.alloc((1056,), F32); rpre = [P.alloc((1056,), F32) for _ in range(2)]; sq = P.alloc((2, 512), BF16); sg = P.alloc((3, 512), BF16)
            wg = [P.alloc((8, 256), BF16) for _ in range(3)]
            wdb = [P.alloc((11, 128), BF16) for _ in range(2)]
            tmp = P.alloc((512,), F32)
            nw = [0, 0]
            pending = [iter(())]
            for sti, stl in enumerate(STS):
                tl = [TILES[i] for i in stl]
                base = tl[0][0]
                for (t0, w) in tl:
                    c0 = t0 - base
                    rstd_of([hT[:, k, t0:t0 + w] for k in range(8)], w, D, rpre[sti][:, c0:c0 + w], sq)
                    for k in range(8):
                        P.stt(xn2[sti][:, k, c0:c0 + w], hT[:, k, t0:t0 + w], gp[:, gcol(l, gpre, k):gcol(l, gpre, k) + 1],
                              rpre[sti][:, c0:c0 + w], ALU.mult, ALU.mult)
            for sti, stl in enumerate(STS):
                tl = [TILES[i] for i in stl]
                base = tl[0][0]
                loc = [(t0 - base, w) for (t0, w) in tl]
                xn = xn2[sti]
                for hh in range(2):
                    for jj in range(11):
                        j = hh * 11 + jj
                        wt = wg[nw[0] % 3]; nw[0] += 1
                        P.dma_in('pool', wt[:], wgu_d[which][l, j], 'wg%d' % (nw[0] % 3))
                        gb = []
                        for (c0, w) in loc:
                            b = P.psum(bank())
                            for k in range(8):
                                P.mm(b[:, 0:w], wt[:, k, 0:128], xn[:, k, c0:c0 + w], start=(k == 0), stop=(k == 7))
                            gb.append(b)
                        sgt = []
                        for ii, ((c0, w), b) in enumerate(zip(loc, gb)):
                            s_ = sg[:, ii, 0:w]
                            P.act(s_, b[:, 0:w], AF.Silu)
                            sgt.append(s_)
                        for (c0, w), s_ in zip(loc, sgt):
                            b = P.psum(bank())
                            for k in range(8):
                                P.mm(b[:, 0:w], wt[:, k, 128:256], xn[:, k, c0:c0 + w], start=(k == 0), stop=(k == 7))
                            P.tt(actb[:, jj, c0:c0 + w], b[:, 0:w], s_, ALU.mult)
                        if hh == 0:
                            next(pending[0], None)
                    if hh == 0:
                        for _ in pending[0]:
                            pass
                    for oc in range(8):
                        wt = wdb[nw[1] % 2]; nw[1] += 1
                        P.dma_in('pool', wt[:], wd_d[which][l, hh, oc], 'wd%d' % (nw[1] % 2))
                        for (c0, w) in loc:
                            b = P.psum(bank())
                            for k in range(11):
                                P.mm(b[:, 0:w], wt[:, k, :], actb[:, k, c0:c0 + w], start=(k == 0), stop=(k == 10))
                            if hh == 0:
                                P.copy(fo[:, oc, c0:c0 + w], b[:, 0:w], eng='act')
                            else:
                                P.tt(fo[:, oc, c0:c0 + w], b[:, 0:w], fo[:, oc, c0:c0 + w], ALU.add)
                def post_gen(tl=tl, loc=loc):
                    for (t0, w), (c0, _) in zip(tl, loc):
                        rstd_of([fo[:, k, c0:c0 + w] for k in range(8)], w, D, rstd[:, c0:c0 + w], sq, post=0.5)
                        yield
                    for k in range(8):
                        for (t0, w), (c0, _) in zip(tl, loc):
                            P.tt(fo[:, k, c0:c0 + w], fo[:, k, c0:c0 + w], rstd[:, c0:c0 + w], ALU.mult, eng=('pool' if POOL_RES else 'dve'))
                            P.stt(hT[:, k, t0:t0 + w], fo[:, k, c0:c0 + w], gph[:, gcol(l, gpost, k):gcol(l, gpost, k) + 1],
                                  hT[:, k, t0:t0 + w], ALU.mult, ALU.add)
                        yield
                pending[0] = post_gen()
            for _ in pending[0]:
                pass

        def attn(kind, h, qf, w, nsub, qn, kbl, ops_bank, tmpb, extra):
            dvx = 64 if kind == 'sb' else 65
            o_ps = P.psum(ops_bank, (nsub, dvx), F32)
            first = [True]
            mask = {'sb': C_MS, 'fox': C_MI, 'mla': C_MC}[kind]
            if kind == 'sb':
                racc = tmpb['racc']
                P.memset(racc[:, 0:w], 0.0, eng='pool')
            nb = len(kbl)
            for bi, (kT, v, nk, diag, c0, negF) in enumerate(kbl):
                dw = min(128, w - c0)
                q = qf(c0, w)
                sid = tmpb['sid']
                if kind == 'sb':
                    zb = P.psum(bank())
                    P.mm(zb[0:nk, c0:w], kT, q)
                    et = tmpb['e'][sid % 2]; spt = tmpb['sp'][sid]
                    P.act(et[0:nk, c0:w], zb[0:nk, c0:w], AF.Exp)
                    P.act(spt[0:nk, c0:w], et[0:nk, c0:w], AF.Ln, bias=1.0)
                    if diag:
                        P.tt(spt[0:nk, c0:c0 + dw], spt[0:nk, c0:c0 + dw], cst[0:nk, mask:mask + dw], ALU.mult, eng='pool')
                    yield
                    lb = P.psum(bank())
                    P.mm(lb[0:nk, c0:w], kT, q, start=True, stop=False)
                    if nk < 128:
                        P.memset(spt[32:64, c0:w], 0.0); P.memset(spt[64:128, c0:w], 0.0)
                        P.mm(lb[0:nk, c0:w], cst[:, C_NTRI:C_NTRI + nk], spt[:, c0:w], start=False, stop=(bi == 0))
                    else:
                        P.mm(lb[0:nk, c0:w], cst[0:nk, C_NTRI:C_NTRI + nk], spt[0:nk, c0:w], start=False, stop=(bi == 0))
                    if bi > 0:
                        P.mm(lb[0:nk, c0:w], cst[:, C_NONES:C_NONES + nk], racc[:, c0:w], start=False, stop=True)
                    src = lb
                else:
                    sb_ = P.psum(bank())
                    if kind == 'fox':
                        P.mm(sb_[0:nk, c0:w], kT, q, start=True, stop=False)
                        P.mm(sb_[0:nk, c0:w], cst[:, C_SELK + h * 128:C_SELK + h * 128 + nk], extra['Fq'](c0, w), start=False, stop=True)
                    else:
                        P.mm(sb_[0:nk, c0:w], kT, q)
                    src = sb_
                pt = tmpb['p'][sid]
                P.act(pt[0:nk, c0:w], src[0:nk, c0:w], AF.Exp, bias=(negF if negF is not None else 0.0))
                if diag:
                    P.tt(pt[0:nk, c0:c0 + dw], pt[0:nk, c0:c0 + dw], cst[0:nk, mask:mask + dw], ALU.mult, eng='pool')
                if kind == 'sb' and bi < nb - 1:
                    P.tt(racc[0:nk, c0:w], racc[0:nk, c0:w], spt[0:nk, c0:w], ALU.add, eng='pool')
                yield
                for sbi in range(c0 // 128, nsub):
                    a = sbi * 128
                    bq = min(a + 128, w)
                    P.mm(o_ps[0:bq - a, sbi, :], pt[0:nk, a:bq], v, start=first[0], stop=(bi == nb - 1 and sbi == nsub - 1))
                    first[0] = False
                yield
            dst = extra['dst']
            if kind == 'sb':
                P.copy(dst, o_ps[0:qn, :, 0:64], eng='dve')
            else:
                rd = tmpb['rd']
                P.copy(rd[0:qn, 0:nsub, 0], o_ps[0:qn, :, 64], eng='dve')
                P.add('dve', lambda hh, a=rd[0:qn, 0:nsub, :]: hh.reciprocal(out=a.ap, in_=a.ap), [rd[0:qn, 0:nsub, :]], [rd[0:qn, 0:nsub, :]])
                a_ = rd[0:qn, 0:nsub, :]
                b_ = o_ps[0:qn, :, 0:64]
                P.add('dve', lambda hh, a_=a_, b_=b_, dst=dst: hh.tensor_tensor(out=dst.ap, in0=b_.ap, in1=a_.ap.to_broadcast([qn, nsub, 64]), op=ALU.mult),
                      [a_, b_], [dst])
            yield

        def interleave(gens):
            gens = list(gens)
            while gens:
                for g in list(gens):
                    try:
                        next(g)
                    except StopIteration:
                        gens.remove(g)

        def mixer(l):
            P.top = PBASE
            sq = P.alloc((2, 512), BF16)
            oTok = P.alloc((17, 1024), BF16)
            ggrp = P.alloc((1024,), F32)
            ATT = P.top
            rstd = P.alloc((TT,), F32)
            xnt = [P.alloc((8, 512), BF16) for _ in range(1)]
            stg = [P.alloc((512,), F32) for _ in range(2)]
            tmpb = {'e': [P.alloc((512,), F32) for _ in range(2)], 'sp': [P.alloc((512,), BF16) for _ in range(4)], 'p': [P.alloc((512,), BF16) for _ in range(4)],
                    'racc': None, 'rd': None, 'n': [0]}
            raccs = [P.alloc((512,), BF16) for _ in range(4)]
            rds = [P.alloc((4, 1), F32) for _ in range(4)]
            P.dma_in('sp', ggrp[:], ggrp_d[l], 'ggrp')
            MB = P.top
            nst = [0]; nx = [0]
            xnbufs = [xnt[0]]

            def stage_out(dst_ap, src, w, p0=0, p1=128, eng='dve'):
                s_ = stg[nst[0] % 2]; nst[0] += 1
                P.copy(s_[p0:p1, 0:w], src, eng=eng)
                P.dma_out('sp', dst_ap, s_[p0:p1, 0:w], 'so%d' % (nst[0] % 2))

            for (t0, w) in TILES:
                rstd_of([hT[:, k, t0:t0 + w] for k in range(8)], w, D, rstd[:, t0:t0 + w], sq)

            def xn_tile(ti):
                t0, w = TILES[ti]
                x = xnbufs[nx[0] % len(xnbufs)]; nx[0] += 1
                for k in range(8):
                    P.stt(x[:, k, 0:w], hT[:, k, t0:t0 + w], gp[:, gcol(l, 'g_mix_pre', k):gcol(l, 'g_mix_pre', k) + 1],
                          rstd[:, t0:t0 + w], ALU.mult, ALU.mult)
                return x

            def proj(x, w, wt, c0, m, out_rows=None):
                b = P.psum(bank())
                for k in range(8):
                    P.mm(b[0:m, 0:w], wt[:, k, c0:c0 + m], x[:, k, 0:w], start=(k == 0), stop=(k == 7))
                return b

            def vproj(x, w, wt, ncols, dstf, out_d, tb0):
                for s0 in range(0, w, 128):
                    n = min(128, w - s0)
                    b = P.psum(bank())
                    for k in range(8):
                        P.mm(b[0:n, 0:ncols], x[:, k, s0:s0 + n], wt[:, k, :], start=(k == 0), stop=(k == 7))
                    tb = tb0 + s0 // 128
                    dstf(tb, n, b)
                    s_ = stg[nst[0] % 2]; nst[0] += 1
                    P.copy(s_[0:n, 0:ncols], b[0:n, 0:ncols], eng='act')
                    P.dma_out('sp', out_d[tb * 128:tb * 128 + n, :], s_[0:n, 0:ncols], 'so%d' % (nst[0] % 2))

            def chain(*gs):
                for g_ in gs:
                    yield from g_

            def qk_attention(kind, wt, Kt, V, Kc, Vc, vstride, extra_fn):
                qb = [[P.alloc((512,), BF16) for _ in range(4)] for _ in range(2)]
                for par in range(2):
                    for h in range(4):
                        dz = (1 - h % 2) * 64
                        P.memset(qb[par][h][dz:dz + 64, :], 0.0)
                def prep(ti):
                    t0, w = TILES[ti]
                    x = xn_tile(ti)
                    for c in range(2):
                        b = proj(x, w, wt, c * 128, 128)
                        for hp in range(2):
                            P.ts(qb[ti % 2][2 * c + hp][hp * 64:hp * 64 + 64, 0:w], b[hp * 64:hp * 64 + 64, 0:w], SB_SCALE, op0=ALU.mult)

                prep(0)
                for ti, (t0, w) in enumerate(TILES):
                    if ti + 1 < 5:
                        prep(ti + 1)
                    gens = []
                    for h in range(4):
                        c, hp = h // 2, h % 2
                        pb = hp * 64
                        q = qb[ti % 2][h]
                        if ti < 4:
                            kbl = []
                            for kb in range(4 * ti + 3, -1, -1):
                                diag = kb >= 4 * ti
                                c0 = 128 * (kb - 4 * ti) if diag else 0
                                kbl.append((Kt[:, c, kb * 128:(kb + 1) * 128], V(kb, h, 128), 128, diag, c0,
                                            extra_fn('negF', kb, h, 128)))
                            nsub, qn, tb0 = 4, 128, 4 * ti
                        else:
                            kbl = [(Kt[:, c, TP:TP + 32], V(16, h, 32), 32, True, 0, extra_fn('negF', 16, h, 32))]
                            for kb in range(7, -1, -1):
                                kbl.append((Kc[:, c, kb * 128:(kb + 1) * 128], Vc(kb, h), 128, False, 0,
                                            extra_fn('negFc', kb, h, 128)))
                            nsub, qn, tb0 = 1, 32, 16
                        col = (0 if kind == 'sb' else 768) + h * 64
                        tb = dict(tmpb); tb['racc'] = raccs[h]; tb['rd'] = rds[h]; tb['sid'] = h
                        ex = {'dst': oTok[0:qn, tb0:tb0 + nsub, col:col + 64],
                              'Fq': (lambda a, b_, t0=t0: extra_fn('Fq', t0 + a, t0 + b_, 0))}
                        gens.append(attn(kind, h, (lambda a, b_, q=q: q[:, a:b_]), w, nsub, qn, kbl, 4 + h, tb, ex))
                    if NS_QK == 4:
                        interleave(gens)
                    else:
                        interleave(gens[0:2]); interleave(gens[2:4])

            if stage >= 2 and 'A' in MIXSEL:
                P.top = MB
                winA = P.alloc((8, 512), BF16); winVa = P.alloc((8, 256), BF16)
                kaT = P.alloc((2, TT), BF16); va = P.alloc((17, 256), BF16)
                kaTc = P.alloc((2, PAST), BF16); vac = P.alloc((8, 256), BF16)
                xnbufs[:] = [xnt[0], P.alloc((8, 512), BF16)]
                P.dma_in('pool', winA[:], winA_d[l], 'wA'); P.dma_in('pool', winVa[:], winVa_d[l], 'wVa')
                P.dma_in('pool', kaTc[:], akc_d[l], 'kc'); P.dma_in('pool', vac[:], avc_d[l], 'vc')
                for ti, (t0, w) in enumerate(TILES):
                    x = xn_tile(ti)
                    for c in range(2):
                        b = proj(x, w, winA, 256 + c * 128, 128)
                        P.copy(kaT[:, c, t0:t0 + w], b[:, 0:w], eng='act')
                        stage_out(akT_o[l, :, c, t0:t0 + w], b[:, 0:w], w)
                    vproj(x, w, winVa, 256, lambda tb, n, b: P.copy(va[0:n, tb, :], b[0:n, 0:256], eng='dve'), av_o[l], t0 // 128)
                if stage >= 3:
                    qk_attention('sb', winA, kaT, lambda kb, h, n: va[0:n, kb, h * 64:(h + 1) * 64], kaTc,
                                 lambda kb, h: vac[:, kb, h * 64:(h + 1) * 64], 64, lambda *a: None)

            xnbufs[:] = [xnt[0]]
            if stage >= 2 and 'C' in MIXSEL:
                P.top = MB
                winC = P.alloc((8, 512), BF16); winVc = P.alloc((8, 256), BF16); winFl = P.alloc((8, 36), BF16)
                kcT = P.alloc((2, TT), BF16); vc = P.alloc((17, 4, 65), BF16)
                kcTc = P.alloc((2, PAST), BF16); vcc = P.alloc((8, 4, 65), BF16)
                Frows = P.alloc((TT,), BF16); Frc = P.alloc((PAST,), BF16)
                negF = P.alloc((17, 4), F32); negFc = P.alloc((8, 4), F32)
                CMARK = P.top
                lf = P.alloc((512,), F32); Ft = P.alloc((512,), F32); lo = P.alloc((512,), F32); one1 = P.alloc((1,), F32); carry = P.alloc((1,), F32)
                clf = P.alloc((512,), F32)
                P.dma_in('pool', winC[:], winC_d[l], 'wA'); P.dma_in('pool', winVc[:], winVc_d[l], 'wVa'); P.dma_in('pool', winFl[:], winFl_d[l], 'wFl')
                P.dma_in('pool', kcTc[:], ckc_d[l], 'kc'); P.dma_in('pool', vcc[:, :, :, 0:64], cvc_d[l].rearrange('p b (h d) -> p b h d', h=4), 'vc')
                P.memset(vc[:, :, :, 64], 1.0); P.memset(vcc[:, :, :, 64], 1.0); P.memset(one1[:], 1.0)
                P.memset(Frows[:], 0.0); P.memset(Frc[:], 0.0)
                nbf = gp[0:36, GC_BF + l:GC_BF + l + 1]

                prev = None
                for ti, (t0, w) in enumerate(TILES):
                    x = xn_tile(ti)
                    for c in range(2):
                        b = proj(x, w, winC, 256 + c * 128, 128)
                        P.copy(kcT[:, c, t0:t0 + w], b[:, 0:w], eng='act')
                        stage_out(ckT_o[l, :, c, t0:t0 + w], b[:, 0:w], w)
                    vproj(x, w, winVc, 256, lambda tb, n, b: P.copy(vc[0:n, tb, :, 0:64], b.view((4, 64), F32)[0:n, :, :], eng='dve'), cv_o[l], t0 // 128)
                    b = proj(x, w, winFl, 0, 36)
                    P.act(lo[0:36, 0:w], b[0:36, 0:w], AF.Exp, bias=nbf, scale=1.0)
                    P.act(lo[0:36, 0:w], lo[0:36, 0:w], AF.Ln, bias=1.0)
                    P.ts(lf[0:36, 0:w], b[0:36, 0:w], nbf, op0=ALU.add)
                    P.tt(lf[0:36, 0:w], lf[0:36, 0:w], lo[0:36, 0:w], ALU.subtract)
                    P.dma_out('sp', clfT_o[l, :, t0:t0 + w], lf[0:4, 0:w], 'lfo')
                    if ti == 4:
                        for half in range(2):
                            a = half * 512
                            P.dma_in('sp', clf[0:36, :], clfc_d[l, :, a:a + 512], 'clf')
                            P.add('dve', lambda hh, a=a, init=(carry[0:36, :] if half else None): hh.tensor_tensor_scan(
                                out=Ft[0:36, 0:512].ap, data0=one1[0:36, :].ap.to_broadcast([36, 512]), data1=clf[0:36, :].ap,
                                initial=(init.ap if init is not None else 0.0), op0=ALU.mult, op1=ALU.add),
                                [one1[0:36, :], clf[0:36, :]] + ([carry[0:36, :]] if half else []), [Ft[0:36, 0:512]])
                            P.copy(carry[0:36, :], Ft[0:36, 511:512], eng='dve')
                            P.copy(Frc[0:36, a:a + 512], Ft[0:36, 0:512], eng='dve')
                            P.copy(lo[32:36, 0:512], Frc[32:36, a:a + 512], eng='dve')
                            P.tt(lo[32:36, 0:512], Ft[32:36, 0:512], lo[32:36, 0:512], ALU.subtract)
                            P.copy(Frc[32:36, a:a + 512], lo[32:36, 0:512], eng='dve')
                        init = carry[0:36, :]
                    else:
                        init = carry[0:36, :] if ti > 0 else None
                    P.add('dve', lambda hh, w=w, init=init: hh.tensor_tensor_scan(
                        out=Ft[0:36, 0:w].ap, data0=one1[0:36, :].ap.to_broadcast([36, w]), data1=lf[0:36, 0:w].ap,
                        initial=(init.ap if init is not None else 0.0), op0=ALU.mult, op1=ALU.add),
                        [one1[0:36, :], lf[0:36, 0:w]] + ([init] if init is not None else []), [Ft[0:36, 0:w]])
                    P.copy(carry[0:36, :], Ft[0:36, w - 1:w], eng='dve')
                    P.copy(Frows[0:36, t0:t0 + w], Ft[0:36, 0:w], eng='dve')
                    P.copy(lo[32:36, 0:w], Frows[32:36, t0:t0 + w], eng='dve')
                    P.tt(lo[32:36, 0:w], Ft[32:36, 0:w], lo[32:36, 0:w], ALU.subtract)
                    P.copy(Frows[32:36, t0:t0 + w], lo[32:36, 0:w], eng='dve')
                selm = cst[0:36, C_SELM:C_SELM + 4]
                for kb in range(17):
                    n = 128 if kb < 16 else 32
                    b = P.psum(bank())
                    P.mm(b[0:n, 0:4], Frows[0:36, kb * 128:kb * 128 + n], selm)
                    P.ts(negF[0:n, kb, :], b[0:n, 0:4], -1.0, op0=ALU.mult)
                for kb in range(8):
                    b = P.psum(bank())
                    P.mm(b[:, 0:4], Frc[0:36, kb * 128:(kb + 1) * 128], selm)
                    P.ts(negFc[:, kb, :], b[:, 0:4], -1.0, op0=ALU.mult)
                if stage >= 3:
                    P.top = CMARK

                    def exf(what, a, b_, n):
                        if what == 'negF':
                            return negF[0:n, a, b_:b_ + 1]
                        if what == 'negFc':
                            return negFc[0:n, a, b_:b_ + 1]
                        return Frows[:, a:b_]
                    qk_attention('fox', winC, kcT, lambda kb, h, n: vc[0:n, kb, h, :], kcTc, lambda kb, h: vcc[:, kb, h, :], 65, exf)

            if stage >= 2 and 'B' in MIXSEL:
                P.top = MB
                cqn = P.alloc((2, TT), BF16); ckvn = P.alloc((TT,), BF16); krT = P.alloc((TT,), BF16)
                ckvc = P.alloc((PAST,), BF16); krc = P.alloc((PAST,), BF16)
                wuq = P.alloc((2, 8, 2, 96), BF16); wukvk = P.alloc((8, 64), BF16); wukvv = P.alloc((512,), BF16)
                P.dma_in('pool', wuq[:], wuq_d[l], 'wA'); P.dma_in('pool', wukvk[:], wukvk_d[l], 'wVa'); P.dma_in('pool', wukvv[:], wukvv_d[l], 'wFl')
                P.dma_in('pool', ckvc[:], ckvc_d[l], 'kc'); P.dma_in('pool', krc[64:96, :], krc_d[l], 'vc')
                MB2 = P.top
                winB = P.alloc((8, 384), BF16); winKr = P.alloc((8, 2, 96), BF16)
                cq32 = P.alloc((3, 512), F32); rk = P.alloc((2, 512), F32); rt = P.alloc((2, 512), F32)
                P.dma_in('pool', winB[:], winB_d[l], 'wB'); P.dma_in('pool', winKr[:], winKr_d[l], 'wKr')
                for ti, (t0, w) in enumerate(TILES):
                    x = xn_tile(ti)
                    P.dma_in('sp', rk[64:96, :, 0:w], ropeK_d[:, :, t0:t0 + w], 'rk')
                    for c in range(2):
                        b = proj(x, w, winB, c * 128, 128)
                        P.copy(cq32[:, c, 0:w], b[:, 0:w], eng='act')
                    rstd_of([cq32[:, 0, 0:w], cq32[:, 1, 0:w]], w, 256, cq32[:, 2, 0:w], sq)
                    for c in range(2):
                        P.stt(cqn[:, c, t0:t0 + w], cq32[:, c, 0:w], gp[:, GC_BQ + l * 2 + c:GC_BQ + l * 2 + c + 1], cq32[:, 2, 0:w], ALU.mult, ALU.mult)
                    b = proj(x, w, winB, 256, 128)
                    P.copy(cq32[:, 0, 0:w], b[:, 0:w], eng='act')
                    rstd_of([cq32[:, 0, 0:w]], w, 128, cq32[:, 2, 0:w], sq)
                    s_ = stg[nst[0] % 2]; nst[0] += 1
                    P.stt(s_[:, 0:w], cq32[:, 0, 0:w], gp[:, GC_BKV + l:GC_BKV + l + 1], cq32[:, 2, 0:w], ALU.mult, ALU.mult)
                    P.copy(ckvn[:, t0:t0 + w], s_[:, 0:w], eng='act')
                    P.dma_out('sp', ckvT_o[l, :, t0:t0 + w], s_[:, 0:w], 'so%d' % (nst[0] % 2))
                    b1 = P.psum(bank()); b2 = P.psum(bank())
                    for k in range(8):
                        P.mm(b1[0:96, 0:w], winKr[:, k, 0, :], x[:, k, 0:w], start=(k == 0), stop=(k == 7))
                    for k in range(8):
                        P.mm(b2[0:96, 0:w], winKr[:, k, 1, :], x[:, k, 0:w], start=(k == 0), stop=(k == 7))
                    P.tt(rt[64:96, 0, 0:w], b1[64:96, 0:w], rk[64:96, 0, 0:w], ALU.mult)
                    P.tt(rt[64:96, 1, 0:w], b2[64:96, 0:w], rk[64:96, 1, 0:w], ALU.mult)
                    s_ = stg[nst[0] % 2]; nst[0] += 1
                    P.tt(s_[64:96, 0:w], rt[64:96, 0, 0:w], rt[64:96, 1, 0:w], ALU.add)
                    P.copy(krT[64:96, t0:t0 + w], s_[64:96, 0:w], eng='act')
                    P.dma_out('sp', krT_o[l, :, t0:t0 + w], s_[64:96, 0:w], 'so%d' % (nst[0] % 2))
                if stage >= 3:
                    P.top = MB2
                    Kh2 = [P.alloc((TT,), BF16) for _ in range(2)]; Khc2 = [P.alloc((PAST,), BF16) for _ in range(2)]
                    vb2 = [P.alloc((17, 65), BF16) for _ in range(2)]; vbc2 = [P.alloc((8, 65), BF16) for _ in range(2)]
                    qh = [P.alloc((512,), BF16) for _ in range(5)]
                    rq = P.alloc((2, 512), F32)
                    rt2 = tmpb['e']
                    for i_ in range(2):
                        P.memset(vb2[i_][:, :, 64], 1.0); P.memset(vbc2[i_][:, :, 64], 1.0)
                        P.memset(Kh2[i_][64:128, :], 0.0); P.memset(Khc2[i_][64:128, :], 0.0)
                    for i_ in range(5):
                        P.memset(qh[i_][64:128, :], 0.0)
                    qhs = [qh, raccs + [tmpb['sp'][0]]]
                    for i_ in range(5):
                        P.memset(qhs[1][i_][64:128, :], 0.0)

                    rqs = [rq, stg[0].view((2, 512), F32)]

                    def prologue(h):
                        Kh, Khc, vb, vbc = Kh2[h % 2], Khc2[h % 2], vb2[h % 2], vbc2[h % 2]
                        for (t0, w) in TILES:
                            b = P.psum(bank())
                            P.mm(b[0:64, 0:w], wukvk[:, h, :], ckvn[:, t0:t0 + w])
                            P.copy(Kh[0:64, t0:t0 + w], b[0:64, 0:w], eng='dve')
                        P.copy(Kh[64:96, :], krT[64:96, :], eng='pool')
                        for a in (0, 512):
                            b = P.psum(bank())
                            P.mm(b[0:64, 0:512], wukvk[:, h, :], ckvc[:, a:a + 512])
                            P.copy(Khc[0:64, a:a + 512], b[0:64, 0:512], eng='dve')
                        P.copy(Khc[64:96, :], krc[64:96, :], eng='pool')
                        for g0 in range(0, 17, 4):
                            b = P.psum(bank(), (4, 64), F32)
                            nb_ = min(4, 17 - g0)
                            for i in range(nb_):
                                kb = g0 + i
                                n = 128 if kb < 16 else 32
                                P.mm(b[0:n, i, :], ckvn[:, kb * 128:kb * 128 + n], wukvv[:, h * 64:(h + 1) * 64], start=(i == 0), stop=(i == nb_ - 1))
                            if g0 < 16:
                                P.copy(vb[:, g0:g0 + 4, 0:64], b[:, 0:4, :], eng='dve')
                            else:
                                P.copy(vb[0:32, 16, 0:64], b[0:32, 0, :], eng='dve')
                        for g0 in (0, 4):
                            b = P.psum(bank(), (4, 64), F32)
                            for i in range(4):
                                P.mm(b[:, i, :], ckvc[:, (g0 + i) * 128:(g0 + i + 1) * 128], wukvv[:, h * 64:(h + 1) * 64], start=(i == 0), stop=(i == 3))
                            P.copy(vbc[:, g0:g0 + 4, 0:64], b[:, 0:4, :], eng='dve')
                        gl = []
                        for ti, (t0, w) in enumerate(TILES):
                            rq = rqs[(h * 5 + ti) % 2]
                            P.dma_in('sp', rq[64:96, :, 0:w], ropeQ_d[:, :, t0:t0 + w], 'rq%d' % ((h * 5 + ti) % 2))
                            b1 = P.psum(bank()); b2 = P.psum(bank())
                            for k in range(2):
                                P.mm(b1[0:96, 0:w], wuq[:, k, h, 0, :], cqn[:, k, t0:t0 + w], start=(k == 0), stop=(k == 1))
                            for k in range(2):
                                P.mm(b2[0:96, 0:w], wuq[:, k, h, 1, :], cqn[:, k, t0:t0 + w], start=(k == 0), stop=(k == 1))
                            q = qhs[h % 2][ti]
                            P.act(q[0:64, 0:w], b1[0:64, 0:w], AF.Copy, scale=MLA_SCALE)
                            P.tt(rt2[0][64:96, 0:w], b1[64:96, 0:w], rq[64:96, 0, 0:w], ALU.mult)
                            P.tt(rt2[1][64:96, 0:w], b2[64:96, 0:w], rq[64:96, 1, 0:w], ALU.mult)
                            P.tt(q[64:96, 0:w], rt2[0][64:96, 0:w], rt2[1][64:96, 0:w], ALU.add)
                            if ti < 4:
                                kbl = []
                                for kb in range(4 * ti + 3, -1, -1):
                                    diag = kb >= 4 * ti
                                    c0 = 128 * (kb - 4 * ti) if diag else 0
                                    kbl.append((Kh[:, kb * 128:(kb + 1) * 128], vb[:, kb, :], 128, diag, c0, None))
                                nsub, qn, tb0 = 4, 128, 4 * ti
                            else:
                                kbl = [(Kh[:, TP:TP + 32], vb[0:32, 16, :], 32, False, 0, None)]
                                for kb in range(7, -1, -1):
                                    kbl.append((Khc[:, kb * 128:(kb + 1) * 128], vbc[:, kb, :], 128, False, 0, None))
                                nsub, qn, tb0 = 1, 32, 16
                            sid = {3: 0, 2: 1, 1: 2, 0: 3, 4: 3}[ti]
                            tb = dict(tmpb); tb['rd'] = rds[sid]; tb['sid'] = sid
                            ex = {'dst': oTok[0:qn, tb0:tb0 + nsub, 256 + h * 64:256 + (h + 1) * 64]}
                            gl.append(attn('mla', h, (lambda a, b_, q=q: q[:, a:b_]), w, nsub, qn, kbl, 4 + sid, tb, ex))
                        return gl

                    pend = prologue(0)
                    for h in range(8):
                        nxt = prologue(h + 1) if h < 7 else None
                        gl = pend
                        interleave([gl[3], gl[2], gl[1], chain(gl[0], gl[4])])
                        pend = nxt

            if stage >= 3:
                P.top = ATT
                ssq = P.alloc((17, 3), F32); junk = P.alloc((512,), BF16)
                wout = P.alloc((8, 1024), BF16)
                onT2 = [P.alloc((8, 512), BF16) for _ in range(2)]; mo2 = [P.alloc((8, 512), F32) for _ in range(3)]
                r22 = [P.alloc((512,), F32) for _ in range(3)]
                P.dma_in('pool', wout[:], wout_d[l], 'wout')
                grp = [(0, 256), (256, 512), (768, 256)]
                P.memset(ssq[:], 1.0, eng='dve')
                def gn(ti):
                    t0, w = TILES[ti]
                    b0 = t0 // 128
                    b1 = b0 + (w + 127) // 128
                    for tbk in range(b0, b1):
                        n = 128 if tbk < 16 else 32
                        for gi, (a, wd_) in enumerate(grp):
                            P.add('act', lambda hh, o=junk[0:n, 0:wd_], i=oTok[0:n, tbk, a:a + wd_], ac=ssq[0:n, tbk, gi:gi + 1]:
                                  hh.activation(out=o.ap, in_=i.ap, func=AF.Square, accum_out=ac.ap),
                                  [oTok[0:n, tbk, a:a + wd_]], [junk[0:n, 0:wd_], ssq[0:n, tbk, gi:gi + 1]])
                    for gi, (a, wd_) in enumerate(grp):
                        P.act(ssq[:, b0:b1, gi], ssq[:, b0:b1, gi], AF.Ln, bias=EPS, scale=1.0 / wd_)
                        P.act(ssq[:, b0:b1, gi], ssq[:, b0:b1, gi], AF.Exp, scale=-0.5)
                    for tbk in range(b0, b1):
                        n = 128 if tbk < 16 else 32
                        for gi, (a, wd_) in enumerate(grp):
                            P.stt(oTok[0:n, tbk, a:a + wd_], oTok[0:n, tbk, a:a + wd_], ssq[0:n, tbk, gi:gi + 1], ggrp[0:n, a:a + wd_], ALU.mult, ALU.mult)

                def wo_a(ti):
                    t0, w = TILES[ti]
                    onT, mo = onT2[ti % 2], mo2[ti % 3]
                    for c in range(8):
                        bt = P.psum(4 + c % 4, (1024,), BF16)
                        off = 0
                        for s0 in range(0, w, 128):
                            n = min(128, w - s0)
                            tbk = (t0 + s0) // 128
                            P.tr(bt[:, off + s0:off + s0 + n], oTok[0:n, tbk, c * 128:(c + 1) * 128], ident(n))
                        P.copy(onT[:, c, 0:w], bt[:, off:off + w], eng=('act' if c % 2 else 'dve'))
                    for oc in range(8):
                        b = P.psum(bank())
                        for k in range(8):
                            P.mm(b[:, 0:w], wout[:, k, oc * 128:(oc + 1) * 128], onT[:, k, 0:w], start=(k == 0), stop=(k == 7))
                        P.copy(mo[:, oc, 0:w], b[:, 0:w], eng=('act' if oc % 2 else 'dve'))

                def wo_b1(ti):
                    t0, w = TILES[ti]
                    mo, r2 = mo2[ti % 3], r22[ti % 3]
                    rstd_of([mo[:, k, 0:w] for k in range(8)], w, D, r2[:, 0:w], sq)
                    for k in range(8):
                        P.tt(mo[:, k, 0:w], mo[:, k, 0:w], r2[:, 0:w], ALU.mult, eng=('pool' if POOL_RES else 'dve'))

                def wo_b2(ti):
                    t0, w = TILES[ti]
                    mo = mo2[ti % 3]
                    for k in range(8):
                        P.stt(hT[:, k, t0:t0 + w], mo[:, k, 0:w], gp[:, gcol(l, 'g_mix_post', k):gcol(l, 'g_mix_post', k) + 1],
                              hT[:, k, t0:t0 + w], ALU.mult, ALU.add)

                gn(0); gn(1)
                for ti in range(5):
                    wo_a(ti)
                    if ti + 2 < 5:
                        gn(ti + 2)
                    if ti > 0:
                        wo_b1(ti - 1)
                    if ti > 1:
                        wo_b2(ti - 2)
                wo_b1(4); wo_b2(3); wo_b2(4)

        def ple(l):
            P.top = PBASE
            wgate = P.alloc((8, 1024), BF16); wproj = P.alloc((2, 1024), BF16)
            xn2 = [P.alloc((8, 512), BF16) for _ in range(2)]; pt = [P.alloc((2, 512), BF16) for _ in range(2)]
            v2 = [P.alloc((8, 512), F32) for _ in range(3)]; et = [P.alloc((512,), F32) for _ in range(2)]
            r1a = [P.alloc((512,), F32) for _ in range(2)]; r1b = [P.alloc((512,), F32) for _ in range(3)]
            sq = P.alloc((2, 512), BF16)
            P.dma_in('pool', wgate[:], wgate_d[l], 'wgate'); P.dma_in('pool', wproj[:], wproj_d[l], 'wproj')

            def ple_n(ti):
                t0, w = TILES[ti]
                p_, xn, r1 = pt[ti % 2], xn2[ti % 2], r1a[ti % 2]
                P.dma_in('pool', p_[:, :, 0:w], pT_d[l, :, :, t0:t0 + w], 'pt%d' % (ti % 2))
                rstd_of([hT[:, k, t0:t0 + w] for k in range(8)], w, D, r1[:, 0:w], sq)
                for k in range(8):
                    P.stt(xn[:, k, 0:w], hT[:, k, t0:t0 + w], gp[:, gcol(l, 'g_ple_pre', k):gcol(l, 'g_ple_pre', k) + 1], r1[:, 0:w], ALU.mult, ALU.mult)

            def ple_a(ti):
                t0, w = TILES[ti]
                p_, xn, v = pt[ti % 2], xn2[ti % 2], v2[ti % 3]
                for oc in range(8):
                    b = P.psum(bank())
                    for k in range(8):
                        P.mm(b[:, 0:w], wgate[:, k, oc * 128:(oc + 1) * 128], xn[:, k, 0:w], start=(k == 0), stop=(k == 7))
                    e = et[oc % 2]
                    if USE_SIG:
                        P.act(e[:, 0:w], b[:, 0:w], AF.Sigmoid)
                    else:
                        P.act(e[:, 0:w], b[:, 0:w], AF.Exp, scale=-1.0)
                        P.ts(e[:, 0:w], e[:, 0:w], 1.0, op0=ALU.add)
                        P.add('dve', lambda hh, a=e[:, 0:w]: hh.reciprocal(out=a.ap, in_=a.ap), [e[:, 0:w]], [e[:, 0:w]])
                    b2 = P.psum(bank())
                    for k in range(2):
                        P.mm(b2[:, 0:w], wproj[:, k, oc * 128:(oc + 1) * 128], p_[:, k, 0:w], start=(k == 0), stop=(k == 1))
                    P.tt(v[:, oc, 0:w], b2[:, 0:w], e[:, 0:w], ALU.mult)

            def ple_b1(ti):
                t0, w = TILES[ti]
                v, r1 = v2[ti % 3], r1b[ti % 3]
                rstd_of([v[:, k, 0:w] for k in range(8)], w, D, r1[:, 0:w], sq)
                for k in range(8):
                    P.tt(v[:, k, 0:w], v[:, k, 0:w], r1[:, 0:w], ALU.mult, eng=('pool' if POOL_RES else 'dve'))

            def ple_b2(ti):
                t0, w = TILES[ti]
                v = v2[ti % 3]
                for k in range(8):
                    P.stt(hT[:, k, t0:t0 + w], v[:, k, 0:w], gp[:, gcol(l, 'g_ple_post', k):gcol(l, 'g_ple_post', k) + 1],
                          hT[:, k, t0:t0 + w], ALU.mult, ALU.add)

            ple_n(0)
            for ti in range(5):
                if ti + 1 < 5:
                    ple_n(ti + 1)
                ple_a(ti)
                if ti > 0:
                    ple_b1(ti - 1)
                if ti > 1:
                    ple_b2(ti - 2)
            ple_b1(4); ple_b2(3); ple_b2(4)

        nlayers = NL if stage >= 4 else 1
        for l in range(nlayers):
            ffn(l, 0)
            if stage >= 2:
                mixer(l)
            if stage >= 4:
                ffn(l, 1)
                ple(l)
        for k in range(8):
            P.dma_out('sp', yT_d[:, k, :], hT[:, k, :], 'y%d' % k)
        P.emit()
        print("ops", len(P.ops), "sems", P.nsem, flush=True)
    return nc


def _blk(W, bw=None):
    Din, C = W.shape
    return np.ascontiguousarray(W.reshape(Din // 128, 128, C).transpose(1, 0, 2))


_NC_CACHE = {}
STAGE = 99
MIXSEL = 'ACB'


def kernel(**inp):
    f32 = np.float32
    g = {k: np.asarray(v, dtype=f32) for k, v in inp.items()}
    sh = {}
    gpk = np.zeros((128, NGC), f32)
    for l in range(NL):
        for n in GN:
            for k in range(8):
                gpk[:, gcol(l, n, k)] = g[n][l, k * 128:(k + 1) * 128]
        for c in range(2):
            gpk[:, GC_BQ + l * 2 + c] = g['g_bq'][l, c * 128:(c + 1) * 128]
        gpk[:, GC_BKV + l] = g['g_bkv'][l]
        gpk[0:4, GC_BF + l] = g['b_f'][l]
        gpk[32:36, GC_BF + l] = g['b_f'][l]
    sh['gp'] = gpk
    sh['ggrp'] = np.ascontiguousarray(np.broadcast_to(g['g_grp'][:, None, :], (NL, 128, 1024)))
    cst = np.zeros((128, NCC), f32)
    s_ = np.arange(128)[:, None]; t_ = np.arange(128)[None, :]
    cst[:, C_ONES:C_ONES + 128] = 1.0
    cst[:, C_NTRI:C_NTRI + 128] = -1.0 * (s_ >= t_)
    cst[:, C_NONES:C_NONES + 128] = -1.0
    cst[:, C_ID:C_ID + 128] = np.eye(128)
    cst[:, C_MS:C_MS + 128] = (s_ < t_)
    cst[:, C_MI:C_MI + 128] = (s_ <= t_)
    cst[:, C_MC:C_MC + 128] = ((s_ // 64) <= (t_ // 64))
    for h in range(4):
        cst[h, C_SELK + h * 128:C_SELK + (h + 1) * 128] = 1.0
        cst[32 + h, C_SELK + h * 128:C_SELK + (h + 1) * 128] = 1.0
        cst[h, C_SELM + h] = 1.0
        cst[32 + h, C_SELM + h] = 1.0
    sh['cst'] = cst
    pos = np.concatenate([np.arange(TP), PAST + np.arange(TS)]).astype(f32)
    inv = (10000.0 ** (-np.arange(16, dtype=f32) / 16)).astype(f32)
    ang = pos[None, :] * inv[:, None]
    cos, sin = np.cos(ang).astype(f32), np.sin(ang).astype(f32)
    rk = np.stack([np.concatenate([cos, cos], 0), np.concatenate([-sin, sin], 0)], 1).astype(f32)
    sh['ropeK'] = np.ascontiguousarray(rk)
    sh['ropeQ'] = np.ascontiguousarray(rk * f32(MLA_SCALE))
    for i, (gu, dn) in enumerate((('w_ff1_gu', 'w_ff1_down'), ('w_ff2_gu', 'w_ff2_down'))):
        wgu = np.zeros((NL, 22, 128, 8, 256), f32); wd = np.zeros((NL, 2, 8, 128, 11, 128), f32)
        for l in range(NL):
            Wb = _blk(g[gu][l])
            for j in range(22):
                wgu[l, j, :, :, 0:128] = Wb[:, :, j * 128:(j + 1) * 128]
                wgu[l, j, :, :, 128:256] = Wb[:, :, DFF + j * 128:DFF + (j + 1) * 128]
            Db = _blk(g[dn][l])
            for hh in range(2):
                for oc in range(8):
                    wd[l, hh, oc] = Db[:, hh * 11:(hh + 1) * 11, oc * 128:(oc + 1) * 128]
        sh['wgu%d' % (i + 1)] = wgu; sh['wd%d' % (i + 1)] = wd
    win = np.stack([_blk(g['w_in'][l]) for l in range(NL)])
    sh['winA'] = np.ascontiguousarray(win[..., 0:512]); sh['winVa'] = np.ascontiguousarray(win[..., 512:768])
    sh['winB'] = np.ascontiguousarray(win[..., 768:1152])
    kr = win[..., 1152:1184]
    wkr = np.zeros((NL, 128, 8, 2, 96), f32)
    wkr[..., 0, 64:96] = kr
    wkr[..., 1, 64:80] = kr[..., 16:32]; wkr[..., 1, 80:96] = kr[..., 0:16]
    sh['winKr'] = wkr
    sh['winC'] = np.ascontiguousarray(win[..., 1184:1696]); sh['winVc'] = np.ascontiguousarray(win[..., 1696:1952])
    wfl = np.zeros((NL, 128, 8, 36), f32)
    wfl[..., 0:4] = win[..., 1952:1956]; wfl[..., 32:36] = win[..., 1952:1956]
    sh['winFl'] = wfl
    uq = np.stack([_blk(g['w_uq'][l]) for l in range(NL)]).reshape(NL, 128, 2, 8, 96)
    wuq = np.zeros((NL, 128, 2, 8, 2, 96), f32)
    wuq[..., 0, :] = uq
    wuq[..., 1, 64:80] = uq[..., 80:96]; wuq[..., 1, 80:96] = uq[..., 64:80]
    sh['wuq'] = wuq
    ukv = g['w_ukv'].reshape(NL, 128, 8, 128)
    sh['wukvk'] = np.ascontiguousarray(ukv[..., 0:64]); sh['wukvv'] = np.ascontiguousarray(ukv[..., 64:128]).reshape(NL, 128, 512)
    sh['wout'] = np.stack([_blk(g['w_out'][l]) for l in range(NL)])
    sh['wgate'] = np.stack([_blk(g['w_ple_gate'][l]) for l in range(NL)])
    sh['wproj'] = np.stack([_blk(g['w_ple_proj'][l]) for l in range(NL)])
    in_maps = []
    for c in range(8):
        m = dict(sh)
        xa = np.concatenate([g['x_prompt'][c], g['x_sample'][c]], 0)
        m['xT'] = np.ascontiguousarray(xa.T.reshape(8, 128, TT).transpose(1, 0, 2))
        pa = np.concatenate([g['p_prompt'][:, c], g['p_sample'][:, c]], 1)
        m['pT'] = np.ascontiguousarray(pa.transpose(0, 2, 1).reshape(NL, 2, 128, TT).transpose(0, 2, 1, 3))
        for nm, src in (('akcT', 'cache_a_k'), ('ckcT', 'cache_c_k')):
            a = g[src][:, c].reshape(NL, PAST, 256)
            m[nm] = np.ascontiguousarray(a.transpose(0, 2, 1).reshape(NL, 2, 128, PAST).transpose(0, 2, 1, 3))
        for nm, src in (('avc', 'cache_a_v'), ('cvc', 'cache_c_v')):
            a = g[src][:, c].reshape(NL, 8, 128, 256)
            m[nm] = np.ascontiguousarray(a.transpose(0, 2, 1, 3))
        m['ckvcT'] = np.ascontiguousarray(g['cache_b_ckv'][:, c].transpose(0, 2, 1))
        m['krcT'] = np.ascontiguousarray(g['cache_b_krope'][:, c].transpose(0, 2, 1))
        lfT = g['cache_c_logf'][:, c].transpose(0, 2, 1)
        cl = np.zeros((NL, 36, PAST), f32); cl[:, 0:4] = lfT; cl[:, 32:36] = lfT
        m['clfcT'] = cl
        in_maps.append(m)
    if STAGE not in _NC_CACHE:
        _NC_CACHE[STAGE] = build_program(STAGE)
    nc = _NC_CACHE[STAGE]
    res = run_bass_kernel_spmd(nc, in_maps, core_ids=list(range(8)))
    R = res.results
    def fm(name, nch):
        a = np.stack([R[c][name] for c in range(8)])
        return a.transpose(0, 1, 4, 3, 2).reshape(8, NL, TT, nch * 128)
    yT = np.stack([R[c]['yT'] for c in range(8)])
    y = yT.transpose(0, 3, 2, 1).reshape(8, TT, D)
    ak = fm('akT_o', 2); ck = fm('ckT_o', 2)
    ckv = np.stack([R[c]['ckvT_o'] for c in range(8)]).transpose(0, 1, 3, 2)
    krr = np.stack([R[c]['krT_o'] for c in range(8)]).transpose(0, 1, 3, 2)
    lfo = np.stack([R[c]['clfT_o'] for c in range(8)]).transpose(0, 1, 3, 2)
    av = np.stack([R[c]['av_o'] for c in range(8)]); cv = np.stack([R[c]['cv_o'] for c in range(8)])

    def sp(a, shp):
        a = a.transpose(1, 0, 2, 3)
        return (np.ascontiguousarray(a[:, :, :TP]).reshape((NL, 8, TP) + shp).astype(f32),
                np.ascontiguousarray(a[:, :, TP:]).reshape((NL, 8, TS) + shp).astype(f32))
    akp, aks = sp(ak, (4, 64)); avp, avs = sp(av, (4, 64)); ckvp, ckvs = sp(ckv, (128,)); krp, krs = sp(krr, (32,))
    ckp, cks = sp(ck, (4, 64)); cvp, cvs = sp(cv, (4, 64)); lfp, lfs = sp(lfo, (4,))
    return (np.ascontiguousarray(y[:, :TP]).astype(f32), np.ascontiguousarray(y[:, TP:]).astype(f32),
            akp, avp, ckvp, krp, ckp, cvp, lfp, aks, avs, ckvs, krs, cks, cvs, lfs)
```

```python
import numpy as np
import concourse.bass as bass
import concourse.mybir as mybir

F32 = mybir.dt.float32
BF16 = mybir.dt.bfloat16
AF = mybir.ActivationFunctionType
ALU = mybir.AluOpType

import os
NS_QK = int(os.environ.get('K_NS', '4'))
NS_MLA = int(os.environ.get('K_MLA', '4'))
USE_SIG = int(os.environ.get('K_SIG', '1'))
POOL_RES = int(os.environ.get('K_POOL', '1'))
STRICT_SAME = bool(int(os.environ.get('K_STRICT', '0')))
G = 64
SB_BYTES = 211968
PS_BYTES = 16384
NG_SB = SB_BYTES // G
NG_PS = PS_BYTES // G
NGT = 4 * (NG_SB + NG_PS)
ENG = ['pe', 'act', 'dve', 'pool', 'sp']
_ES = {F32: 4, BF16: 2}


class Reg:
    __slots__ = ('ap', 'gr')

    def __init__(self, ap, gr):
        self.ap = ap
        self.gr = gr


class Buf:
    def __init__(self, root, gbase, ngs, off, fshape, dtype, P=128, p0=0):
        self.root, self.gbase, self.ngs = root, gbase, ngs
        self.off, self.fshape, self.dtype, self.P, self.p0 = off, tuple(fshape), dtype, P, p0
        es = _ES[dtype]
        self.es = es
        n = int(np.prod(fshape))
        self.nbytes = n * es
        assert off % 4 == 0, (off, self.nbytes)
        ap = root[p0:p0 + P, off // 4: (off + self.nbytes + 3) // 4]
        if dtype != F32:
            ap = ap.bitcast(dtype)[:, 0:n]
        if len(fshape) > 1:
            names = ' '.join('d%d' % i for i in range(len(fshape)))
            kw = {'d%d' % i: int(s) for i, s in enumerate(fshape)}
            ap = ap.rearrange('p (%s) -> p %s' % (names, names), **kw)
        self.ap = ap
        st = [1] * len(fshape)
        for i in range(len(fshape) - 2, -1, -1):
            st[i] = st[i + 1] * fshape[i + 1]
        self.st = st
        self._cache = {}

    def view(self, fshape, dtype, boff=0, P=None, p0=None):
        return Buf(self.root, self.gbase, self.ngs, self.off + boff, fshape, dtype,
                   self.P if P is None else P, self.p0 if p0 is None else p0)

    def __getitem__(self, key):
        if not isinstance(key, tuple):
            key = (key,)
        key = key + (slice(None),) * (1 + len(self.fshape) - len(key))
        ck = tuple((k.start, k.stop) if isinstance(k, slice) else k for k in key)
        r = self._cache.get(ck)
        if r is not None:
            return r
        ps = key[0]
        if isinstance(ps, int):
            pa, pb = ps, ps + 1
            key = (slice(pa, pb),) + key[1:]
        else:
            pa, pb, _ = ps.indices(self.P)
        ap = self.ap[key]
        offs = np.zeros(1, dtype=np.int64)
        fk = key[1:]
        nd = len(fk)
        for d in range(nd - 1):
            k = fk[d]
            if isinstance(k, int):
                ix = np.array([k])
            else:
                a, b, _ = k.indices(self.fshape[d])
                ix = np.arange(a, b)
            offs = (offs[:, None] + ix[None, :] * self.st[d]).ravel()
        k = fk[-1]
        if isinstance(k, int):
            a, b = k, k + 1
        else:
            a, b, _ = k.indices(self.fshape[-1])
        s = self.off + (offs + a) * self.es
        e = self.off + (offs + b) * self.es - 1
        g0 = s // G
        g1 = e // G
        span = int((g1 - g0).max()) + 1
        gg = g0[:, None] + np.arange(span)[None, :]
        gg = np.unique(gg[gg <= g1[:, None]])
        if self.gbase > 0:
            gg = np.unique(gg * G // 2048)
        q0, q1 = (self.p0 + pa) // 32, (self.p0 + pb - 1) // 32
        if self.gbase > 0:
            q0, q1 = 0, 3
        gr = np.concatenate([self.gbase + q * self.ngs + gg for q in range(q0, q1 + 1)])
        r = Reg(ap, gr)
        self._cache[ck] = r
        return r


class Op:
    __slots__ = ('eng', 'fn', 'dma', 'deps', 'signal', 'val', 'idx')


class Prog:
    def __init__(self, nc, stack):
        self.nc = nc
        self.ops = []
        self.lw = np.full(NGT, -1, dtype=np.int64)
        self.lr = np.full((len(ENG) + 1, NGT), -1, dtype=np.int64)
        self.last_dma = {}
        self.arena_t = stack.enter_context(nc.sbuf_tensor("arena", [128, SB_BYTES // 4], F32))
        self.psum_t = stack.enter_context(nc.psum_tensor("psum", [128, PS_BYTES // 4], F32))
        self.aroot = self.arena_t[:, :]
        self.proot = self.psum_t[:, :]
        self.top = 0
        self.stack = stack

    def alloc(self, fshape, dtype, P=128, p0=0):
        n = int(np.prod(fshape)) * _ES[dtype]
        n = (n + 63) // 64 * 64
        off = self.top
        self.top += n
        assert self.top <= SB_BYTES, "SBUF arena overflow %d" % self.top
        return Buf(self.aroot, 0, NG_SB, off, fshape, dtype, P, p0)

    def psum(self, bank, fshape=(512,), dtype=F32, boff=0, P=128, p0=0):
        return Buf(self.proot, 4 * NG_SB, NG_PS, bank * 2048 + boff, fshape, dtype, P, p0)

    def add(self, eng, fn, reads=(), writes=(), dma=None):
        op = Op()
        op.eng, op.fn, op.dma = eng, fn, dma
        op.signal, op.val = False, None
        i = len(self.ops)
        op.idx = i
        ei = ENG.index(eng)
        deps = set()
        rg = [r.gr for r in reads if r is not None and len(r.gr)]
        wg = [w.gr for w in writes if w is not None and len(w.gr)]
        rg = np.concatenate(rg) if rg else np.zeros(0, dtype=np.int64)
        wg = np.concatenate(wg) if wg else np.zeros(0, dtype=np.int64)
        same_ok = dma is None
        if len(rg):
            for j in np.unique(self.lw[rg]):
                if j < 0:
                    continue
                pj = self.ops[j]
                if same_ok and pj.dma is None and pj.eng == eng and eng == 'pe':
                    continue
                deps.add(int(j))
            prg = rg[rg >= 4 * NG_SB]
            if len(prg):
                for e2 in range(len(ENG)):
                    if e2 != ei:
                        j = int(self.lr[e2][prg].max())
                        if j >= 0:
                            deps.add(j)
        if len(wg):
            for j in np.unique(self.lw[wg]):
                if j < 0:
                    continue
                pj = self.ops[j]
                if same_ok and pj.dma is None and pj.eng == eng and (eng == 'pe' or not STRICT_SAME):
                    continue
                deps.add(int(j))
            for e2 in range(len(ENG)):
                j = int(self.lr[e2][wg].max())
                if j < 0:
                    continue
                if same_ok and e2 == ei and (eng == 'pe' or not STRICT_SAME):
                    continue
                deps.add(j)
            for j in np.unique(self.lr[len(ENG)][wg]):
                if j >= 0:
                    deps.add(int(j))
        if dma is not None:
            pj = self.last_dma.get(dma)
            if pj is not None:
                deps.add(pj)
            self.last_dma[dma] = i
            if len(rg):
                for j in np.unique(self.lr[len(ENG)][rg]):
                    if j >= 0:
                        deps.add(int(j))
        deps.discard(i)
        op.deps = sorted(deps)
        for j in op.deps:
            self.ops[j].signal = True
        if len(rg):
            if dma is None:
                self.lr[ei][rg] = i
            else:
                self.lr[len(ENG)][rg] = i
        if len(wg):
            self.lw[wg] = i
            self.lr[:, wg] = -1
        self.ops.append(op)
        return op

    def emit(self):
        nc = self.nc
        stack = self.stack
        cnt = {e: 0 for e in ENG}
        dcnt = {}
        for op in self.ops:
            if op.dma is not None:
                dcnt[op.dma] = dcnt.get(op.dma, 0) + 16
                op.val = dcnt[op.dma]
            elif op.signal:
                cnt[op.eng] += 1
                op.val = cnt[op.eng]
        sems = {e: stack.enter_context(nc.semaphore("s_" + e)) for e in ENG}
        dsem = {k: stack.enter_context(nc.semaphore("d_" + k)) for k in dcnt}
        self.nsem = len(sems) + len(dsem)
        per = {e: [op for op in self.ops if op.eng == e] for e in ENG}
        ops = self.ops
        final = [(dsem[k], v) for k, v in dcnt.items()]

        def run(e, h):
            waited = {}
            for op in per[e]:
                need = {}
                for j in op.deps:
                    pj = ops[j]
                    if pj.dma is not None:
                        s = dsem[pj.dma]
                    else:
                        s = sems[pj.eng]
                    k = id(s)
                    if pj.val > need.get(k, (None, 0))[1]:
                        need[k] = (s, pj.val)
                for k, (s, v) in need.items():
                    if waited.get(k, 0) < v:
                        h.wait_ge(s, v)
                        waited[k] = v
                ins = op.fn(h)
                if op.dma is not None:
                    ins.then_inc(dsem[op.dma], 16)
                elif op.signal:
                    ins.then_inc(sems[e], 1)
            if e == 'sp':
                for s, v in final:
                    h.wait_ge(s, v)

        with nc.Block() as block:
            @block.tensor
            def _(h):
                run('pe', h)

            @block.scalar
            def _(h):
                run('act', h)

            @block.vector
            def _(h):
                run('dve', h)

            @block.gpsimd
            def _(h):
                run('pool', h)

            @block.sync
            def _(h):
                run('sp', h)

    def mm(self, out, lhsT, rhs, start=True, stop=True):
        return self.add('pe', lambda h: h.matmul(out.ap, lhsT.ap, rhs.ap, start=start, stop=stop, skip_group_check=True),
                        [lhsT, rhs], [out])

    def tr(self, out, in_, ident):
        return self.add('pe', lambda h: h.transpose(out.ap, in_.ap, ident.ap), [in_, ident], [out])

    def act(self, out, in_, func, bias=0.0, scale=1.0, eng='act'):
        rd = [in_]
        b = bias
        s = scale
        if isinstance(bias, Reg):
            rd.append(bias)
            b = bias.ap
        if isinstance(scale, Reg):
            rd.append(scale)
            s = scale.ap
        return self.add('act', lambda h: h.activation(out=out.ap, in_=in_.ap, func=func, bias=b, scale=s), rd, [out])

    def tt(self, out, in0, in1, op, eng='dve'):
        return self.add(eng, lambda h: h.tensor_tensor(out=out.ap, in0=in0.ap, in1=in1.ap, op=op), [in0, in1], [out])

    def ts(self, out, in0, s1, s2=None, op0=ALU.mult, op1=None, eng='dve'):
        rd = [in0]
        a1, a2 = s1, s2
        if isinstance(s1, Reg):
            rd.append(s1)
            a1 = s1.ap
        if isinstance(s2, Reg):
            rd.append(s2)
            a2 = s2.ap
        if op1 is None:
            return self.add(eng, lambda h: h.tensor_scalar(out=out.ap, in0=in0.ap, scalar1=a1, scalar2=None, op0=op0), rd, [out])
        return self.add(eng, lambda h: h.tensor_scalar(out=out.ap, in0=in0.ap, scalar1=a1, scalar2=a2, op0=op0, op1=op1), rd, [out])

    def stt(self, out, in0, scalar, in1, op0, op1):
        rd = [in0, in1]
        a = scalar
        if isinstance(scalar, Reg):
            rd.append(scalar)
            a = scalar.ap
        return self.add('dve', lambda h: h.scalar_tensor_tensor(out=out.ap, in0=in0.ap, scalar=a, in1=in1.ap, op0=op0, op1=op1), rd, [out])

    def copy(self, out, in_, eng='dve'):
        if eng == 'act':
            return self.add('act', lambda h: h.activation(out=out.ap, in_=in_.ap, func=AF.Copy), [in_], [out])
        return self.add(eng, lambda h: h.tensor_copy(out=out.ap, in_=in_.ap), [in_], [out])

    def memset(self, out, v, eng='pool'):
        return self.add(eng, lambda h: h.memset(out.ap, v), [], [out])

    def dma_in(self, q, out, src_ap, key):
        return self.add(q, lambda h: h.dma_start(out=out.ap, in_=src_ap), [], [out], dma=key)

    def dma_out(self, q, dst_ap, in_, key):
        return self.add(q, lambda h: h.dma_start(out=dst_ap, in_=in_.ap), [in_], [], dma=key)


from contextlib import ExitStack
from concourse.bass_utils import run_bass_kernel_spmd

D = 1024; KD = 8; TP = 2048; TS = 32; TT = 2080; PAST = 1024; DFF = 2816; NL = 2
TILES = [(0, 512), (512, 512), (1024, 512), (1536, 512), (2048, 32)]
STS = [[0, 1], [2, 3, 4]]
EPS = 1e-6
SB_SCALE = 0.125; FOX_SCALE = 0.125; MLA_SCALE = 96.0 ** -0.5
GN = ['g_ff1_pre', 'g_ff1_post', 'g_mix_pre', 'g_mix_post', 'g_ff2_pre', 'g_ff2_post', 'g_ple_pre', 'g_ple_post']
GC_BQ = 2 * 8 * 8
GC_BKV = GC_BQ + 4
GC_BF = GC_BKV + 2
NGC = GC_BF + 2
C_ONES, C_NTRI, C_NONES, C_ID, C_MS, C_MI, C_MC = [i * 128 for i in range(7)]
C_SELK = 7 * 128
C_SELM = C_SELK + 512
NCC = C_SELM + 4


def gcol(l, name, k):
    return (l * 8 + GN.index(name)) * 8 + k


def build_program(stage=99):
    nc = bass.Bass("TRN2", target_bir_lowering=False)
    dt_in = lambda n, s: nc.dram_tensor(n, list(s), F32, kind="ExternalInput").ap()
    dt_out = lambda n, s: nc.dram_tensor(n, list(s), F32, kind="ExternalOutput").ap()
    xT_d = dt_in("xT", (128, 8, TT)); pT_d = dt_in("pT", (NL, 128, 2, TT))
    gp_d = dt_in("gp", (128, NGC)); ggrp_d = dt_in("ggrp", (NL, 128, 1024)); cst_d = dt_in("cst", (128, NCC))
    ropeK_d = dt_in("ropeK", (32, 2, TT)); ropeQ_d = dt_in("ropeQ", (32, 2, TT))
    wgu_d = [dt_in("wgu%d" % i, (NL, 22, 128, 8, 256)) for i in (1, 2)]
    wd_d = [dt_in("wd%d" % i, (NL, 2, 8, 128, 11, 128)) for i in (1, 2)]
    winA_d = dt_in("winA", (NL, 128, 8, 512)); winVa_d = dt_in("winVa", (NL, 128, 8, 256))
    winC_d = dt_in("winC", (NL, 128, 8, 512)); winVc_d = dt_in("winVc", (NL, 128, 8, 256))
    winFl_d = dt_in("winFl", (NL, 128, 8, 36)); winB_d = dt_in("winB", (NL, 128, 8, 384))
    winKr_d = dt_in("winKr", (NL, 128, 8, 2, 96)); wuq_d = dt_in("wuq", (NL, 128, 2, 8, 2, 96))
    wukvk_d = dt_in("wukvk", (NL, 128, 8, 64)); wukvv_d = dt_in("wukvv", (NL, 128, 512))
    wout_d = dt_in("wout", (NL, 128, 8, 1024)); wgate_d = dt_in("wgate", (NL, 128, 8, 1024)); wproj_d = dt_in("wproj", (NL, 128, 2, 1024))
    akc_d = dt_in("akcT", (NL, 128, 2, PAST)); avc_d = dt_in("avc", (NL, 128, 8, 256))
    ckvc_d = dt_in("ckvcT", (NL, 128, PAST)); krc_d = dt_in("krcT", (NL, 32, PAST))
    ckc_d = dt_in("ckcT", (NL, 128, 2, PAST)); cvc_d = dt_in("cvc", (NL, 128, 8, 256)); clfc_d = dt_in("clfcT", (NL, 36, PAST))
    yT_d = dt_out("yT", (128, 8, TT))
    akT_o = dt_out("akT_o", (NL, 128, 2, TT)); av_o = dt_out("av_o", (NL, TT, 256))
    ckvT_o = dt_out("ckvT_o", (NL, 128, TT)); krT_o = dt_out("krT_o", (NL, 32, TT))
    ckT_o = dt_out("ckT_o", (NL, 128, 2, TT)); cv_o = dt_out("cv_o", (NL, TT, 256)); clfT_o = dt_out("clfT_o", (NL, 4, TT))

    st = ExitStack()
    with st:
        P = Prog(nc, st)
        hT = P.alloc((8, TT), F32)
        gp = P.alloc((NGC,), F32)
        cst = P.alloc((NCC,), BF16)
        P.dma_in('sp', gp[:], gp_d, 'gp')
        gph = P.alloc((NGC,), F32)
        P.ts(gph[:], gp[:], 0.5, op0=ALU.mult)
        P.dma_in('pool', cst[:], cst_d, 'cst')
        for k in range(8):
            P.dma_in('sp', hT[:, k, :], xT_d[:, k, :], 'x%d' % k)
        ones = cst[:, C_ONES:C_ONES + 128]
        ident = lambda n: cst[0:n, C_ID:C_ID + n]
        PBASE = P.top
        ring = [0]

        def bank():
            b = ring[0]
            ring[0] = (b + 1) % 4
            return b

        sqn = [0]

        def rstd_of(srcs, w, width, out, sq, post=1.0):
            b = P.psum(bank())
            n = len(srcs)
            for i, s in enumerate(srcs):
                t = sq[:, sqn[0] % 2, 0:w]
                sqn[0] += 1
                P.act(t, s, AF.Square)
                P.mm(b[:, 0:w], ones, t, start=(i == 0), stop=(i == n - 1))
            P.act(out, b[:, 0:w], AF.Ln, bias=EPS, scale=1.0 / width)
            P.act(out, out, AF.Exp, scale=-0.5)

        def ffn(l, which):
            P.top = PBASE
            gpre, gpost = ('g_ff1_pre', 'g_ff1_post') if which == 0 else ('g_ff2_pre', 'g_ff2_post')
            xn2 = [P.alloc((8, 1056), BF16) for _ in range(2)]; actb = P.alloc((11, 1056), BF16); fo = P.alloc((8, 1056), F32)
            rstd = P.alloc((1056,), F32); rpre = [P.alloc((1056,), F32) for _ in range(2)]; sq = P.alloc((2, 512), BF16); sg = P.alloc((3, 512), BF16)
            wg = [P.alloc((8, 256), BF16) for _ in range(3)]
            wdb = [P.alloc((11, 128), BF16) for _ in range(2)]
            tmp = P.alloc((512,), F32)
            nw = [0, 0]
            pending = [iter(())]
            for sti, stl in enumerate(STS):
                tl = [TILES[i] for i in stl]
                base = tl[0][0]
                for (t0, w) in tl:
                    c0 = t0 - base
                    rstd_of([hT[:, k, t0:t0 + w] for k in range(8)], w, D, rpre[sti][:, c0:c0 + w], sq)
                    for k in range(8):
                        P.stt(xn2[sti][:, k, c0:c0 + w], hT[:, k, t0:t0 + w], gp[:, gcol(l, gpre, k):gcol(l, gpre, k) + 1],
                              rpre[sti][:, c0:c0 + w], ALU.mult, ALU.mult)
            for sti, stl in enumerate(STS):
                tl = [TILES[i] for i in stl]
                base = tl[0][0]
                loc = [(t0 - base, w) for (t0, w) in tl]
                xn = xn2[sti]
                for hh in range(2):
                    for jj in range(11):
                        j = hh * 11 + jj
                        wt = wg[nw[0] % 3]; nw[0] += 1
                        P.dma_in('pool', wt[:], wgu_d[which][l, j], 'wg%d' % (nw[0] % 3))
                        gb = []
                        for (c0, w) in loc:
                            b = P.psum(bank())
                            for k in range(8):
                                P.mm(b[:, 0:w], wt[:, k, 0:128], xn[:, k, c0:c0 + w], start=(k == 0), stop=(k == 7))
                            gb.append(b)
                        sgt = []
                        for ii, ((c0, w), b) in enumerate(zip(loc, gb)):
                            s_ = sg[:, ii, 0:w]
                            P.act(s_, b[:, 0:w], AF.Silu)
                            sgt.append(s_)
                        for (c0, w), s_ in zip(loc, sgt):
                            b = P.psum(bank())
                            for k in range(8):
                                P.mm(b[:, 0:w], wt[:, k, 128:256], xn[:, k, c0:c0 + w], start=(k == 0), stop=(k == 7))
                            P.tt(actb[:, jj, c0:c0 + w], b[:, 0:w], s_, ALU.mult)
                        if hh == 0:
                            next(pending[0], None)
                    if hh == 0:
                        for _ in pending[0]:
                            pass
                    for oc in range(8):
                        wt = wdb[nw[1] % 2]; nw[1] += 1
                        P.dma_in('pool', wt[:], wd_d[which][l, hh, oc], 'wd%d' % (nw[1] % 2))
                        for (c0, w) in loc:
                            b = P.psum(bank())
                            for k in range(11):
                                P.mm(b[:, 0:w], wt[:, k, :], actb[:, k, c0:c0 + w], start=(k == 0), stop=(k == 10))
                            if hh == 0:
                                P.copy(fo[:, oc, c0:c0 + w], b[:, 0:w], eng='act')
                            else:
                                P.tt(fo[:, oc, c0:c0 + w], b[:, 0:w], fo[:, oc, c0:c0 + w], ALU.add)
                def post_gen(tl=tl, loc=loc):
                    for (t0, w), (c0, _) in zip(tl, loc):
                        rstd_of([fo[:, k, c0:c0 + w] for k in range(8)], w, D, rstd[:, c0:c0 + w], sq, post=0.5)
                        yield
                    for k in range(8):
                        for (t0, w), (c0, _) in zip(tl, loc):
                            P.tt(fo[:, k, c0:c0 + w], fo[:, k, c0:c0 + w], rstd[:, c0:c0 + w], ALU.mult, eng=('pool' if POOL_RES else 'dve'))
                            P.stt(hT[:, k, t0:t0 + w], fo[:, k, c0:c0 + w], gph[:, gcol(l, gpost, k):gcol(l, gpost, k) + 1],
                                  hT[:, k, t0:t0 + w], ALU.mult, ALU.add)
                        yield
                pending[0] = post_gen()
            for _ in pending[0]:
                pass

        def attn(kind, h, qf, w, nsub, qn, kbl, ops_bank, tmpb, extra):
            dvx = 64 if kind == 'sb' else 65
            o_ps = P.psum(ops_bank, (nsub, dvx), F32)
            first = [True]
            mask = {'sb': C_MS, 'fox': C_MI, 'mla': C_MC}[kind]
            if kind == 'sb':
                racc = tmpb['racc']
                P.memset(racc[:, 0:w], 0.0, eng='pool')
            nb = len(kbl)
            for bi, (kT, v, nk, diag, c0, negF) in enumerate(kbl):
                dw = min(128, w - c0)
                q = qf(c0, w)
                sid = tmpb['sid']
                if kind == 'sb':
                    zb = P.psum(bank())
                    P.mm(zb[0:nk, c0:w], kT, q)
                    et = tmpb['e'][sid % 2]; spt = tmpb['sp'][sid]
                    P.act(et[0:nk, c0:w], zb[0:nk, c0:w], AF.Exp)
                    P.act(spt[0:nk, c0:w], et[0:nk, c0:w], AF.Ln, bias=1.0)
                    if diag:
                        P.tt(spt[0:nk, c0:c0 + dw], spt[0:nk, c0:c0 + dw], cst[0:nk, mask:mask + dw], ALU.mult, eng='dve')
                    yield
                    lb = P.psum(bank())
                    P.mm(lb[0:nk, c0:w], kT, q, start=True, stop=False)
                    if nk < 128:
                        P.memset(spt[32:64, c0:w], 0.0); P.memset(spt[64:128, c0:w], 0.0)
                        P.mm(lb[0:nk, c0:w], cst[:, C_NTRI:C_NTRI + nk], spt[:, c0:w], start=False, stop=(bi == 0))
                    else:
                        P.mm(lb[0:nk, c0:w], cst[0:nk, C_NTRI:C_NTRI + nk], spt[0:nk, c0:w], start=False, stop=(bi == 0))
                    if bi > 0:
                        P.mm(lb[0:nk, c0:w], cst[:, C_NONES:C_NONES + nk], racc[:, c0:w], start=False, stop=True)
                    src = lb
                else:
                    sb_ = P.psum(bank())
                    if kind == 'fox':
                        P.mm(sb_[0:nk, c0:w], kT, q, start=True, stop=False)
                        P.mm(sb_[0:nk, c0:w], cst[:, C_SELK + h * 128:C_SELK + h * 128 + nk], extra['Fq'](c0, w), start=False, stop=True)
                    else:
                        P.mm(sb_[0:nk, c0:w], kT, q)
                    src = sb_
                pt = tmpb['p'][sid]
                P.act(pt[0:nk, c0:w], src[0:nk, c0:w], AF.Exp, bias=(negF if negF is not None else 0.0))
                if diag:
                    P.tt(pt[0:nk, c0:c0 + dw], pt[0:nk, c0:c0 + dw], cst[0:nk, mask:mask + dw], ALU.mult, eng='dve')
                if kind == 'sb' and bi < nb - 1:
                    P.tt(racc[0:nk, c0:w], racc[0:nk, c0:w], spt[0:nk, c0:w], ALU.add, eng='dve')
                yield
                for sbi in range(c0 // 128, nsub):
                    a = sbi * 128
                    bq = min(a + 128, w)
                    P.mm(o_ps[0:bq - a, sbi, :], pt[0:nk, a:bq], v, start=first[0], stop=(bi == nb - 1 and sbi == nsub - 1))
                    first[0] = False
                yield
            dst = extra['dst']
            if kind == 'sb':
                P.copy(dst, o_ps[0:qn, :, 0:64], eng='dve')
            else:
                rd = tmpb['rd']
                P.copy(rd[0:qn, 0:nsub, 0], o_ps[0:qn, :, 64], eng='dve')
                P.add('dve', lambda hh, a=rd[0:qn, 0:nsub, :]: hh.reciprocal(out=a.ap, in_=a.ap), [rd[0:qn, 0:nsub, :]], [rd[0:qn, 0:nsub, :]])
                a_ = rd[0:qn, 0:nsub, :]
                b_ = o_ps[0:qn, :, 0:64]
                P.add('dve', lambda hh, a_=a_, b_=b_, dst=dst: hh.tensor_tensor(out=dst.ap, in0=b_.ap, in1=a_.ap.to_broadcast([qn, nsub, 64]), op=ALU.mult),
                      [a_, b_], [dst])
            yield

        def interleave(gens):
            gens = list(gens)
            while gens:
                for g in list(gens):
                    try:
                        next(g)
                    except StopIteration:
                        gens.remove(g)

        def mixer(l):
            P.top = PBASE
            sq = P.alloc((2, 512), BF16)
            oTok = P.alloc((17, 1024), BF16)
            ggrp = P.alloc((1024,), F32)
            ATT = P.top
            rstd = P.alloc((TT,), F32)
            xnt = [P.alloc((8, 512), BF16) for _ in range(1)]
            stg = [P.alloc((512,), F32) for _ in range(2)]
            tmpb = {'e': [P.alloc((512,), F32) for _ in range(2)], 'sp': [P.alloc((512,), BF16) for _ in range(4)], 'p': [P.alloc((512,), BF16) for _ in range(4)],
                    'racc': None, 'rd': None, 'n': [0]}
            raccs = [P.alloc((512,), BF16) for _ in range(4)]
            rds = [P.alloc((4, 1), F32) for _ in range(4)]
            P.dma_in('sp', ggrp[:], ggrp_d[l], 'ggrp')
            MB = P.top
            nst = [0]; nx = [0]
            xnbufs = [xnt[0]]

            def stage_out(dst_ap, src, w, p0=0, p1=128, eng='dve'):
                s_ = stg[nst[0] % 2]; nst[0] += 1
                P.copy(s_[p0:p1, 0:w], src, eng=eng)
                P.dma_out('sp', dst_ap, s_[p0:p1, 0:w], 'so%d' % (nst[0] % 2))

            for (t0, w) in TILES:
                rstd_of([hT[:, k, t0:t0 + w] for k in range(8)], w, D, rstd[:, t0:t0 + w], sq)

            def xn_tile(ti):
                t0, w = TILES[ti]
                x = xnbufs[nx[0] % len(xnbufs)]; nx[0] += 1
                for k in range(8):
                    P.stt(x[:, k, 0:w], hT[:, k, t0:t0 + w], gp[:, gcol(l, 'g_mix_pre', k):gcol(l, 'g_mix_pre', k) + 1],
                          rstd[:, t0:t0 + w], ALU.mult, ALU.mult)
                return x

            def proj(x, w, wt, c0, m, out_rows=None):
                b = P.psum(bank())
                for k in range(8):
                    P.mm(b[0:m, 0:w], wt[:, k, c0:c0 + m], x[:, k, 0:w], start=(k == 0), stop=(k == 7))
                return b

            def vproj(x, w, wt, ncols, dstf, out_d, tb0):
                for s0 in range(0, w, 128):
                    n = min(128, w - s0)
                    b = P.psum(bank())
                    for k in range(8):
                        P.mm(b[0:n, 0:ncols], x[:, k, s0:s0 + n], wt[:, k, :], start=(k == 0), stop=(k == 7))
                    tb = tb0 + s0 // 128
                    dstf(tb, n, b)
                    s_ = stg[nst[0] % 2]; nst[0] += 1
                    P.copy(s_[0:n, 0:ncols], b[0:n, 0:ncols], eng='act')
                    P.dma_out('sp', out_d[tb * 128:tb * 128 + n, :], s_[0:n, 0:ncols], 'so%d' % (nst[0] % 2))

            def chain(*gs):
                for g_ in gs:
                    yield from g_

            def qk_attention(kind, wt, Kt, V, Kc, Vc, vstride, extra_fn):
                qb = [[P.alloc((512,), BF16) for _ in range(4)] for _ in range(2)]
                for par in range(2):
                    for h in range(4):
                        dz = (1 - h % 2) * 64
                        P.memset(qb[par][h][dz:dz + 64, :], 0.0)
                def prep(ti):
                    t0, w = TILES[ti]
                    x = xn_tile(ti)
                    for c in range(2):
                        b = proj(x, w, wt, c * 128, 128)
                        for hp in range(2):
                            P.act(qb[ti % 2][2 * c + hp][hp * 64:hp * 64 + 64, 0:w], b[hp * 64:hp * 64 + 64, 0:w], AF.Copy, scale=SB_SCALE)

                prep(0)
                for ti, (t0, w) in enumerate(TILES):
                    if ti + 1 < 5:
                        prep(ti + 1)
                    gens = []
                    for h in range(4):
                        c, hp = h // 2, h % 2
                        pb = hp * 64
                        q = qb[ti % 2][h]
                        if ti < 4:
                            kbl = []
                            for kb in range(4 * ti + 3, -1, -1):
                                diag = kb >= 4 * ti
                                c0 = 128 * (kb - 4 * ti) if diag else 0
                                kbl.append((Kt[:, c, kb * 128:(kb + 1) * 128], V(kb, h, 128), 128, diag, c0,
                                            extra_fn('negF', kb, h, 128)))
                            nsub, qn, tb0 = 4, 128, 4 * ti
                        else:
                            kbl = [(Kt[:, c, TP:TP + 32], V(16, h, 32), 32, True, 0, extra_fn('negF', 16, h, 32))]
                            for kb in range(7, -1, -1):
                                kbl.append((Kc[:, c, kb * 128:(kb + 1) * 128], Vc(kb, h), 128, False, 0,
                                            extra_fn('negFc', kb, h, 128)))
                            nsub, qn, tb0 = 1, 32, 16
                        col = (0 if kind == 'sb' else 768) + h * 64
                        tb = dict(tmpb); tb['racc'] = raccs[h]; tb['rd'] = rds[h]; tb['sid'] = h
                        ex = {'dst': oTok[0:qn, tb0:tb0 + nsub, col:col + 64],
                              'Fq': (lambda a, b_, t0=t0: extra_fn('Fq', t0 + a, t0 + b_, 0))}
                        gens.append(attn(kind, h, (lambda a, b_, q=q: q[:, a:b_]), w, nsub, qn, kbl, 4 + h, tb, ex))
                    if NS_QK == 4:
                        interleave(gens)
                    else:
                        interleave(gens[0:2]); interleave(gens[2:4])

            if stage >= 2 and 'A' in MIXSEL:
                P.top = MB
                winA = P.alloc((8, 512), BF16); winVa = P.alloc((8, 256), BF16)
                kaT = P.alloc((2, TT), BF16); va = P.alloc((17, 256), BF16)
                kaTc = P.alloc((2, PAST), BF16); vac = P.alloc((8, 256), BF16)
                xnbufs[:] = [xnt[0], P.alloc((8, 512), BF16)]
                P.dma_in('pool', winA[:], winA_d[l], 'wA'); P.dma_in('pool', winVa[:], winVa_d[l], 'wVa')
                P.dma_in('pool', kaTc[:], akc_d[l], 'kc'); P.dma_in('pool', vac[:], avc_d[l], 'vc')
                for ti, (t0, w) in enumerate(TILES):
                    x = xn_tile(ti)
                    for c in range(2):
                        b = proj(x, w, winA, 256 + c * 128, 128)
                        P.copy(kaT[:, c, t0:t0 + w], b[:, 0:w], eng='act')
                        stage_out(akT_o[l, :, c, t0:t0 + w], b[:, 0:w], w)
                    vproj(x, w, winVa, 256, lambda tb, n, b: P.copy(va[0:n, tb, :], b[0:n, 0:256], eng='dve'), av_o[l], t0 // 128)
                if stage >= 3:
                    qk_attention('sb', winA, kaT, lambda kb, h, n: va[0:n, kb, h * 64:(h + 1) * 64], kaTc,
                                 lambda kb, h: vac[:, kb, h * 64:(h + 1) * 64], 64, lambda *a: None)

            xnbufs[:] = [xnt[0]]
            if stage >= 2 and 'C' in MIXSEL:
                P.top = MB
                winC = P.alloc((8, 512), BF16); winVc = P.alloc((8, 256), BF16); winFl = P.alloc((8, 36), BF16)
                kcT = P.alloc((2, TT), BF16); vc = P.alloc((17, 4, 65), BF16)
                kcTc = P.alloc((2, PAST), BF16); vcc = P.alloc((8, 4, 65), BF16)
                Frows = P.alloc((TT,), BF16); Frc = P.alloc((PAST,), BF16)
                negF = P.alloc((17, 4), F32); negFc = P.alloc((8, 4), F32)
                CMARK = P.top
                lf = P.alloc((512,), F32); Ft = P.alloc((512,), F32); lo = P.alloc((512,), F32); one1 = P.alloc((1,), F32); carry = P.alloc((1,), F32)
                clf = P.alloc((512,), F32)
                P.dma_in('pool', winC[:], winC_d[l], 'wA'); P.dma_in('pool', winVc[:], winVc_d[l], 'wVa'); P.dma_in('pool', winFl[:], winFl_d[l], 'wFl')
                P.dma_in('pool', kcTc[:], ckc_d[l], 'kc'); P.dma_in('pool', vcc[:, :, :, 0:64], cvc_d[l].rearrange('p b (h d) -> p b h d', h=4), 'vc')
                P.memset(vc[:, :, :, 64], 1.0); P.memset(vcc[:, :, :, 64], 1.0); P.memset(one1[:], 1.0)
                P.memset(Frows[:], 0.0); P.memset(Frc[:], 0.0)
                nbf = gp[0:36, GC_BF + l:GC_BF + l + 1]

                prev = None
                for ti, (t0, w) in enumerate(TILES):
                    x = xn_tile(ti)
                    for c in range(2):
                        b = proj(x, w, winC, 256 + c * 128, 128)
                        P.copy(kcT[:, c, t0:t0 + w], b[:, 0:w], eng='act')
                        stage_out(ckT_o[l, :, c, t0:t0 + w], b[:, 0:w], w)
                    vproj(x, w, winVc, 256, lambda tb, n, b: P.copy(vc[0:n, tb, :, 0:64], b.view((4, 64), F32)[0:n, :, :], eng='dve'), cv_o[l], t0 // 128)
                    b = proj(x, w, winFl, 0, 36)
                    P.act(lo[0:36, 0:w], b[0:36, 0:w], AF.Exp, bias=nbf, scale=1.0)
                    P.act(lo[0:36, 0:w], lo[0:36, 0:w], AF.Ln, bias=1.0)
                    P.ts(lf[0:36, 0:w], b[0:36, 0:w], nbf, op0=ALU.add)
                    P.tt(lf[0:36, 0:w], lf[0:36, 0:w], lo[0:36, 0:w], ALU.subtract)
                    P.dma_out('sp', clfT_o[l, :, t0:t0 + w], lf[0:4, 0:w], 'lfo')
                    if ti == 4:
                        for half in range(2):
                            a = half * 512
                            P.dma_in('sp', clf[0:36, :], clfc_d[l, :, a:a + 512], 'clf')
                            P.add('dve', lambda hh, a=a, init=(carry[0:36, :] if half else None): hh.tensor_tensor_scan(
                                out=Ft[0:36, 0:512].ap, data0=one1[0:36, :].ap.to_broadcast([36, 512]), data1=clf[0:36, :].ap,
                                initial=(init.ap if init is not None else 0.0), op0=ALU.mult, op1=ALU.add),
                                [one1[0:36, :], clf[0:36, :]] + ([carry[0:36, :]] if half else []), [Ft[0:36, 0:512]])
                            P.copy(carry[0:36, :], Ft[0:36, 511:512], eng='dve')
                            P.copy(Frc[0:36, a:a + 512], Ft[0:36, 0:512], eng='dve')
                            P.copy(lo[32:36, 0:512], Frc[32:36, a:a + 512], eng='dve')
                            P.tt(lo[32:36, 0:512], Ft[32:36, 0:512], lo[32:36, 0:512], ALU.subtract)
                            P.copy(Frc[32:36, a:a + 512], lo[32:36, 0:512], eng='dve')
                        init = carry[0:36, :]
                    else:
                        init = carry[0:36, :] if ti > 0 else None
                    P.add('dve', lambda hh, w=w, init=init: hh.tensor_tensor_scan(
                        out=Ft[0:36, 0:w].ap, data0=one1[0:36, :].ap.to_broadcast([36, w]), data1=lf[0:36, 0:w].ap,
                        initial=(init.ap if init is not None else 0.0), op0=ALU.mult, op1=ALU.add),
                        [one1[0:36, :], lf[0:36, 0:w]] + ([init] if init is not None else []), [Ft[0:36, 0:w]])
                    P.copy(carry[0:36, :], Ft[0:36, w - 1:w], eng='dve')
                    P.copy(Frows[0:36, t0:t0 + w], Ft[0:36, 0:w], eng='dve')
                    P.copy(lo[32:36, 0:w], Frows[32:36, t0:t0 + w], eng='dve')
                    P.tt(lo[32:36, 0:w], Ft[32:36, 0:w], lo[32:36, 0:w], ALU.subtract)
                    P.copy(Frows[32:36, t0:t0 + w], lo[32:36, 0:w], eng='dve')
                selm = cst[0:36, C_SELM:C_SELM + 4]
                for kb in range(17):
                    n = 128 if kb < 16 else 32
                    b = P.psum(bank())
                    P.mm(b[0:n, 0:4], Frows[0:36, kb * 128:kb * 128 + n], selm)
                    P.ts(negF[0:n, kb, :], b[0:n, 0:4], -1.0, op0=ALU.mult)
                for kb in range(8):
                    b = P.psum(bank())
                    P.mm(b[:, 0:4], Frc[0:36, kb * 128:(kb + 1) * 128], selm)
                    P.ts(negFc[:, kb, :], b[:, 0:4], -1.0, op0=ALU.mult)
                if stage >= 3:
                    P.top = CMARK

                    def exf(what, a, b_, n):
                        if what == 'negF':
                            return negF[0:n, a, b_:b_ + 1]
                        if what == 'negFc':
                            return negFc[0:n, a, b_:b_ + 1]
                        return Frows[:, a:b_]
                    qk_attention('fox', winC, kcT, lambda kb, h, n: vc[0:n, kb, h, :], kcTc, lambda kb, h: vcc[:, kb, h, :], 65, exf)

            if stage >= 2 and 'B' in MIXSEL:
                P.top = MB
                cqn = P.alloc((2, TT), BF16); ckvn = P.alloc((TT,), BF16); krT = P.alloc((TT,), BF16)
                ckvc = P.alloc((PAST,), BF16); krc = P.alloc((PAST,), BF16)
                wuq = P.alloc((2, 8, 2, 96), BF16); wukvk = P.alloc((8, 64), BF16); wukvv = P.alloc((512,), BF16)
                P.dma_in('pool', wuq[:], wuq_d[l], 'wA'); P.dma_in('pool', wukvk[:], wukvk_d[l], 'wVa'); P.dma_in('pool', wukvv[:], wukvv_d[l], 'wFl')
                P.dma_in('pool', ckvc[:], ckvc_d[l], 'kc'); P.dma_in('pool', krc[64:96, :], krc_d[l], 'vc')
                MB2 = P.top
                winB = P.alloc((8, 384), BF16); winKr = P.alloc((8, 2, 96), BF16)
                cq32 = P.alloc((3, 512), F32); rk = P.alloc((2, 512), F32); rt = P.alloc((2, 512), F32)
                P.dma_in('pool', winB[:], winB_d[l], 'wB'); P.dma_in('pool', winKr[:], winKr_d[l], 'wKr')
                for ti, (t0, w) in enumerate(TILES):
                    x = xn_tile(ti)
                    P.dma_in('sp', rk[64:96, :, 0:w], ropeK_d[:, :, t0:t0 + w], 'rk')
                    for c in range(2):
                        b = proj(x, w, winB, c * 128, 128)
                        P.copy(cq32[:, c, 0:w], b[:, 0:w], eng='act')
                    rstd_of([cq32[:, 0, 0:w], cq32[:, 1, 0:w]], w, 256, cq32[:, 2, 0:w], sq)
                    for c in range(2):
                        P.stt(cqn[:, c, t0:t0 + w], cq32[:, c, 0:w], gp[:, GC_BQ + l * 2 + c:GC_BQ + l * 2 + c + 1], cq32[:, 2, 0:w], ALU.mult, ALU.mult)
                    b = proj(x, w, winB, 256, 128)
                    P.copy(cq32[:, 0, 0:w], b[:, 0:w], eng='act')
                    rstd_of([cq32[:, 0, 0:w]], w, 128, cq32[:, 2, 0:w], sq)
                    s_ = stg[nst[0] % 2]; nst[0] += 1
                    P.stt(s_[:, 0:w], cq32[:, 0, 0:w], gp[:, GC_BKV + l:GC_BKV + l + 1], cq32[:, 2, 0:w], ALU.mult, ALU.mult)
                    P.copy(ckvn[:, t0:t0 + w], s_[:, 0:w], eng='act')
                    P.dma_out('sp', ckvT_o[l, :, t0:t0 + w], s_[:, 0:w], 'so%d' % (nst[0] % 2))
                    b1 = P.psum(bank()); b2 = P.psum(bank())
                    for k in range(8):
                        P.mm(b1[0:96, 0:w], winKr[:, k, 0, :], x[:, k, 0:w], start=(k == 0), stop=(k == 7))
                    for k in range(8):
                        P.mm(b2[0:96, 0:w], winKr[:, k, 1, :], x[:, k, 0:w], start=(k == 0), stop=(k == 7))
                    P.tt(rt[64:96, 0, 0:w], b1[64:96, 0:w], rk[64:96, 0, 0:w], ALU.mult)
                    P.tt(rt[64:96, 1, 0:w], b2[64:96, 0:w], rk[64:96, 1, 0:w], ALU.mult)
                    s_ = stg[nst[0] % 2]; nst[0] += 1
                    P.tt(s_[64:96, 0:w], rt[64:96, 0, 0:w], rt[64:96, 1, 0:w], ALU.add)
                    P.copy(krT[64:96, t0:t0 + w], s_[64:96, 0:w], eng='act')
                    P.dma_out('sp', krT_o[l, :, t0:t0 + w], s_[64:96, 0:w], 'so%d' % (nst[0] % 2))
                if stage >= 3:
                    P.top = MB2
                    Kh2 = [P.alloc((TT,), BF16) for _ in range(2)]; Khc2 = [P.alloc((PAST,), BF16) for _ in range(2)]
                    vb2 = [P.alloc((17, 65), BF16) for _ in range(2)]; vbc2 = [P.alloc((8, 65), BF16) for _ in range(2)]
                    qh = [P.alloc((512,), BF16) for _ in range(5)]
                    rq = P.alloc((2, 512), F32)
                    rt2 = tmpb['e']
                    for i_ in range(2):
                        P.memset(vb2[i_][:, :, 64], 1.0); P.memset(vbc2[i_][:, :, 64], 1.0)
                        P.memset(Kh2[i_][64:128, :], 0.0); P.memset(Khc2[i_][64:128, :], 0.0)
                    for i_ in range(5):
                        P.memset(qh[i_][64:128, :], 0.0)
                    qhs = [qh, raccs + [tmpb['sp'][0]]]
                    for i_ in range(5):
                        P.memset(qhs[1][i_][64:128, :], 0.0)

                    rqs = [rq, stg[0].view((2, 512), F32)]

                    def prologue(h):
                        Kh, Khc, vb, vbc = Kh2[h % 2], Khc2[h % 2], vb2[h % 2], vbc2[h % 2]
                        for (t0, w) in TILES:
                            b = P.psum(bank())
                            P.mm(b[0:64, 0:w], wukvk[:, h, :], ckvn[:, t0:t0 + w])
                            P.copy(Kh[0:64, t0:t0 + w], b[0:64, 0:w], eng='act')
                        P.copy(Kh[64:96, :], krT[64:96, :], eng='pool')
                        for a in (0, 512):
                            b = P.psum(bank())
                            P.mm(b[0:64, 0:512], wukvk[:, h, :], ckvc[:, a:a + 512])
                            P.copy(Khc[0:64, a:a + 512], b[0:64, 0:512], eng='act')
                        P.copy(Khc[64:96, :], krc[64:96, :], eng='pool')
                        for g0 in range(0, 17, 4):
                            b = P.psum(bank(), (4, 64), F32)
                            nb_ = min(4, 17 - g0)
                            for i in range(nb_):
                                kb = g0 + i
                                n = 128 if kb < 16 else 32
                                P.mm(b[0:n, i, :], ckvn[:, kb * 128:kb * 128 + n], wukvv[:, h * 64:(h + 1) * 64], start=(i == 0), stop=(i == nb_ - 1))
                            if g0 < 16:
                                P.copy(vb[:, g0:g0 + 4, 0:64], b[:, 0:4, :], eng='dve')
                            else:
                                P.copy(vb[0:32, 16, 0:64], b[0:32, 0, :], eng='dve')
                        for g0 in (0, 4):
                            b = P.psum(bank(), (4, 64), F32)
                            for i in range(4):
                                P.mm(b[:, i, :], ckvc[:, (g0 + i) * 128:(g0 + i + 1) * 128], wukvv[:, h * 64:(h + 1) * 64], start=(i == 0), stop=(i == 3))
                            P.copy(vbc[:, g0:g0 + 4, 0:64], b[:, 0:4, :], eng='dve')
                        gl = []
                        for ti, (t0, w) in enumerate(TILES):
                            rq = rqs[(h * 5 + ti) % 2]
                            P.dma_in('sp', rq[64:96, :, 0:w], ropeQ_d[:, :, t0:t0 + w], 'rq%d' % ((h * 5 + ti) % 2))
                            b1 = P.psum(bank()); b2 = P.psum(bank())
                            for k in range(2):
                                P.mm(b1[0:96, 0:w], wuq[:, k, h, 0, :], cqn[:, k, t0:t0 + w], start=(k == 0), stop=(k == 1))
                            for k in range(2):
                                P.mm(b2[0:96, 0:w], wuq[:, k, h, 1, :], cqn[:, k, t0:t0 + w], start=(k == 0), stop=(k == 1))
                            q = qhs[h % 2][ti]
                            P.act(q[0:64, 0:w], b1[0:64, 0:w], AF.Copy, scale=MLA_SCALE)
                            P.tt(rt2[0][64:96, 0:w], b1[64:96, 0:w], rq[64:96, 0, 0:w], ALU.mult)
                            P.tt(rt2[1][64:96, 0:w], b2[64:96, 0:w], rq[64:96, 1, 0:w], ALU.mult)
                            P.tt(q[64:96, 0:w], rt2[0][64:96, 0:w], rt2[1][64:96, 0:w], ALU.add)
                            if ti < 4:
                                kbl = []
                                for kb in range(4 * ti + 3, -1, -1):
                                    diag = kb >= 4 * ti
                                    c0 = 128 * (kb - 4 * ti) if diag else 0
                                    kbl.append((Kh[:, kb * 128:(kb + 1) * 128], vb[:, kb, :], 128, diag, c0, None))
                                nsub, qn, tb0 = 4, 128, 4 * ti
                            else:
                                kbl = [(Kh[:, TP:TP + 32], vb[0:32, 16, :], 32, False, 0, None)]
                                for kb in range(7, -1, -1):
                                    kbl.append((Khc[:, kb * 128:(kb + 1) * 128], vbc[:, kb, :], 128, False, 0, None))
                                nsub, qn, tb0 = 1, 32, 16
                            sid = {3: 0, 2: 1, 1: 2, 0: 3, 4: 3}[ti]
                            tb = dict(tmpb); tb['rd'] = rds[sid]; tb['sid'] = sid
                            ex = {'dst': oTok[0:qn, tb0:tb0 + nsub, 256 + h * 64:256 + (h + 1) * 64]}
                            gl.append(attn('mla', h, (lambda a, b_, q=q: q[:, a:b_]), w, nsub, qn, kbl, 4 + sid, tb, ex))
                        return gl

                    pend = prologue(0)
                    for h in range(8):
                        nxt = prologue(h + 1) if h < 7 else None
                        gl = pend
                        interleave([gl[3], gl[2], gl[1], chain(gl[0], gl[4])])
                        pend = nxt

            if stage >= 3:
                P.top = ATT
                ssq = P.alloc((17, 3), F32); junk = P.alloc((512,), BF16)
                wout = P.alloc((8, 1024), BF16)
                onT2 = [P.alloc((8, 512), BF16) for _ in range(2)]; mo2 = [P.alloc((8, 512), F32) for _ in range(3)]
                r22 = [P.alloc((512,), F32) for _ in range(3)]
                P.dma_in('pool', wout[:], wout_d[l], 'wout')
                grp = [(0, 256), (256, 512), (768, 256)]
                P.memset(ssq[:], 1.0, eng='dve')
                def gn(ti):
                    t0, w = TILES[ti]
                    b0 = t0 // 128
                    b1 = b0 + (w + 127) // 128
                    for tbk in range(b0, b1):
                        n = 128 if tbk < 16 else 32
                        for gi, (a, wd_) in enumerate(grp):
                            P.add('act', lambda hh, o=junk[0:n, 0:wd_], i=oTok[0:n, tbk, a:a + wd_], ac=ssq[0:n, tbk, gi:gi + 1]:
                                  hh.activation(out=o.ap, in_=i.ap, func=AF.Square, accum_out=ac.ap),
                                  [oTok[0:n, tbk, a:a + wd_]], [junk[0:n, 0:wd_], ssq[0:n, tbk, gi:gi + 1]])
                    for gi, (a, wd_) in enumerate(grp):
                        P.act(ssq[:, b0:b1, gi], ssq[:, b0:b1, gi], AF.Ln, bias=EPS, scale=1.0 / wd_)
                        P.act(ssq[:, b0:b1, gi], ssq[:, b0:b1, gi], AF.Exp, scale=-0.5)
                    for tbk in range(b0, b1):
                        n = 128 if tbk < 16 else 32
                        for gi, (a, wd_) in enumerate(grp):
                            P.stt(oTok[0:n, tbk, a:a + wd_], oTok[0:n, tbk, a:a + wd_], ssq[0:n, tbk, gi:gi + 1], ggrp[0:n, a:a + wd_], ALU.mult, ALU.mult)

                def wo_a(ti):
                    t0, w = TILES[ti]
                    onT, mo = onT2[ti % 2], mo2[ti % 3]
                    for c in range(8):
                        bt = P.psum(4 + c % 4, (1024,), BF16)
                        off = 0
                        for s0 in range(0, w, 128):
                            n = min(128, w - s0)
                            tbk = (t0 + s0) // 128
                            P.tr(bt[:, off + s0:off + s0 + n], oTok[0:n, tbk, c * 128:(c + 1) * 128], ident(n))
                        P.copy(onT[:, c, 0:w], bt[:, off:off + w], eng=('act' if c % 2 else 'dve'))
                    for oc in range(8):
                        b = P.psum(bank())
                        for k in range(8):
                            P.mm(b[:, 0:w], wout[:, k, oc * 128:(oc + 1) * 128], onT[:, k, 0:w], start=(k == 0), stop=(k == 7))
                        P.copy(mo[:, oc, 0:w], b[:, 0:w], eng=('act' if oc % 2 else 'dve'))

                def wo_b1(ti):
                    t0, w = TILES[ti]
                    mo, r2 = mo2[ti % 3], r22[ti % 3]
                    rstd_of([mo[:, k, 0:w] for k in range(8)], w, D, r2[:, 0:w], sq)
                    for k in range(8):
                        P.tt(mo[:, k, 0:w], mo[:, k, 0:w], r2[:, 0:w], ALU.mult, eng=('pool' if POOL_RES else 'dve'))

                def wo_b2(ti):
                    t0, w = TILES[ti]
                    mo = mo2[ti % 3]
                    for k in range(8):
                        P.stt(hT[:, k, t0:t0 + w], mo[:, k, 0:w], gp[:, gcol(l, 'g_mix_post', k):gcol(l, 'g_mix_post', k) + 1],
                              hT[:, k, t0:t0 + w], ALU.mult, ALU.add)

                gn(0); gn(1)
                for ti in range(5):
                    wo_a(ti)
                    if ti + 2 < 5:
                        gn(ti + 2)
                    if ti > 0:
                        wo_b1(ti - 1)
                    if ti > 1:
                        wo_b2(ti - 2)
                wo_b1(4); wo_b2(3); wo_b2(4)

        def ple(l):
            P.top = PBASE
            wgate = P.alloc((8, 1024), BF16); wproj = P.alloc((2, 1024), BF16)
            xn2 = [P.alloc((8, 512), BF16) for _ in range(2)]; pt = [P.alloc((2, 512), BF16) for _ in range(2)]
            v2 = [P.alloc((8, 512), F32) for _ in range(3)]; et = [P.alloc((512,), F32) for _ in range(2)]
            r1a = [P.alloc((512,), F32) for _ in range(2)]; r1b = [P.alloc((512,), F32) for _ in range(3)]
            sq = P.alloc((2, 512), BF16)
            P.dma_in('pool', wgate[:], wgate_d[l], 'wgate'); P.dma_in('pool', wproj[:], wproj_d[l], 'wproj')

            def ple_n(ti):
                t0, w = TILES[ti]
                p_, xn, r1 = pt[ti % 2], xn2[ti % 2], r1a[ti % 2]
                P.dma_in('pool', p_[:, :, 0:w], pT_d[l, :, :, t0:t0 + w], 'pt%d' % (ti % 2))
                rstd_of([hT[:, k, t0:t0 + w] for k in range(8)], w, D, r1[:, 0:w], sq)
                for k in range(8):
                    P.stt(xn[:, k, 0:w], hT[:, k, t0:t0 + w], gp[:, gcol(l, 'g_ple_pre', k):gcol(l, 'g_ple_pre', k) + 1], r1[:, 0:w], ALU.mult, ALU.mult)

            def ple_a(ti):
                t0, w = TILES[ti]
                p_, xn, v = pt[ti % 2], xn2[ti % 2], v2[ti % 3]
                for oc in range(8):
                    b = P.psum(bank())
                    for k in range(8):
                        P.mm(b[:, 0:w], wgate[:, k, oc * 128:(oc + 1) * 128], xn[:, k, 0:w], start=(k == 0), stop=(k == 7))
                    e = et[oc % 2]
                    if USE_SIG:
                        P.act(e[:, 0:w], b[:, 0:w], AF.Sigmoid)
                    else:
                        P.act(e[:, 0:w], b[:, 0:w], AF.Exp, scale=-1.0)
                        P.ts(e[:, 0:w], e[:, 0:w], 1.0, op0=ALU.add)
                        P.add('dve', lambda hh, a=e[:, 0:w]: hh.reciprocal(out=a.ap, in_=a.ap), [e[:, 0:w]], [e[:, 0:w]])
                    b2 = P.psum(bank())
                    for k in range(2):
                        P.mm(b2[:, 0:w], wproj[:, k, oc * 128:(oc + 1) * 128], p_[:, k, 0:w], start=(k == 0), stop=(k == 1))
                    P.tt(v[:, oc, 0:w], b2[:, 0:w], e[:, 0:w], ALU.mult)

            def ple_b1(ti):
                t0, w = TILES[ti]
                v, r1 = v2[ti % 3], r1b[ti % 3]
                rstd_of([v[:, k, 0:w] for k in range(8)], w, D, r1[:, 0:w], sq)
                for k in range(8):
                    P.tt(v[:, k, 0:w], v[:, k, 0:w], r1[:, 0:w], ALU.mult, eng=('pool' if POOL_RES else 'dve'))

            def ple_b2(ti):
                t0, w = TILES[ti]
                v = v2[ti % 3]
                for k in range(8):
                    P.stt(hT[:, k, t0:t0 + w], v[:, k, 0:w], gp[:, gcol(l, 'g_ple_post', k):gcol(l, 'g_ple_post', k) + 1],
                          hT[:, k, t0:t0 + w], ALU.mult, ALU.add)

            ple_n(0)
            for ti in range(5):
                if ti + 1 < 5:
                    ple_n(ti + 1)
                ple_a(ti)
                if ti > 0:
                    ple_b1(ti - 1)
                if ti > 1:
                    ple_b2(ti - 2)
            ple_b1(4); ple_b2(3); ple_b2(4)

        nlayers = NL if stage >= 4 else 1
        for l in range(nlayers):
            ffn(l, 0)
            if stage >= 2:
                mixer(l)
            if stage >= 4:
                ffn(l, 1)
                ple(l)
        for k in range(8):
            P.dma_out('sp', yT_d[:, k, :], hT[:, k, :], 'y%d' % k)
        P.emit()
        print("ops", len(P.ops), "sems", P.nsem, flush=True)
    return nc


def _blk(W, bw=None):
    Din, C = W.shape
    return np.ascontiguousarray(W.reshape(Din // 128, 128, C).transpose(1, 0, 2))


_NC_CACHE = {}
STAGE = 99
MIXSEL = 'ACB'


def kernel(**inp):
    f32 = np.float32
    g = {k: np.asarray(v, dtype=f32) for k, v in inp.items()}
    sh = {}
    gpk = np.zeros((128, NGC), f32)
    for l in range(NL):
        for n in GN:
            for k in range(8):
                gpk[:, gcol(l, n, k)] = g[n][l, k * 128:(k + 1) * 128]
        for c in range(2):
            gpk[:, GC_BQ + l * 2 + c] = g['g_bq'][l, c * 128:(c + 1) * 128]
        gpk[:, GC_BKV + l] = g['g_bkv'][l]
        gpk[0:4, GC_BF + l] = g['b_f'][l]
        gpk[32:36, GC_BF + l] = g['b_f'][l]
    sh['gp'] = gpk
    sh['ggrp'] = np.ascontiguousarray(np.broadcast_to(g['g_grp'][:, None, :], (NL, 128, 1024)))
    cst = np.zeros((128, NCC), f32)
    s_ = np.arange(128)[:, None]; t_ = np.arange(128)[None, :]
    cst[:, C_ONES:C_ONES + 128] = 1.0
    cst[:, C_NTRI:C_NTRI + 128] = -1.0 * (s_ >= t_)
    cst[:, C_NONES:C_NONES + 128] = -1.0
    cst[:, C_ID:C_ID + 128] = np.eye(128)
    cst[:, C_MS:C_MS + 128] = (s_ < t_)
    cst[:, C_MI:C_MI + 128] = (s_ <= t_)
    cst[:, C_MC:C_MC + 128] = ((s_ // 64) <= (t_ // 64))
    for h in range(4):
        cst[h, C_SELK + h * 128:C_SELK + (h + 1) * 128] = 1.0
        cst[32 + h, C_SELK + h * 128:C_SELK + (h + 1) * 128] = 1.0
        cst[h, C_SELM + h] = 1.0
        cst[32 + h, C_SELM + h] = 1.0
    sh['cst'] = cst
    pos = np.concatenate([np.arange(TP), PAST + np.arange(TS)]).astype(f32)
    inv = (10000.0 ** (-np.arange(16, dtype=f32) / 16)).astype(f32)
    ang = pos[None, :] * inv[:, None]
    cos, sin = np.cos(ang).astype(f32), np.sin(ang).astype(f32)
    rk = np.stack([np.concatenate([cos, cos], 0), np.concatenate([-sin, sin], 0)], 1).astype(f32)
    sh['ropeK'] = np.ascontiguousarray(rk)
    sh['ropeQ'] = np.ascontiguousarray(rk * f32(MLA_SCALE))
    for i, (gu, dn) in enumerate((('w_ff1_gu', 'w_ff1_down'), ('w_ff2_gu', 'w_ff2_down'))):
        wgu = np.zeros((NL, 22, 128, 8, 256), f32); wd = np.zeros((NL, 2, 8, 128, 11, 128), f32)
        for l in range(NL):
            Wb = _blk(g[gu][l])
            for j in range(22):
                wgu[l, j, :, :, 0:128] = Wb[:, :, j * 128:(j + 1) * 128]
                wgu[l, j, :, :, 128:256] = Wb[:, :, DFF + j * 128:DFF + (j + 1) * 128]
            Db = _blk(g[dn][l])
            for hh in range(2):
                for oc in range(8):
                    wd[l, hh, oc] = Db[:, hh * 11:(hh + 1) * 11, oc * 128:(oc + 1) * 128]
        sh['wgu%d' % (i + 1)] = wgu; sh['wd%d' % (i + 1)] = wd
    win = np.stack([_blk(g['w_in'][l]) for l in range(NL)])
    sh['winA'] = np.ascontiguousarray(win[..., 0:512]); sh['winVa'] = np.ascontiguousarray(win[..., 512:768])
    sh['winB'] = np.ascontiguousarray(win[..., 768:1152])
    kr = win[..., 1152:1184]
    wkr = np.zeros((NL, 128, 8, 2, 96), f32)
    wkr[..., 0, 64:96] = kr
    wkr[..., 1, 64:80] = kr[..., 16:32]; wkr[..., 1, 80:96] = kr[..., 0:16]
    sh['winKr'] = wkr
    sh['winC'] = np.ascontiguousarray(win[..., 1184:1696]); sh['winVc'] = np.ascontiguousarray(win[..., 1696:1952])
    wfl = np.zeros((NL, 128, 8, 36), f32)
    wfl[..., 0:4] = win[..., 1952:1956]; wfl[..., 32:36] = win[..., 1952:1956]
    sh['winFl'] = wfl
    uq = np.stack([_blk(g['w_uq'][l]) for l in range(NL)]).reshape(NL, 128, 2, 8, 96)
    wuq = np.zeros((NL, 128, 2, 8, 2, 96), f32)
    wuq[..., 0, :] = uq
    wuq[..., 1, 64:80] = uq[..., 80:96]; wuq[..., 1, 80:96] = uq[..., 64:80]
    sh['wuq'] = wuq
    ukv = g['w_ukv'].reshape(NL, 128, 8, 128)
    sh['wukvk'] = np.ascontiguousarray(ukv[..., 0:64]); sh['wukvv'] = np.ascontiguousarray(ukv[..., 64:128]).reshape(NL, 128, 512)
    sh['wout'] = np.stack([_blk(g['w_out'][l]) for l in range(NL)])
    sh['wgate'] = np.stack([_blk(g['w_ple_gate'][l]) for l in range(NL)])
    sh['wproj'] = np.stack([_blk(g['w_ple_proj'][l]) for l in range(NL)])
    in_maps = []
    for c in range(8):
        m = dict(sh)
        xa = np.concatenate([g['x_prompt'][c], g['x_sample'][c]], 0)
        m['xT'] = np.ascontiguousarray(xa.T.reshape(8, 128, TT).transpose(1, 0, 2))
        pa = np.concatenate([g['p_prompt'][:, c], g['p_sample'][:, c]], 1)
        m['pT'] = np.ascontiguousarray(pa.transpose(0, 2, 1).reshape(NL, 2, 128, TT).transpose(0, 2, 1, 3))
        for nm, src in (('akcT', 'cache_a_k'), ('ckcT', 'cache_c_k')):
            a = g[src][:, c].reshape(NL, PAST, 256)
            m[nm] = np.ascontiguousarray(a.transpose(0, 2, 1).reshape(NL, 2, 128, PAST).transpose(0, 2, 1, 3))
        for nm, src in (('avc', 'cache_a_v'), ('cvc', 'cache_c_v')):
            a = g[src][:, c].reshape(NL, 8, 128, 256)
            m[nm] = np.ascontiguousarray(a.transpose(0, 2, 1, 3))
        m['ckvcT'] = np.ascontiguousarray(g['cache_b_ckv'][:, c].transpose(0, 2, 1))
        m['krcT'] = np.ascontiguousarray(g['cache_b_krope'][:, c].transpose(0, 2, 1))
        lfT = g['cache_c_logf'][:, c].transpose(0, 2, 1)
        cl = np.zeros((NL, 36, PAST), f32); cl[:, 0:4] = lfT; cl[:, 32:36] = lfT
        m['clfcT'] = cl
        in_maps.append(m)
    if STAGE not in _NC_CACHE:
        _NC_CACHE[STAGE] = build_program(STAGE)
    nc = _NC_CACHE[STAGE]
    res = run_bass_kernel_spmd(nc, in_maps, core_ids=list(range(8)))
    R = res.results
    def fm(name, nch):
        a = np.stack([R[c][name] for c in range(8)])
        return a.transpose(0, 1, 4, 3, 2).reshape(8, NL, TT, nch * 128)
    yT = np.stack([R[c]['yT'] for c in range(8)])
    y = yT.transpose(0, 3, 2, 1).reshape(8, TT, D)
    ak = fm('akT_o', 2); ck = fm('ckT_o', 2)
    ckv = np.stack([R[c]['ckvT_o'] for c in range(8)]).transpose(0, 1, 3, 2)
    krr = np.stack([R[c]['krT_o'] for c in range(8)]).transpose(0, 1, 3, 2)
    lfo = np.stack([R[c]['clfT_o'] for c in range(8)]).transpose(0, 1, 3, 2)
    av = np.stack([R[c]['av_o'] for c in range(8)]); cv = np.stack([R[c]['cv_o'] for c in range(8)])

    def sp(a, shp):
        a = a.transpose(1, 0, 2, 3)
        return (np.ascontiguousarray(a[:, :, :TP]).reshape((NL, 8, TP) + shp).astype(f32),
                np.ascontiguousarray(a[:, :, TP:]).reshape((NL, 8, TS) + shp).astype(f32))
    akp, aks = sp(ak, (4, 64)); avp, avs = sp(av, (4, 64)); ckvp, ckvs = sp(ckv, (128,)); krp, krs = sp(krr, (32,))
    ckp, cks = sp(ck, (4, 64)); cvp, cvs = sp(cv, (4, 64)); lfp, lfs = sp(lfo, (4,))
    return (np.ascontiguousarray(y[:, :TP]).astype(f32), np.ascontiguousarray(y[:, TP:]).astype(f32),
            akp, avp, ckvp, krp, ckp, cvp, lfp, aks, avs, ckvs, krs, cks, cvs, lfs)
```

```python
import numpy as np
import concourse.bass as bass
import concourse.mybir as mybir

F32 = mybir.dt.float32
BF16 = mybir.dt.bfloat16
AF = mybir.ActivationFunctionType
ALU = mybir.AluOpType

import os
NS_QK = int(os.environ.get('K_NS', '4'))
NS_MLA = int(os.environ.get('K_MLA', '4'))
USE_SIG = int(os.environ.get('K_SIG', '1'))
POOL_RES = int(os.environ.get('K_POOL', '1'))
STRICT_SAME = bool(int(os.environ.get('K_STRICT', '0')))
G = 64
SB_BYTES = 211968
PS_BYTES = 16384
NG_SB = SB_BYTES // G
NG_PS = PS_BYTES // G
NGT = 4 * (NG_SB + NG_PS)
ENG = ['pe', 'act', 'dve', 'pool', 'sp']
_ES = {F32: 4, BF16: 2}


class Reg:
    __slots__ = ('ap', 'gr')

    def __init__(self, ap, gr):
        self.ap = ap
        self.gr = gr


class Buf:
    def __init__(self, root, gbase, ngs, off, fshape, dtype, P=128, p0=0):
        self.root, self.gbase, self.ngs = root, gbase, ngs
        self.off, self.fshape, self.dtype, self.P, self.p0 = off, tuple(fshape), dtype, P, p0
        es = _ES[dtype]
        self.es = es
        n = int(np.prod(fshape))
        self.nbytes = n * es
        assert off % 4 == 0, (off, self.nbytes)
        ap = root[p0:p0 + P, off // 4: (off + self.nbytes + 3) // 4]
        if dtype != F32:
            ap = ap.bitcast(dtype)[:, 0:n]
        if len(fshape) > 1:
            names = ' '.join('d%d' % i for i in range(len(fshape)))
            kw = {'d%d' % i: int(s) for i, s in enumerate(fshape)}
            ap = ap.rearrange('p (%s) -> p %s' % (names, names), **kw)
        self.ap = ap
        st = [1] * len(fshape)
        for i in range(len(fshape) - 2, -1, -1):
            st[i] = st[i + 1] * fshape[i + 1]
        self.st = st
        self._cache = {}

    def view(self, fshape, dtype, boff=0, P=None, p0=None):
        return Buf(self.root, self.gbase, self.ngs, self.off + boff, fshape, dtype,
                   self.P if P is None else P, self.p0 if p0 is None else p0)

    def __getitem__(self, key):
        if not isinstance(key, tuple):
            key = (key,)
        key = key + (slice(None),) * (1 + len(self.fshape) - len(key))
        ck = tuple((k.start, k.stop) if isinstance(k, slice) else k for k in key)
        r = self._cache.get(ck)
        if r is not None:
            return r
        ps = key[0]
        if isinstance(ps, int):
            pa, pb = ps, ps + 1
            key = (slice(pa, pb),) + key[1:]
        else:
            pa, pb, _ = ps.indices(self.P)
        ap = self.ap[key]
        offs = np.zeros(1, dtype=np.int64)
        fk = key[1:]
        nd = len(fk)
        for d in range(nd - 1):
            k = fk[d]
            if isinstance(k, int):
                ix = np.array([k])
            else:
                a, b, _ = k.indices(self.fshape[d])
                ix = np.arange(a, b)
            offs = (offs[:, None] + ix[None, :] * self.st[d]).ravel()
        k = fk[-1]
        if isinstance(k, int):
            a, b = k, k + 1
        else:
            a, b, _ = k.indices(self.fshape[-1])
        s = self.off + (offs + a) * self.es
        e = self.off + (offs + b) * self.es - 1
        g0 = s // G
        g1 = e // G
        span = int((g1 - g0).max()) + 1
        gg = g0[:, None] + np.arange(span)[None, :]
        gg = np.unique(gg[gg <= g1[:, None]])
        if self.gbase > 0:
            gg = np.unique(gg * G // 2048)
        q0, q1 = (self.p0 + pa) // 32, (self.p0 + pb - 1) // 32
        if self.gbase > 0:
            q0, q1 = 0, 3
        gr = np.concatenate([self.gbase + q * self.ngs + gg for q in range(q0, q1 + 1)])
        r = Reg(ap, gr)
        self._cache[ck] = r
        return r


class Op:
    __slots__ = ('eng', 'fn', 'dma', 'deps', 'signal', 'val', 'idx')


class Prog:
    def __init__(self, nc, stack):
        self.nc = nc
        self.ops = []
        self.lw = np.full(NGT, -1, dtype=np.int64)
        self.lr = np.full((len(ENG) + 1, NGT), -1, dtype=np.int64)
        self.last_dma = {}
        self.arena_t = stack.enter_context(nc.sbuf_tensor("arena", [128, SB_BYTES // 4], F32))
        self.psum_t = stack.enter_context(nc.psum_tensor("psum", [128, PS_BYTES // 4], F32))
        self.aroot = self.arena_t[:, :]
        self.proot = self.psum_t[:, :]
        self.top = 0
        self.stack = stack

    def alloc(self, fshape, dtype, P=128, p0=0):
        n = int(np.prod(fshape)) * _ES[dtype]
        n = (n + 63) // 64 * 64
        off = self.top
        self.top += n
        assert self.top <= SB_BYTES, "SBUF arena overflow %d" % self.top
        return Buf(self.aroot, 0, NG_SB, off, fshape, dtype, P, p0)

    def psum(self, bank, fshape=(512,), dtype=F32, boff=0, P=128, p0=0):
        return Buf(self.proot, 4 * NG_SB, NG_PS, bank * 2048 + boff, fshape, dtype, P, p0)

    def add(self, eng, fn, reads=(), writes=(), dma=None):
        op = Op()
        op.eng, op.fn, op.dma = eng, fn, dma
        op.signal, op.val = False, None
        i = len(self.ops)
        op.idx = i
        ei = ENG.index(eng)
        deps = set()
        rg = [r.gr for r in reads if r is not None and len(r.gr)]
        wg = [w.gr for w in writes if w is not None and len(w.gr)]
        rg = np.concatenate(rg) if rg else np.zeros(0, dtype=np.int64)
        wg = np.concatenate(wg) if wg else np.zeros(0, dtype=np.int64)
        same_ok = dma is None
        if len(rg):
            for j in np.unique(self.lw[rg]):
                if j < 0:
                    continue
                pj = self.ops[j]
                if same_ok and pj.dma is None and pj.eng == eng and eng == 'pe':
                    continue
                deps.add(int(j))
            prg = rg[rg >= 4 * NG_SB]
            if len(prg):
                for e2 in range(len(ENG)):
                    if e2 != ei:
                        j = int(self.lr[e2][prg].max())
                        if j >= 0:
                            deps.add(j)
        if len(wg):
            for j in np.unique(self.lw[wg]):
                if j < 0:
                    continue
                pj = self.ops[j]
                if same_ok and pj.dma is None and pj.eng == eng and (eng == 'pe' or not STRICT_SAME):
                    continue
                deps.add(int(j))
            for e2 in range(len(ENG)):
                j = int(self.lr[e2][wg].max())
                if j < 0:
                    continue
                if same_ok and e2 == ei and (eng == 'pe' or not STRICT_SAME):
                    continue
                deps.add(j)
            for j in np.unique(self.lr[len(ENG)][wg]):
                if j >= 0:
                    deps.add(int(j))
        if dma is not None:
            pj = self.last_dma.get(dma)
            if pj is not None:
                deps.add(pj)
            self.last_dma[dma] = i
            if len(rg):
                for j in np.unique(self.lr[len(ENG)][rg]):
                    if j >= 0:
                        deps.add(int(j))
        deps.discard(i)
        op.deps = sorted(deps)
        for j in op.deps:
            self.ops[j].signal = True
        if len(rg):
            if dma is None:
                self.lr[ei][rg] = i
            else:
                self.lr[len(ENG)][rg] = i
        if len(wg):
            self.lw[wg] = i
            self.lr[:, wg] = -1
        self.ops.append(op)
        return op

    def emit(self):
        nc = self.nc
        stack = self.stack
        cnt = {e: 0 for e in ENG}
        dcnt = {}
        for op in self.ops:
            if op.dma is not None:
                dcnt[op.dma] = dcnt.get(op.dma, 0) + 16
                op.val = dcnt[op.dma]
            elif op.signal:
                cnt[op.eng] += 1
                op.val = cnt[op.eng]
        sems = {e: stack.enter_context(nc.semaphore("s_" + e)) for e in ENG}
        dsem = {k: stack.enter_context(nc.semaphore("d_" + k)) for k in dcnt}
        self.nsem = len(sems) + len(dsem)
        per = {e: [op for op in self.ops if op.eng == e] for e in ENG}
        ops = self.ops
        final = [(dsem[k], v) for k, v in dcnt.items()]

        def run(e, h):
            waited = {}
            for op in per[e]:
                need = {}
                for j in op.deps:
                    pj = ops[j]
                    if pj.dma is not None:
                        s = dsem[pj.dma]
                    else:
                        s = sems[pj.eng]
                    k = id(s)
                    if pj.val > need.get(k, (None, 0))[1]:
                        need[k] = (s, pj.val)
                for k, (s, v) in need.items():
                    if waited.get(k, 0) < v:
                        h.wait_ge(s, v)
                        waited[k] = v
                ins = op.fn(h)
                if op.dma is not None:
                    ins.then_inc(dsem[op.dma], 16)
                elif op.signal:
                    ins.then_inc(sems[e], 1)
            if e == 'sp':
                for s, v in final:
                    h.wait_ge(s, v)

        with nc.Block() as block:
            @block.tensor
            def _(h):
                run('pe', h)

            @block.scalar
            def _(h):
                run('act', h)

            @block.vector
            def _(h):
                run('dve', h)

            @block.gpsimd
            def _(h):
                run('pool', h)

            @block.sync
            def _(h):
                run('sp', h)

    def mm(self, out, lhsT, rhs, start=True, stop=True):
        return self.add('pe', lambda h: h.matmul(out.ap, lhsT.ap, rhs.ap, start=start, stop=stop, skip_group_check=True),
                        [lhsT, rhs], [out])

    def tr(self, out, in_, ident):
        return self.add('pe', lambda h: h.transpose(out.ap, in_.ap, ident.ap), [in_, ident], [out])

    def act(self, out, in_, func, bias=0.0, scale=1.0, eng='act'):
        rd = [in_]
        b = bias
        s = scale
        if isinstance(bias, Reg):
            rd.append(bias)
            b = bias.ap
        if isinstance(scale, Reg):
            rd.append(scale)
            s = scale.ap
        return self.add('act', lambda h: h.activation(out=out.ap, in_=in_.ap, func=func, bias=b, scale=s), rd, [out])

    def tt(self, out, in0, in1, op, eng='dve'):
        return self.add(eng, lambda h: h.tensor_tensor(out=out.ap, in0=in0.ap, in1=in1.ap, op=op), [in0, in1], [out])

    def ts(self, out, in0, s1, s2=None, op0=ALU.mult, op1=None, eng='dve'):
        rd = [in0]
        a1, a2 = s1, s2
        if isinstance(s1, Reg):
            rd.append(s1)
            a1 = s1.ap
        if isinstance(s2, Reg):
            rd.append(s2)
            a2 = s2.ap
        if op1 is None:
            return self.add(eng, lambda h: h.tensor_scalar(out=out.ap, in0=in0.ap, scalar1=a1, scalar2=None, op0=op0), rd, [out])
        return self.add(eng, lambda h: h.tensor_scalar(out=out.ap, in0=in0.ap, scalar1=a1, scalar2=a2, op0=op0, op1=op1), rd, [out])

    def stt(self, out, in0, scalar, in1, op0, op1):
        rd = [in0, in1]
        a = scalar
        if isinstance(scalar, Reg):
            rd.append(scalar)
            a = scalar.ap
        return self.add('dve', lambda h: h.scalar_tensor_tensor(out=out.ap, in0=in0.ap, scalar=a, in1=in1.ap, op0=op0, op1=op1), rd, [out])

    def copy(self, out, in_, eng='dve'):
        if eng == 'act':
            return self.add('act', lambda h: h.activation(out=out.ap, in_=in_.ap, func=AF.Copy), [in_], [out])
        return self.add(eng, lambda h: h.tensor_copy(out=out.ap, in_=in_.ap), [in_], [out])

    def memset(self, out, v, eng='pool'):
        return self.add(eng, lambda h: h.memset(out.ap, v), [], [out])

    def dma_in(self, q, out, src_ap, key):
        return self.add(q, lambda h: h.dma_start(out=out.ap, in_=src_ap), [], [out], dma=key)

    def dma_out(self, q, dst_ap, in_, key):
        return self.add(q, lambda h: h.dma_start(out=dst_ap, in_=in_.ap), [in_], [], dma=key)


from contextlib import ExitStack
from concourse.bass_utils import run_bass_kernel_spmd

D = 1024; KD = 8; TP = 2048; TS = 32; TT = 2080; PAST = 1024; DFF = 2816; NL = 2
TILES = [(0, 512), (512, 512), (1024, 512), (1536, 512), (2048, 32)]
STS = [[0, 1], [2, 3, 4]]
EPS = 1e-6
SB_SCALE = 0.125; FOX_SCALE = 0.125; MLA_SCALE = 96.0 ** -0.5
GN = ['g_ff1_pre', 'g_ff1_post', 'g_mix_pre', 'g_mix_post', 'g_ff2_pre', 'g_ff2_post', 'g_ple_pre', 'g_ple_post']
GC_BQ = 2 * 8 * 8
GC_BKV = GC_BQ + 4
GC_BF = GC_BKV + 2
NGC = GC_BF + 2
C_ONES, C_NTRI, C_NONES, C_ID, C_MS, C_MI, C_MC = [i * 128 for i in range(7)]
C_SELK = 7 * 128
C_SELM = C_SELK + 512
NCC = C_SELM + 4


def gcol(l, name, k):
    return (l * 8 + GN.index(name)) * 8 + k


def build_program(stage=99):
    nc = bass.Bass("TRN2", target_bir_lowering=False)
    dt_in = lambda n, s: nc.dram_tensor(n, list(s), F32, kind="ExternalInput").ap()
    dt_out = lambda n, s: nc.dram_tensor(n, list(s), F32, kind="ExternalOutput").ap()
    xT_d = dt_in("xT", (128, 8, TT)); pT_d = dt_in("pT", (NL, 128, 2, TT))
    gp_d = dt_in("gp", (128, NGC)); ggrp_d = dt_in("ggrp", (NL, 128, 1024)); cst_d = dt_in("cst", (128, NCC))
    ropeK_d = dt_in("ropeK", (32, 2, TT)); ropeQ_d = dt_in("ropeQ", (32, 2, TT))
    wgu_d = [dt_in("wgu%d" % i, (NL, 22, 128, 8, 256)) for i in (1, 2)]
    wd_d = [dt_in("wd%d" % i, (NL, 2, 8, 128, 11, 128)) for i in (1, 2)]
    winA_d = dt_in("winA", (NL, 128, 8, 512)); winVa_d = dt_in("winVa", (NL, 128, 8, 256))
    winC_d = dt_in("winC", (NL, 128, 8, 512)); winVc_d = dt_in("winVc", (NL, 128, 8, 256))
    winFl_d = dt_in("winFl", (NL, 128, 8, 36)); winB_d = dt_in("winB", (NL, 128, 8, 384))
    winKr_d = dt_in("winKr", (NL, 128, 8, 2, 96)); wuq_d = dt_in("wuq", (NL, 128, 2, 8, 2, 96))
    wukvk_d = dt_in("wukvk", (NL, 128, 8, 64)); wukvv_d = dt_in("wukvv", (NL, 128, 512))
    wout_d = dt_in("wout", (NL, 128, 8, 1024)); wgate_d = dt_in("wgate", (NL, 128, 8, 1024)); wproj_d = dt_in("wproj", (NL, 128, 2, 1024))
    akc_d = dt_in("akcT", (NL, 128, 2, PAST)); avc_d = dt_in("avc", (NL, 128, 8, 256))
    ckvc_d = dt_in("ckvcT", (NL, 128, PAST)); krc_d = dt_in("krcT", (NL, 32, PAST))
    ckc_d = dt_in("ckcT", (NL, 128, 2, PAST)); cvc_d = dt_in("cvc", (NL, 128, 8, 256)); clfc_d = dt_in("clfcT", (NL, 36, PAST))
    yT_d = dt_out("yT", (128, 8, TT))
    akT_o = dt_out("akT_o", (NL, 128, 2, TT)); av_o = dt_out("av_o", (NL, TT, 256))
    ckvT_o = dt_out("ckvT_o", (NL, 128, TT)); krT_o = dt_out("krT_o", (NL, 32, TT))
    ckT_o = dt_out("ckT_o", (NL, 128, 2, TT)); cv_o = dt_out("cv_o", (NL, TT, 256)); clfT_o = dt_out("clfT_o", (NL, 4, TT))

    st = ExitStack()
    with st:
        P = Prog(nc, st)
        hT = P.alloc((8, TT), F32)
        gp = P.alloc((NGC,), F32)
        cst = P.alloc((NCC,), BF16)
        P.dma_in('sp', gp[:], gp_d, 'gp')
        gph = P.alloc((NGC,), F32)
        P.ts(gph[:], gp[:], 0.5, op0=ALU.mult)
        P.dma_in('pool', cst[:], cst_d, 'cst')
        for k in range(8):
            P.dma_in('sp', hT[:, k, :], xT_d[:, k, :], 'x%d' % k)
        ones = cst[:, C_ONES:C_ONES + 128]
        ident = lambda n: cst[0:n, C_ID:C_ID + n]
        PBASE = P.top
        ring = [0]
        ringn = [4]

        def bank():
            b = ring[0]
            ring[0] = (b + 1) % ringn[0]
            return b

        sqn = [0]

        def rstd_of(srcs, w, width, out, sq, post=1.0):
            b = P.psum(bank())
            n = len(srcs)
            for i, s in enumerate(srcs):
                t = sq[:, sqn[0] % 2, 0:w]
                sqn[0] += 1
                P.act(t, s, AF.Square)
                P.mm(b[:, 0:w], ones, t, start=(i == 0), stop=(i == n - 1))
            P.act(out, b[:, 0:w], AF.Ln, bias=EPS, scale=1.0 / width)
            P.act(out, out, AF.Exp, scale=-0.5)

        def ffn(l, which):
            P.top = PBASE
            ringn[0] = 8
            gpre, gpost = ('g_ff1_pre', 'g_ff1_post') if which == 0 else ('g_ff2_pre', 'g_ff2_post')
            xn2 = [P.alloc((8, 1056), BF16) for _ in range(2)]; actb = P.alloc((11, 1056), BF16); fo = P.alloc((8, 1056), F32)
            rstd = P.alloc((1056,), F32); rpre = [P.alloc((1056,), F32) for _ in range(2)]; sq = P.alloc((2, 512), BF16); sg = P.alloc((3, 512), BF16)
            wg = [P.alloc((8, 256), BF16) for _ in range(3)]
            wdb = [P.alloc((11, 128), BF16) for _ in range(2)]
            tmp = P.alloc((512,), F32)
            nw = [0, 0]
            pending = [iter(())]
            for sti, stl in enumerate(STS):
                tl = [TILES[i] for i in stl]
                base = tl[0][0]
                for (t0, w) in tl:
                    c0 = t0 - base
                    rstd_of([hT[:, k, t0:t0 + w] for k in range(8)], w, D, rpre[sti][:, c0:c0 + w], sq)
                    for k in range(8):
                        P.stt(xn2[sti][:, k, c0:c0 + w], hT[:, k, t0:t0 + w], gp[:, gcol(l, gpre, k):gcol(l, gpre, k) + 1],
                              rpre[sti][:, c0:c0 + w], ALU.mult, ALU.mult)
            for sti, stl in enumerate(STS):
                tl = [TILES[i] for i in stl]
                base = tl[0][0]
                loc = [(t0 - base, w) for (t0, w) in tl]
                xn = xn2[sti]
                for hh in range(2):
                    for jj in range(11):
                        j = hh * 11 + jj
                        wt = wg[nw[0] % 3]; nw[0] += 1
                        P.dma_in('pool', wt[:], wgu_d[which][l, j], 'wg%d' % (nw[0] % 3))
                        gb = []
                        for (c0, w) in loc:
                            b = P.psum(bank())
                            for k in range(8):
                                P.mm(b[:, 0:w], wt[:, k, 0:128], xn[:, k, c0:c0 + w], start=(k == 0), stop=(k == 7))
                            gb.append(b)
                        sgt = []
                        for ii, ((c0, w), b) in enumerate(zip(loc, gb)):
                            s_ = sg[:, ii, 0:w]
                            P.act(s_, b[:, 0:w], AF.Silu)
                            sgt.append(s_)
                        for (c0, w), s_ in zip(loc, sgt):
                            b = P.psum(bank())
                            for k in range(8):
                                P.mm(b[:, 0:w], wt[:, k, 128:256], xn[:, k, c0:c0 + w], start=(k == 0), stop=(k == 7))
                            P.tt(actb[:, jj, c0:c0 + w], b[:, 0:w], s_, ALU.mult)
                        if hh == 0:
                            next(pending[0], None)
                    if hh == 0:
                        for _ in pending[0]:
                            pass
                    for oc in range(8):
                        wt = wdb[nw[1] % 2]; nw[1] += 1
                        P.dma_in('pool', wt[:], wd_d[which][l, hh, oc], 'wd%d' % (nw[1] % 2))
                        for (c0, w) in loc:
                            b = P.psum(bank())
                            for k in range(11):
                                P.mm(b[:, 0:w], wt[:, k, :], actb[:, k, c0:c0 + w], start=(k == 0), stop=(k == 10))
                            if hh == 0:
                                P.copy(fo[:, oc, c0:c0 + w], b[:, 0:w], eng='act')
                            else:
                                P.tt(fo[:, oc, c0:c0 + w], b[:, 0:w], fo[:, oc, c0:c0 + w], ALU.add)
                def post_gen(tl=tl, loc=loc):
                    for (t0, w), (c0, _) in zip(tl, loc):
                        rstd_of([fo[:, k, c0:c0 + w] for k in range(8)], w, D, rstd[:, c0:c0 + w], sq, post=0.5)
                        yield
                    for k in range(8):
                        for (t0, w), (c0, _) in zip(tl, loc):
                            P.tt(fo[:, k, c0:c0 + w], fo[:, k, c0:c0 + w], rstd[:, c0:c0 + w], ALU.mult, eng=('pool' if POOL_RES else 'dve'))
                            P.stt(hT[:, k, t0:t0 + w], fo[:, k, c0:c0 + w], gph[:, gcol(l, gpost, k):gcol(l, gpost, k) + 1],
                                  hT[:, k, t0:t0 + w], ALU.mult, ALU.add)
                        yield
                pending[0] = post_gen()
            for _ in pending[0]:
                pass

        def attn(kind, h, qf, w, nsub, qn, kbl, ops_bank, tmpb, extra):
            dvx = 64 if kind == 'sb' else 65
            o_ps = P.psum(ops_bank, (nsub, dvx), F32)
            first = [True]
            mask = {'sb': C_MS, 'fox': C_MI, 'mla': C_MC}[kind]
            if kind == 'sb':
                racc = tmpb['racc']
                P.memset(racc[:, 0:w], 0.0, eng='pool')
            nb = len(kbl)
            for bi, (kT, v, nk, diag, c0, negF) in enumerate(kbl):
                dw = min(128, w - c0)
                q = qf(c0, w)
                sid = tmpb['sid']
                if kind == 'sb':
                    zb = P.psum(bank())
                    P.mm(zb[0:nk, c0:w], kT, q)
                    et = tmpb['e'][sid % 2]; spt = tmpb['sp'][sid]
                    P.act(et[0:nk, c0:w], zb[0:nk, c0:w], AF.Exp)
                    P.act(spt[0:nk, c0:w], et[0:nk, c0:w], AF.Ln, bias=1.0)
                    if diag:
                        P.tt(spt[0:nk, c0:c0 + dw], spt[0:nk, c0:c0 + dw], cst[0:nk, mask:mask + dw], ALU.mult, eng='dve')
                    yield
                    lb = P.psum(bank())
                    P.mm(lb[0:nk, c0:w], kT, q, start=True, stop=False)
                    if nk < 128:
                        P.memset(spt[32:64, c0:w], 0.0); P.memset(spt[64:128, c0:w], 0.0)
                        P.mm(lb[0:nk, c0:w], cst[:, C_NTRI:C_NTRI + nk], spt[:, c0:w], start=False, stop=(bi == 0))
                    else:
                        P.mm(lb[0:nk, c0:w], cst[0:nk, C_NTRI:C_NTRI + nk], spt[0:nk, c0:w], start=False, stop=(bi == 0))
                    if bi > 0:
                        P.mm(lb[0:nk, c0:w], cst[:, C_NONES:C_NONES + nk], racc[:, c0:w], start=False, stop=True)
                    src = lb
                else:
                    sb_ = P.psum(bank())
                    if kind == 'fox':
                        P.mm(sb_[0:nk, c0:w], kT, q, start=True, stop=False)
                        P.mm(sb_[0:nk, c0:w], cst[:, C_SELK + h * 128:C_SELK + h * 128 + nk], extra['Fq'](c0, w), start=False, stop=True)
                    else:
                        P.mm(sb_[0:nk, c0:w], kT, q)
                    src = sb_
                pt = tmpb['p'][sid]
                P.act(pt[0:nk, c0:w], src[0:nk, c0:w], AF.Exp, bias=(negF if negF is not None else 0.0))
                if diag:
                    P.tt(pt[0:nk, c0:c0 + dw], pt[0:nk, c0:c0 + dw], cst[0:nk, mask:mask + dw], ALU.mult, eng='dve')
                if kind == 'sb' and bi < nb - 1:
                    P.tt(racc[0:nk, c0:w], racc[0:nk, c0:w], spt[0:nk, c0:w], ALU.add, eng='dve')
                yield
                for sbi in range(c0 // 128, nsub):
                    a = sbi * 128
                    bq = min(a + 128, w)
                    P.mm(o_ps[0:bq - a, sbi, :], pt[0:nk, a:bq], v, start=first[0], stop=(bi == nb - 1 and sbi == nsub - 1))
                    first[0] = False
                yield
            dst = extra['dst']
            if kind == 'sb':
                P.copy(dst, o_ps[0:qn, :, 0:64], eng='dve')
            else:
                rd = tmpb['rd']
                P.copy(rd[0:qn, 0:nsub, 0], o_ps[0:qn, :, 64], eng='dve')
                P.add('dve', lambda hh, a=rd[0:qn, 0:nsub, :]: hh.reciprocal(out=a.ap, in_=a.ap), [rd[0:qn, 0:nsub, :]], [rd[0:qn, 0:nsub, :]])
                a_ = rd[0:qn, 0:nsub, :]
                b_ = o_ps[0:qn, :, 0:64]
                P.add('dve', lambda hh, a_=a_, b_=b_, dst=dst: hh.tensor_tensor(out=dst.ap, in0=b_.ap, in1=a_.ap.to_broadcast([qn, nsub, 64]), op=ALU.mult),
                      [a_, b_], [dst])
            yield

        def interleave(gens):
            gens = list(gens)
            while gens:
                for g in list(gens):
                    try:
                        next(g)
                    except StopIteration:
                        gens.remove(g)

        def mixer(l):
            P.top = PBASE
            ringn[0] = 4; ring[0] = ring[0] % 4
            sq = P.alloc((2, 512), BF16)
            oTok = P.alloc((17, 1024), BF16)
            ggrp = P.alloc((1024,), F32)
            ATT = P.top
            rstd = P.alloc((TT,), F32)
            xnt = [P.alloc((8, 512), BF16) for _ in range(1)]
            stg = [P.alloc((512,), F32) for _ in range(2)]
            tmpb = {'e': [P.alloc((512,), F32) for _ in range(2)], 'sp': [P.alloc((512,), BF16) for _ in range(4)], 'p': [P.alloc((512,), BF16) for _ in range(4)],
                    'racc': None, 'rd': None, 'n': [0]}
            raccs = [P.alloc((512,), BF16) for _ in range(4)]
            rds = [P.alloc((4, 1), F32) for _ in range(4)]
            P.dma_in('sp', ggrp[:], ggrp_d[l], 'ggrp')
            MB = P.top
            nst = [0]; nx = [0]
            xnbufs = [xnt[0]]

            def stage_out(dst_ap, src, w, p0=0, p1=128, eng='dve'):
                s_ = stg[nst[0] % 2]; nst[0] += 1
                P.copy(s_[p0:p1, 0:w], src, eng=eng)
                P.dma_out('sp', dst_ap, s_[p0:p1, 0:w], 'so%d' % (nst[0] % 2))

            for (t0, w) in TILES:
                rstd_of([hT[:, k, t0:t0 + w] for k in range(8)], w, D, rstd[:, t0:t0 + w], sq)

            def xn_tile(ti):
                t0, w = TILES[ti]
                x = xnbufs[nx[0] % len(xnbufs)]; nx[0] += 1
                for k in range(8):
                    P.stt(x[:, k, 0:w], hT[:, k, t0:t0 + w], gp[:, gcol(l, 'g_mix_pre', k):gcol(l, 'g_mix_pre', k) + 1],
                          rstd[:, t0:t0 + w], ALU.mult, ALU.mult)
                return x

            def proj(x, w, wt, c0, m, out_rows=None):
                b = P.psum(bank())
                for k in range(8):
                    P.mm(b[0:m, 0:w], wt[:, k, c0:c0 + m], x[:, k, 0:w], start=(k == 0), stop=(k == 7))
                return b

            def vproj(x, w, wt, ncols, dstf, out_d, tb0):
                for s0 in range(0, w, 128):
                    n = min(128, w - s0)
                    b = P.psum(bank())
                    for k in range(8):
                        P.mm(b[0:n, 0:ncols], x[:, k, s0:s0 + n], wt[:, k, :], start=(k == 0), stop=(k == 7))
                    tb = tb0 + s0 // 128
                    dstf(tb, n, b)
                    s_ = stg[nst[0] % 2]; nst[0] += 1
                    P.copy(s_[0:n, 0:ncols], b[0:n, 0:ncols], eng='act')
                    P.dma_out('sp', out_d[tb * 128:tb * 128 + n, :], s_[0:n, 0:ncols], 'so%d' % (nst[0] % 2))

            def chain(*gs):
                for g_ in gs:
                    yield from g_

            def qk_attention(kind, wt, Kt, V, Kc, Vc, vstride, extra_fn):
                qb = [[P.alloc((512,), BF16) for _ in range(4)] for _ in range(2)]
                for par in range(2):
                    for h in range(4):
                        dz = (1 - h % 2) * 64
                        P.memset(qb[par][h][dz:dz + 64, :], 0.0)
                def prep(ti):
                    t0, w = TILES[ti]
                    x = xn_tile(ti)
                    for c in range(2):
                        b = proj(x, w, wt, c * 128, 128)
                        for hp in range(2):
                            P.act(qb[ti % 2][2 * c + hp][hp * 64:hp * 64 + 64, 0:w], b[hp * 64:hp * 64 + 64, 0:w], AF.Copy, scale=SB_SCALE)

                prep(0)
                for ti, (t0, w) in enumerate(TILES):
                    if ti + 1 < 5:
                        prep(ti + 1)
                    gens = []
                    for h in range(4):
                        c, hp = h // 2, h % 2
                        pb = hp * 64
                        q = qb[ti % 2][h]
                        if ti < 4:
                            kbl = []
                            for kb in range(4 * ti + 3, -1, -1):
                                diag = kb >= 4 * ti
                                c0 = 128 * (kb - 4 * ti) if diag else 0
                                kbl.append((Kt[:, c, kb * 128:(kb + 1) * 128], V(kb, h, 128), 128, diag, c0,
                                            extra_fn('negF', kb, h, 128)))
                            nsub, qn, tb0 = 4, 128, 4 * ti
                        else:
                            kbl = [(Kt[:, c, TP:TP + 32], V(16, h, 32), 32, True, 0, extra_fn('negF', 16, h, 32))]
                            for kb in range(7, -1, -1):
                                kbl.append((Kc[:, c, kb * 128:(kb + 1) * 128], Vc(kb, h), 128, False, 0,
                                            extra_fn('negFc', kb, h, 128)))
                            nsub, qn, tb0 = 1, 32, 16
                        col = (0 if kind == 'sb' else 768) + h * 64
                        tb = dict(tmpb); tb['racc'] = raccs[h]; tb['rd'] = rds[h]; tb['sid'] = h
                        ex = {'dst': oTok[0:qn, tb0:tb0 + nsub, col:col + 64],
                              'Fq': (lambda a, b_, t0=t0: extra_fn('Fq', t0 + a, t0 + b_, 0))}
                        gens.append(attn(kind, h, (lambda a, b_, q=q: q[:, a:b_]), w, nsub, qn, kbl, 4 + h, tb, ex))
                    if NS_QK == 4:
                        interleave(gens)
                    else:
                        interleave(gens[0:2]); interleave(gens[2:4])

            if stage >= 2 and 'A' in MIXSEL:
                P.top = MB
                winA = P.alloc((8, 512), BF16); winVa = P.alloc((8, 256), BF16)
                kaT = P.alloc((2, TT), BF16); va = P.alloc((17, 256), BF16)
                kaTc = P.alloc((2, PAST), BF16); vac = P.alloc((8, 256), BF16)
                xnbufs[:] = [xnt[0], P.alloc((8, 512), BF16)]
                P.dma_in('pool', winA[:], winA_d[l], 'wA'); P.dma_in('pool', winVa[:], winVa_d[l], 'wVa')
                P.dma_in('pool', kaTc[:], akc_d[l], 'kc'); P.dma_in('pool', vac[:], avc_d[l], 'vc')
                for ti, (t0, w) in enumerate(TILES):
                    x = xn_tile(ti)
                    for c in range(2):
                        b = proj(x, w, winA, 256 + c * 128, 128)
                        P.copy(kaT[:, c, t0:t0 + w], b[:, 0:w], eng='act')
                        stage_out(akT_o[l, :, c, t0:t0 + w], b[:, 0:w], w)
                    vproj(x, w, winVa, 256, lambda tb, n, b: P.copy(va[0:n, tb, :], b[0:n, 0:256], eng='dve'), av_o[l], t0 // 128)
                if stage >= 3:
                    qk_attention('sb', winA, kaT, lambda kb, h, n: va[0:n, kb, h * 64:(h + 1) * 64], kaTc,
                                 lambda kb, h: vac[:, kb, h * 64:(h + 1) * 64], 64, lambda *a: None)

            xnbufs[:] = [xnt[0]]
            if stage >= 2 and 'C' in MIXSEL:
                P.top = MB
                winC = P.alloc((8, 512), BF16); winVc = P.alloc((8, 256), BF16); winFl = P.alloc((8, 36), BF16)
                kcT = P.alloc((2, TT), BF16); vc = P.alloc((17, 4, 65), BF16)
                kcTc = P.alloc((2, PAST), BF16); vcc = P.alloc((8, 4, 65), BF16)
                Frows = P.alloc((TT,), BF16); Frc = P.alloc((PAST,), BF16)
                negF = P.alloc((17, 4), F32); negFc = P.alloc((8, 4), F32)
                CMARK = P.top
                lf = P.alloc((512,), F32); Ft = P.alloc((512,), F32); lo = P.alloc((512,), F32); one1 = P.alloc((1,), F32); carry = P.alloc((1,), F32)
                clf = P.alloc((512,), F32)
                P.dma_in('pool', winC[:], winC_d[l], 'wA'); P.dma_in('pool', winVc[:], winVc_d[l], 'wVa'); P.dma_in('pool', winFl[:], winFl_d[l], 'wFl')
                P.dma_in('pool', kcTc[:], ckc_d[l], 'kc'); P.dma_in('pool', vcc[:, :, :, 0:64], cvc_d[l].rearrange('p b (h d) -> p b h d', h=4), 'vc')
                P.memset(vc[:, :, :, 64], 1.0); P.memset(vcc[:, :, :, 64], 1.0); P.memset(one1[:], 1.0)
                P.memset(Frows[:], 0.0); P.memset(Frc[:], 0.0)
                nbf = gp[0:36, GC_BF + l:GC_BF + l + 1]

                prev = None
                for ti, (t0, w) in enumerate(TILES):
                    x = xn_tile(ti)
                    for c in range(2):
                        b = proj(x, w, winC, 256 + c * 128, 128)
                        P.copy(kcT[:, c, t0:t0 + w], b[:, 0:w], eng='act')
                        stage_out(ckT_o[l, :, c, t0:t0 + w], b[:, 0:w], w)
                    vproj(x, w, winVc, 256, lambda tb, n, b: P.copy(vc[0:n, tb, :, 0:64], b.view((4, 64), F32)[0:n, :, :], eng='dve'), cv_o[l], t0 // 128)
                    b = proj(x, w, winFl, 0, 36)
                    P.act(lo[0:36, 0:w], b[0:36, 0:w], AF.Exp, bias=nbf, scale=1.0)
                    P.act(lo[0:36, 0:w], lo[0:36, 0:w], AF.Ln, bias=1.0)
                    P.ts(lf[0:36, 0:w], b[0:36, 0:w], nbf, op0=ALU.add)
                    P.tt(lf[0:36, 0:w], lf[0:36, 0:w], lo[0:36, 0:w], ALU.subtract)
                    P.dma_out('sp', clfT_o[l, :, t0:t0 + w], lf[0:4, 0:w], 'lfo')
                    if ti == 4:
                        for half in range(2):
                            a = half * 512
                            P.dma_in('sp', clf[0:36, :], clfc_d[l, :, a:a + 512], 'clf')
                            P.add('dve', lambda hh, a=a, init=(carry[0:36, :] if half else None): hh.tensor_tensor_scan(
                                out=Ft[0:36, 0:512].ap, data0=one1[0:36, :].ap.to_broadcast([36, 512]), data1=clf[0:36, :].ap,
                                initial=(init.ap if init is not None else 0.0), op0=ALU.mult, op1=ALU.add),
                                [one1[0:36, :], clf[0:36, :]] + ([carry[0:36, :]] if half else []), [Ft[0:36, 0:512]])
                            P.copy(carry[0:36, :], Ft[0:36, 511:512], eng='dve')
                            P.copy(Frc[0:36, a:a + 512], Ft[0:36, 0:512], eng='dve')
                            P.copy(lo[32:36, 0:512], Frc[32:36, a:a + 512], eng='dve')
                            P.tt(lo[32:36, 0:512], Ft[32:36, 0:512], lo[32:36, 0:512], ALU.subtract)
                            P.copy(Frc[32:36, a:a + 512], lo[32:36, 0:512], eng='dve')
                        init = carry[0:36, :]
                    else:
                        init = carry[0:36, :] if ti > 0 else None
                    P.add('dve', lambda hh, w=w, init=init: hh.tensor_tensor_scan(
                        out=Ft[0:36, 0:w].ap, data0=one1[0:36, :].ap.to_broadcast([36, w]), data1=lf[0:36, 0:w].ap,
                        initial=(init.ap if init is not None else 0.0), op0=ALU.mult, op1=ALU.add),
                        [one1[0:36, :], lf[0:36, 0:w]] + ([init] if init is not None else []), [Ft[0:36, 0:w]])
                    P.copy(carry[0:36, :], Ft[0:36, w - 1:w], eng='dve')
                    P.copy(Frows[0:36, t0:t0 + w], Ft[0:36, 0:w], eng='dve')
                    P.copy(lo[32:36, 0:w], Frows[32:36, t0:t0 + w], eng='dve')
                    P.tt(lo[32:36, 0:w], Ft[32:36, 0:w], lo[32:36, 0:w], ALU.subtract)
                    P.copy(Frows[32:36, t0:t0 + w], lo[32:36, 0:w], eng='dve')
                selm = cst[0:36, C_SELM:C_SELM + 4]
                for kb in range(17):
                    n = 128 if kb < 16 else 32
                    b = P.psum(bank())
                    P.mm(b[0:n, 0:4], Frows[0:36, kb * 128:kb * 128 + n], selm)
                    P.ts(negF[0:n, kb, :], b[0:n, 0:4], -1.0, op0=ALU.mult)
                for kb in range(8):
                    b = P.psum(bank())
                    P.mm(b[:, 0:4], Frc[0:36, kb * 128:(kb + 1) * 128], selm)
                    P.ts(negFc[:, kb, :], b[:, 0:4], -1.0, op0=ALU.mult)
                if stage >= 3:
                    P.top = CMARK

                    def exf(what, a, b_, n):
                        if what == 'negF':
                            return negF[0:n, a, b_:b_ + 1]
                        if what == 'negFc':
                            return negFc[0:n, a, b_:b_ + 1]
                        return Frows[:, a:b_]
                    qk_attention('fox', winC, kcT, lambda kb, h, n: vc[0:n, kb, h, :], kcTc, lambda kb, h: vcc[:, kb, h, :], 65, exf)

            if stage >= 2 and 'B' in MIXSEL:
                P.top = MB
                cqn = P.alloc((2, TT), BF16); ckvn = P.alloc((TT,), BF16); krT = P.alloc((TT,), BF16)
                ckvc = P.alloc((PAST,), BF16); krc = P.alloc((PAST,), BF16)
                wuq = P.alloc((2, 8, 2, 96), BF16); wukvk = P.alloc((8, 64), BF16); wukvv = P.alloc((512,), BF16)
                P.dma_in('pool', wuq[:], wuq_d[l], 'wA'); P.dma_in('pool', wukvk[:], wukvk_d[l], 'wVa'); P.dma_in('pool', wukvv[:], wukvv_d[l], 'wFl')
                P.dma_in('pool', ckvc[:], ckvc_d[l], 'kc'); P.dma_in('pool', krc[64:96, :], krc_d[l], 'vc')
                MB2 = P.top
                winB = P.alloc((8, 384), BF16); winKr = P.alloc((8, 2, 96), BF16)
                cq32 = P.alloc((3, 512), F32); rk = P.alloc((2, 512), F32); rt = P.alloc((2, 512), F32)
                P.dma_in('pool', winB[:], winB_d[l], 'wB'); P.dma_in('pool', winKr[:], winKr_d[l], 'wKr')
                for ti, (t0, w) in enumerate(TILES):
                    x = xn_tile(ti)
                    P.dma_in('sp', rk[64:96, :, 0:w], ropeK_d[:, :, t0:t0 + w], 'rk')
                    for c in range(2):
                        b = proj(x, w, winB, c * 128, 128)
                        P.copy(cq32[:, c, 0:w], b[:, 0:w], eng='act')
                    rstd_of([cq32[:, 0, 0:w], cq32[:, 1, 0:w]], w, 256, cq32[:, 2, 0:w], sq)
                    for c in range(2):
                        P.stt(cqn[:, c, t0:t0 + w], cq32[:, c, 0:w], gp[:, GC_BQ + l * 2 + c:GC_BQ + l * 2 + c + 1], cq32[:, 2, 0:w], ALU.mult, ALU.mult)
                    b = proj(x, w, winB, 256, 128)
                    P.copy(cq32[:, 0, 0:w], b[:, 0:w], eng='act')
                    rstd_of([cq32[:, 0, 0:w]], w, 128, cq32[:, 2, 0:w], sq)
                    s_ = stg[nst[0] % 2]; nst[0] += 1
                    P.stt(s_[:, 0:w], cq32[:, 0, 0:w], gp[:, GC_BKV + l:GC_BKV + l + 1], cq32[:, 2, 0:w], ALU.mult, ALU.mult)
                    P.copy(ckvn[:, t0:t0 + w], s_[:, 0:w], eng='act')
                    P.dma_out('sp', ckvT_o[l, :, t0:t0 + w], s_[:, 0:w], 'so%d' % (nst[0] % 2))
                    b1 = P.psum(bank()); b2 = P.psum(bank())
                    for k in range(8):
                        P.mm(b1[0:96, 0:w], winKr[:, k, 0, :], x[:, k, 0:w], start=(k == 0), stop=(k == 7))
                    for k in range(8):
                        P.mm(b2[0:96, 0:w], winKr[:, k, 1, :], x[:, k, 0:w], start=(k == 0), stop=(k == 7))
                    P.tt(rt[64:96, 0, 0:w], b1[64:96, 0:w], rk[64:96, 0, 0:w], ALU.mult)
                    P.tt(rt[64:96, 1, 0:w], b2[64:96, 0:w], rk[64:96, 1, 0:w], ALU.mult)
                    s_ = stg[nst[0] % 2]; nst[0] += 1
                    P.tt(s_[64:96, 0:w], rt[64:96, 0, 0:w], rt[64:96, 1, 0:w], ALU.add)
                    P.copy(krT[64:96, t0:t0 + w], s_[64:96, 0:w], eng='act')
                    P.dma_out('sp', krT_o[l, :, t0:t0 + w], s_[64:96, 0:w], 'so%d' % (nst[0] % 2))
                if stage >= 3:
                    P.top = MB2
                    Kh2 = [P.alloc((TT,), BF16) for _ in range(2)]; Khc2 = [P.alloc((PAST,), BF16) for _ in range(2)]
                    vb2 = [P.alloc((17, 65), BF16) for _ in range(2)]; vbc2 = [P.alloc((8, 65), BF16) for _ in range(2)]
                    qh = [P.alloc((512,), BF16) for _ in range(5)]
                    rq = P.alloc((2, 512), F32)
                    rt2 = tmpb['e']
                    for i_ in range(2):
                        P.memset(vb2[i_][:, :, 64], 1.0); P.memset(vbc2[i_][:, :, 64], 1.0)
                        P.memset(Kh2[i_][64:128, :], 0.0); P.memset(Khc2[i_][64:128, :], 0.0)
                    for i_ in range(5):
                        P.memset(qh[i_][64:128, :], 0.0)
                    qhs = [qh, raccs + [tmpb['sp'][0]]]
                    for i_ in range(5):
                        P.memset(qhs[1][i_][64:128, :], 0.0)

                    rqs = [rq, stg[0].view((2, 512), F32)]

                    def prologue(h):
                        Kh, Khc, vb, vbc = Kh2[h % 2], Khc2[h % 2], vb2[h % 2], vbc2[h % 2]
                        for (t0, w) in TILES:
                            b = P.psum(bank())
                            P.mm(b[0:64, 0:w], wukvk[:, h, :], ckvn[:, t0:t0 + w])
                            P.copy(Kh[0:64, t0:t0 + w], b[0:64, 0:w], eng='act')
                        P.copy(Kh[64:96, :], krT[64:96, :], eng='pool')
                        for a in (0, 512):
                            b = P.psum(bank())
                            P.mm(b[0:64, 0:512], wukvk[:, h, :], ckvc[:, a:a + 512])
                            P.copy(Khc[0:64, a:a + 512], b[0:64, 0:512], eng='act')
                        P.copy(Khc[64:96, :], krc[64:96, :], eng='pool')
                        for g0 in range(0, 17, 4):
                            b = P.psum(bank(), (4, 64), F32)
                            nb_ = min(4, 17 - g0)
                            for i in range(nb_):
                                kb = g0 + i
                                n = 128 if kb < 16 else 32
                                P.mm(b[0:n, i, :], ckvn[:, kb * 128:kb * 128 + n], wukvv[:, h * 64:(h + 1) * 64], start=(i == 0), stop=(i == nb_ - 1))
                            if g0 < 16:
                                P.copy(vb[:, g0:g0 + 4, 0:64], b[:, 0:4, :], eng='dve')
                            else:
                                P.copy(vb[0:32, 16, 0:64], b[0:32, 0, :], eng='dve')
                        for g0 in (0, 4):
                            b = P.psum(bank(), (4, 64), F32)
                            for i in range(4):
                                P.mm(b[:, i, :], ckvc[:, (g0 + i) * 128:(g0 + i + 1) * 128], wukvv[:, h * 64:(h + 1) * 64], start=(i == 0), stop=(i == 3))
                            P.copy(vbc[:, g0:g0 + 4, 0:64], b[:, 0:4, :], eng='dve')
                        gl = []
                        for ti, (t0, w) in enumerate(TILES):
                            rq = rqs[(h * 5 + ti) % 2]
                            P.dma_in('sp', rq[64:96, :, 0:w], ropeQ_d[:, :, t0:t0 + w], 'rq%d' % ((h * 5 + ti) % 2))
                            b1 = P.psum(bank()); b2 = P.psum(bank())
                            for k in range(2):
                                P.mm(b1[0:96, 0:w], wuq[:, k, h, 0, :], cqn[:, k, t0:t0 + w], start=(k == 0), stop=(k == 1))
                            for k in range(2):
                                P.mm(b2[0:96, 0:w], wuq[:, k, h, 1, :], cqn[:, k, t0:t0 + w], start=(k == 0), stop=(k == 1))
                            q = qhs[h % 2][ti]
                            P.act(q[0:64, 0:w], b1[0:64, 0:w], AF.Copy, scale=MLA_SCALE)
                            P.tt(rt2[0][64:96, 0:w], b1[64:96, 0:w], rq[64:96, 0, 0:w], ALU.mult)
                            P.tt(rt2[1][64:96, 0:w], b2[64:96, 0:w], rq[64:96, 1, 0:w], ALU.mult)
                            P.tt(q[64:96, 0:w], rt2[0][64:96, 0:w], rt2[1][64:96, 0:w], ALU.add)
                            if ti < 4:
                                kbl = []
                                for kb in range(4 * ti + 3, -1, -1):
                                    diag = kb >= 4 * ti
                                    c0 = 128 * (kb - 4 * ti) if diag else 0
                                    kbl.append((Kh[:, kb * 128:(kb + 1) * 128], vb[:, kb, :], 128, diag, c0, None))
                                nsub, qn, tb0 = 4, 128, 4 * ti
                            else:
                                kbl = [(Kh[:, TP:TP + 32], vb[0:32, 16, :], 32, False, 0, None)]
                                for kb in range(7, -1, -1):
                                    kbl.append((Khc[:, kb * 128:(kb + 1) * 128], vbc[:, kb, :], 128, False, 0, None))
                                nsub, qn, tb0 = 1, 32, 16
                            sid = {3: 0, 2: 1, 1: 2, 0: 3, 4: 3}[ti]
                            tb = dict(tmpb); tb['rd'] = rds[sid]; tb['sid'] = sid
                            ex = {'dst': oTok[0:qn, tb0:tb0 + nsub, 256 + h * 64:256 + (h + 1) * 64]}
                            gl.append(attn('mla', h, (lambda a, b_, q=q: q[:, a:b_]), w, nsub, qn, kbl, 4 + sid, tb, ex))
                        return gl

                    pend = prologue(0)
                    for h in range(8):
                        nxt = prologue(h + 1) if h < 7 else None
                        gl = pend
                        interleave([gl[3], gl[2], gl[1], chain(gl[0], gl[4])])
                        pend = nxt

            if stage >= 3:
                P.top = ATT
                ssq = P.alloc((17, 3), F32); junk = P.alloc((512,), BF16)
                wout = P.alloc((8, 1024), BF16)
                onT2 = [P.alloc((8, 512), BF16) for _ in range(2)]; mo2 = [P.alloc((8, 512), F32) for _ in range(3)]
                r22 = [P.alloc((512,), F32) for _ in range(3)]
                P.dma_in('pool', wout[:], wout_d[l], 'wout')
                grp = [(0, 256), (256, 512), (768, 256)]
                P.memset(ssq[:], 1.0, eng='dve')
                def gn(ti):
                    t0, w = TILES[ti]
                    b0 = t0 // 128
                    b1 = b0 + (w + 127) // 128
                    for tbk in range(b0, b1):
                        n = 128 if tbk < 16 else 32
                        for gi, (a, wd_) in enumerate(grp):
                            P.add('act', lambda hh, o=junk[0:n, 0:wd_], i=oTok[0:n, tbk, a:a + wd_], ac=ssq[0:n, tbk, gi:gi + 1]:
                                  hh.activation(out=o.ap, in_=i.ap, func=AF.Square, accum_out=ac.ap),
                                  [oTok[0:n, tbk, a:a + wd_]], [junk[0:n, 0:wd_], ssq[0:n, tbk, gi:gi + 1]])
                    for gi, (a, wd_) in enumerate(grp):
                        P.act(ssq[:, b0:b1, gi], ssq[:, b0:b1, gi], AF.Ln, bias=EPS, scale=1.0 / wd_)
                        P.act(ssq[:, b0:b1, gi], ssq[:, b0:b1, gi], AF.Exp, scale=-0.5)
                    for tbk in range(b0, b1):
                        n = 128 if tbk < 16 else 32
                        for gi, (a, wd_) in enumerate(grp):
                            P.stt(oTok[0:n, tbk, a:a + wd_], oTok[0:n, tbk, a:a + wd_], ssq[0:n, tbk, gi:gi + 1], ggrp[0:n, a:a + wd_], ALU.mult, ALU.mult)

                def wo_a(ti):
                    t0, w = TILES[ti]
                    onT, mo = onT2[ti % 2], mo2[ti % 3]
                    for c in range(8):
                        bt = P.psum(4 + c % 4, (1024,), BF16)
                        off = 0
                        for s0 in range(0, w, 128):
                            n = min(128, w - s0)
                            tbk = (t0 + s0) // 128
                            P.tr(bt[:, off + s0:off + s0 + n], oTok[0:n, tbk, c * 128:(c + 1) * 128], ident(n))
                        P.copy(onT[:, c, 0:w], bt[:, off:off + w], eng=('act' if c % 2 else 'dve'))
                    for oc in range(8):
                        b = P.psum(bank())
                        for k in range(8):
                            P.mm(b[:, 0:w], wout[:, k, oc * 128:(oc + 1) * 128], onT[:, k, 0:w], start=(k == 0), stop=(k == 7))
                        P.copy(mo[:, oc, 0:w], b[:, 0:w], eng=('act' if oc % 2 else 'dve'))

                def wo_b1(ti):
                    t0, w = TILES[ti]
                    mo, r2 = mo2[ti % 3], r22[ti % 3]
                    rstd_of([mo[:, k, 0:w] for k in range(8)], w, D, r2[:, 0:w], sq)
                    for k in range(8):
                        P.tt(mo[:, k, 0:w], mo[:, k, 0:w], r2[:, 0:w], ALU.mult, eng=('pool' if POOL_RES else 'dve'))

                def wo_b2(ti):
                    t0, w = TILES[ti]
                    mo = mo2[ti % 3]
                    for k in range(8):
                        P.stt(hT[:, k, t0:t0 + w], mo[:, k, 0:w], gp[:, gcol(l, 'g_mix_post', k):gcol(l, 'g_mix_post', k) + 1],
                              hT[:, k, t0:t0 + w], ALU.mult, ALU.add)

                gn(0); gn(1)
                for ti in range(5):
                    wo_a(ti)
                    if ti + 2 < 5:
                        gn(ti + 2)
                    if ti > 0:
                        wo_b1(ti - 1)
                    if ti > 1:
                        wo_b2(ti - 2)
                wo_b1(4); wo_b2(3); wo_b2(4)

        def ple(l):
            P.top = PBASE
            ringn[0] = 8
            wgate = P.alloc((8, 1024), BF16); wproj = P.alloc((2, 1024), BF16)
            xn2 = [P.alloc((8, 512), BF16) for _ in range(2)]; pt = [P.alloc((2, 512), BF16) for _ in range(2)]
            v2 = [P.alloc((8, 512), F32) for _ in range(3)]; et = [P.alloc((512,), F32) for _ in range(2)]
            r1a = [P.alloc((512,), F32) for _ in range(2)]; r1b = [P.alloc((512,), F32) for _ in range(3)]
            sq = P.alloc((2, 512), BF16)
            P.dma_in('pool', wgate[:], wgate_d[l], 'wgate'); P.dma_in('pool', wproj[:], wproj_d[l], 'wproj')

            def ple_n(ti):
                t0, w = TILES[ti]
                p_, xn, r1 = pt[ti % 2], xn2[ti % 2], r1a[ti % 2]
                P.dma_in('pool', p_[:, :, 0:w], pT_d[l, :, :, t0:t0 + w], 'pt%d' % (ti % 2))
                rstd_of([hT[:, k, t0:t0 + w] for k in range(8)], w, D, r1[:, 0:w], sq)
                for k in range(8):
                    P.stt(xn[:, k, 0:w], hT[:, k, t0:t0 + w], gp[:, gcol(l, 'g_ple_pre', k):gcol(l, 'g_ple_pre', k) + 1], r1[:, 0:w], ALU.mult, ALU.mult)

            def ple_a(ti):
                t0, w = TILES[ti]
                p_, xn, v = pt[ti % 2], xn2[ti % 2], v2[ti % 3]
                for oc in range(8):
                    b = P.psum(bank())
                    for k in range(8):
                        P.mm(b[:, 0:w], wgate[:, k, oc * 128:(oc + 1) * 128], xn[:, k, 0:w], start=(k == 0), stop=(k == 7))
                    e = et[oc % 2]
                    if USE_SIG:
                        P.act(e[:, 0:w], b[:, 0:w], AF.Sigmoid)
                    else:
                        P.act(e[:, 0:w], b[:, 0:w], AF.Exp, scale=-1.0)
                        P.ts(e[:, 0:w], e[:, 0:w], 1.0, op0=ALU.add)
                        P.add('dve', lambda hh, a=e[:, 0:w]: hh.reciprocal(out=a.ap, in_=a.ap), [e[:, 0:w]], [e[:, 0:w]])
                    b2 = P.psum(bank())
                    for k in range(2):
                        P.mm(b2[:, 0:w], wproj[:, k, oc * 128:(oc + 1) * 128], p_[:, k, 0:w], start=(k == 0), stop=(k == 1))
                    P.tt(v[:, oc, 0:w], b2[:, 0:w], e[:, 0:w], ALU.mult)

            def ple_b1(ti):
                t0, w = TILES[ti]
                v, r1 = v2[ti % 3], r1b[ti % 3]
                rstd_of([v[:, k, 0:w] for k in range(8)], w, D, r1[:, 0:w], sq)
                for k in range(8):
                    P.tt(v[:, k, 0:w], v[:, k, 0:w], r1[:, 0:w], ALU.mult, eng=('pool' if POOL_RES else 'dve'))

            def ple_b2(ti):
                t0, w = TILES[ti]
                v = v2[ti % 3]
                for k in range(8):
                    P.stt(hT[:, k, t0:t0 + w], v[:, k, 0:w], gp[:, gcol(l, 'g_ple_post', k):gcol(l, 'g_ple_post', k) + 1],
                          hT[:, k, t0:t0 + w], ALU.mult, ALU.add)

            ple_n(0)
            for ti in range(5):
                if ti + 1 < 5:
                    ple_n(ti + 1)
                ple_a(ti)
                if ti > 0:
                    ple_b1(ti - 1)
                if ti > 1:
                    ple_b2(ti - 2)
            ple_b1(4); ple_b2(3); ple_b2(4)

        nlayers = NL if stage >= 4 else 1
        for l in range(nlayers):
            ffn(l, 0)
            if stage >= 2:
                mixer(l)
            if stage >= 4:
                ffn(l, 1)
                ple(l)
        for k in range(8):
            P.dma_out('sp', yT_d[:, k, :], hT[:, k, :], 'y%d' % k)
        P.emit()
        print("ops", len(P.ops), "sems", P.nsem, flush=True)
    return nc


def _blk(W, bw=None):
    Din, C = W.shape
    return np.ascontiguousarray(W.reshape(Din // 128, 128, C).transpose(1, 0, 2))


_NC_CACHE = {}
STAGE = 99
MIXSEL = 'ACB'


def kernel(**inp):
    f32 = np.float32
    g = {k: np.asarray(v, dtype=f32) for k, v in inp.items()}
    sh = {}
    gpk = np.zeros((128, NGC), f32)
    for l in range(NL):
        for n in GN:
            for k in range(8):
                gpk[:, gcol(l, n, k)] = g[n][l, k * 128:(k + 1) * 128]
        for c in range(2):
            gpk[:, GC_BQ + l * 2 + c] = g['g_bq'][l, c * 128:(c + 1) * 128]
        gpk[:, GC_BKV + l] = g['g_bkv'][l]
        gpk[0:4, GC_BF + l] = g['b_f'][l]
        gpk[32:36, GC_BF + l] = g['b_f'][l]
    sh['gp'] = gpk
    sh['ggrp'] = np.ascontiguousarray(np.broadcast_to(g['g_grp'][:, None, :], (NL, 128, 1024)))
    cst = np.zeros((128, NCC), f32)
    s_ = np.arange(128)[:, None]; t_ = np.arange(128)[None, :]
    cst[:, C_ONES:C_ONES + 128] = 1.0
    cst[:, C_NTRI:C_NTRI + 128] = -1.0 * (s_ >= t_)
    cst[:, C_NONES:C_NONES + 128] = -1.0
    cst[:, C_ID:C_ID + 128] = np.eye(128)
    cst[:, C_MS:C_MS + 128] = (s_ < t_)
    cst[:, C_MI:C_MI + 128] = (s_ <= t_)
    cst[:, C_MC:C_MC + 128] = ((s_ // 64) <= (t_ // 64))
    for h in range(4):
        cst[h, C_SELK + h * 128:C_SELK + (h + 1) * 128] = 1.0
        cst[32 + h, C_SELK + h * 128:C_SELK + (h + 1) * 128] = 1.0
        cst[h, C_SELM + h] = 1.0
        cst[32 + h, C_SELM + h] = 1.0
    sh['cst'] = cst
    pos = np.concatenate([np.arange(TP), PAST + np.arange(TS)]).astype(f32)
    inv = (10000.0 ** (-np.arange(16, dtype=f32) / 16)).astype(f32)
    ang = pos[None, :] * inv[:, None]
    cos, sin = np.cos(ang).astype(f32), np.sin(ang).astype(f32)
    rk = np.stack([np.concatenate([cos, cos], 0), np.concatenate([-sin, sin], 0)], 1).astype(f32)
    sh['ropeK'] = np.ascontiguousarray(rk)
    sh['ropeQ'] = np.ascontiguousarray(rk * f32(MLA_SCALE))
    for i, (gu, dn) in enumerate((('w_ff1_gu', 'w_ff1_down'), ('w_ff2_gu', 'w_ff2_down'))):
        wgu = np.zeros((NL, 22, 128, 8, 256), f32); wd = np.zeros((NL, 2, 8, 128, 11, 128), f32)
        for l in range(NL):
            Wb = _blk(g[gu][l])
            for j in range(22):
                wgu[l, j, :, :, 0:128] = Wb[:, :, j * 128:(j + 1) * 128]
                wgu[l, j, :, :, 128:256] = Wb[:, :, DFF + j * 128:DFF + (j + 1) * 128]
            Db = _blk(g[dn][l])
            for hh in range(2):
                for oc in range(8):
                    wd[l, hh, oc] = Db[:, hh * 11:(hh + 1) * 11, oc * 128:(oc + 1) * 128]
        sh['wgu%d' % (i + 1)] = wgu; sh['wd%d' % (i + 1)] = wd
    win = np.stack([_blk(g['w_in'][l]) for l in range(NL)])
    sh['winA'] = np.ascontiguousarray(win[..., 0:512]); sh['winVa'] = np.ascontiguousarray(win[..., 512:768])
    sh['winB'] = np.ascontiguousarray(win[..., 768:1152])
    kr = win[..., 1152:1184]
    wkr = np.zeros((NL, 128, 8, 2, 96), f32)
    wkr[..., 0, 64:96] = kr
    wkr[..., 1, 64:80] = kr[..., 16:32]; wkr[..., 1, 80:96] = kr[..., 0:16]
    sh['winKr'] = wkr
    sh['winC'] = np.ascontiguousarray(win[..., 1184:1696]); sh['winVc'] = np.ascontiguousarray(win[..., 1696:1952])
    wfl = np.zeros((NL, 128, 8, 36), f32)
    wfl[..., 0:4] = win[..., 1952:1956]; wfl[..., 32:36] = win[..., 1952:1956]
    sh['winFl'] = wfl
    uq = np.stack([_blk(g['w_uq'][l]) for l in range(NL)]).reshape(NL, 128, 2, 8, 96)
    wuq = np.zeros((NL, 128, 2, 8, 2, 96), f32)
    wuq[..., 0, :] = uq
    wuq[..., 1, 64:80] = uq[..., 80:96]; wuq[..., 1, 80:96] = uq[..., 64:80]
    sh['wuq'] = wuq
    ukv = g['w_ukv'].reshape(NL, 128, 8, 128)
    sh['wukvk'] = np.ascontiguousarray(ukv[..., 0:64]); sh['wukvv'] = np.ascontiguousarray(ukv[..., 64:128]).reshape(NL, 128, 512)
    sh['wout'] = np.stack([_blk(g['w_out'][l]) for l in range(NL)])
    sh['wgate'] = np.stack([_blk(g['w_ple_gate'][l]) for l in range(NL)])
    sh['wproj'] = np.stack([_blk(g['w_ple_proj'][l]) for l in range(NL)])
    in_maps = []
    for c in range(8):
        m = dict(sh)
        xa = np.concatenate([g['x_prompt'][c], g['x_sample'][c]], 0)
        m['xT'] = np.ascontiguousarray(xa.T.reshape(8, 128, TT).transpose(1, 0, 2))
        pa = np.concatenate([g['p_prompt'][:, c], g['p_sample'][:, c]], 1)
        m['pT'] = np.ascontiguousarray(pa.transpose(0, 2, 1).reshape(NL, 2, 128, TT).transpose(0, 2, 1, 3))
        for nm, src in (('akcT', 'cache_a_k'), ('ckcT', 'cache_c_k')):
            a = g[src][:, c].reshape(NL, PAST, 256)
            m[nm] = np.ascontiguousarray(a.transpose(0, 2, 1).reshape(NL, 2, 128, PAST).transpose(0, 2, 1, 3))
        for nm, src in (('avc', 'cache_a_v'), ('cvc', 'cache_c_v')):
            a = g[src][:, c].reshape(NL, 8, 128, 256)
            m[nm] = np.ascontiguousarray(a.transpose(0, 2, 1, 3))
        m['ckvcT'] = np.ascontiguousarray(g['cache_b_ckv'][:, c].transpose(0, 2, 1))
        m['krcT'] = np.ascontiguousarray(g['cache_b_krope'][:, c].transpose(0, 2, 1))
        lfT = g['cache_c_logf'][:, c].transpose(0, 2, 1)
        cl = np.zeros((NL, 36, PAST), f32); cl[:, 0:4] = lfT; cl[:, 32:36] = lfT
        m['clfcT'] = cl
        in_maps.append(m)
    if STAGE not in _NC_CACHE:
        _NC_CACHE[STAGE] = build_program(STAGE)
    nc = _NC_CACHE[STAGE]
    res = run_bass_kernel_spmd(nc, in_maps, core_ids=list(range(8)))
    R = res.results
    def fm(name, nch):
        a = np.stack([R[c][name] for c in range(8)])
        return a.transpose(0, 1, 4, 3, 2).reshape(8, NL, TT, nch * 128)
    yT = np.stack([R[c]['yT'] for c in range(8)])
    y = yT.transpose(0, 3, 2, 1).reshape(8, TT, D)
    ak = fm('akT_o', 2); ck = fm('ckT_o', 2)
    ckv = np.stack([R[c]['ckvT_o'] for c in range(8)]).transpose(0, 1, 3, 2)
    krr = np.stack([R[c]['krT_o'] for c in range(8)]).transpose(0, 1, 3, 2)
    lfo = np.stack([R[c]['clfT_o'] for c in range(8)]).transpose(0, 1, 3, 2)
    av = np.stack([R[c]['av_o'] for c in range(8)]); cv = np.stack([R[c]['cv_o'] for c in range(8)])

    def sp(a, shp):
        a = a.transpose(1, 0, 2, 3)
        return (np.ascontiguousarray(a[:, :, :TP]).reshape((NL, 8, TP) + shp).astype(f32),
                np.ascontiguousarray(a[:, :, TP:]).reshape((NL, 8, TS) + shp).astype(f32))
    akp, aks = sp(ak, (4, 64)); avp, avs = sp(av, (4, 64)); ckvp, ckvs = sp(ckv, (128,)); krp, krs = sp(krr, (32,))
    ckp, cks = sp(ck, (4, 64)); cvp, cvs = sp(cv, (4, 64)); lfp, lfs = sp(lfo, (4,))
    return (np.ascontiguousarray(y[:, :TP]).astype(f32), np.ascontiguousarray(y[:, TP:]).astype(f32),
            akp, avp, ckvp, krp, ckp, cvp, lfp, aks, avs, ckvs, krs, cks, cvs, lfs)
```
